# Optimizing a Trainium2 kernel written in Bass

```python
import math
import jax, jax.numpy as jnp
from jax import lax
import numpy as np

D_MODEL = 1024
BATCH = 4
SEQ = 4096
DEPTH = 2
DEC_BATCH = 128
DEC_SEQ = 8
PAST_LEN = 16384
PAGE_SIZE = 128

BRANCH_W = D_MODEL // 2
N_BRANCH = 3
RET_HEADS = 4
RET_DV = BRANCH_W // RET_HEADS
RET_DK = RET_DV // 2
RET_W = RET_HEADS * RET_DV
RET_CHUNK = 128
ATT_HEADS = 8
ATT_KV_HEADS = 2
ATT_DH = BRANCH_W // ATT_HEADS
WINDOW = 128
CONV_DIM = BRANCH_W
CONV_W = 3
D_FF = 4 * D_MODEL
D_PLE = 256
EPS = 1e-6

IN_SPLITS = (RET_HEADS * RET_DK, RET_HEADS * RET_DK, RET_W, RET_W,
             ATT_HEADS * ATT_DH, ATT_KV_HEADS * ATT_DH, ATT_KV_HEADS * ATT_DH,
             CONV_DIM, CONV_DIM, CONV_DIM, N_BRANCH * D_MODEL)
N_IN = sum(IN_SPLITS)

kernel_name = 'hybrid_retention_swa_shortconv_decoder_step'


def _rmsnorm(x, g):
    xf = x.astype(jnp.float32)
    y = xf * lax.rsqrt(jnp.mean(xf * xf, axis=-1, keepdims=True) + EPS)
    return (y * g.astype(jnp.float32)).astype(x.dtype)


def _split_in(z):
    offs = [int(o) for o in np.cumsum(IN_SPLITS)[:-1]]
    return jnp.split(z, offs, axis=-1)


def _retention_scan(q, k, v, s0):
    b, h, t, _ = q.shape
    dv = v.shape[-1]
    c = math.gcd(t, RET_CHUNK)
    n = t // c
    lg = jnp.log1p(-jnp.exp2(-5.0 - jnp.arange(RET_HEADS, dtype=jnp.float32)))
    idx = jnp.arange(c, dtype=jnp.float32)
    diff = idx[:, None] - idx[None, :]
    dmask = jnp.where(diff >= 0, jnp.exp(lg[:, None, None] * jnp.maximum(diff, 0.0)), 0.0)
    q_dec = jnp.exp(lg[:, None] * (idx + 1.0))[:, :, None]
    k_dec = jnp.exp(lg[:, None] * (c - 1.0 - idx))[:, :, None]
    c_dec = jnp.exp(lg * c)[:, None, None]

    def chunks(a):
        return a.astype(jnp.float32).reshape(b, h, n, c, a.shape[-1]).transpose(2, 0, 1, 3, 4)

    def step(s, inp):
        qi, ki, vi = inp
        inner = jnp.einsum('bhid,bhjd->bhij', qi, ki) * dmask
        o = jnp.einsum('bhij,bhje->bhie', inner, vi) + jnp.einsum('bhid,bhde->bhie', qi * q_dec, s)
        s = s * c_dec + jnp.einsum('bhjd,bhje->bhde', ki * k_dec, vi)
        return s, o

    s, o = lax.scan(step, s0.astype(jnp.float32), (chunks(q), chunks(k), chunks(v)))
    return o.transpose(1, 2, 0, 3, 4).reshape(b, h, t, dv), s


def _retention_branch(rq, rk, rv, rg, s0):
    b, t, _ = rq.shape

    def heads(a, d):
        return a.reshape(b, t, RET_HEADS, d).transpose(0, 2, 1, 3)

    o, s = _retention_scan(heads(rq, RET_DK), heads(rk, RET_DK) * (RET_DK ** -0.5), heads(rv, RET_DV), s0)
    o = o * lax.rsqrt(jnp.mean(o * o, axis=-1, keepdims=True) + EPS)
    o = o.transpose(0, 2, 1, 3).reshape(b, t, RET_W).astype(rq.dtype)
    return jax.nn.silu(rg) * o, s


def _sink_window_attend(q, k, v, qpos, kpos, sinks):
    g = ATT_HEADS // ATT_KV_HEADS
    s = jnp.einsum('...qkgd,...skd->...kgqs', q.astype(jnp.float32), k.astype(jnp.float32)) * (ATT_DH ** -0.5)
    dist = qpos[..., :, None] - kpos[..., None, :]
    allowed = (dist >= 0) & (dist < WINDOW) & (kpos[..., None, :] >= 0)
    slopes = jnp.exp2(-8.0 * (jnp.arange(ATT_HEADS, dtype=jnp.float32) + 1.0) / ATT_HEADS).reshape(ATT_KV_HEADS, g)
    s = s - slopes[:, :, None, None] * dist[..., None, None, :, :].astype(jnp.float32)
    s = jnp.where(allowed[..., None, None, :, :], s, -jnp.inf)
    sink = sinks.astype(jnp.float32).reshape(ATT_KV_HEADS, g)[:, :, None, None]
    m = jnp.maximum(jnp.max(s, axis=-1, keepdims=True), sink)
    pr = jnp.exp(s - m)
    pr = pr / (jnp.sum(pr, axis=-1, keepdims=True) + jnp.exp(sink - m))
    return jnp.einsum('...kgqs,...skd->...qkgd', pr, v.astype(jnp.float32))


def _attn_prompt(aq, ak, av, sinks):
    b, t, _ = aq.shape
    g = ATT_HEADS // ATT_KV_HEADS
    blk = WINDOW
    nb = t // blk
    qb = aq.reshape(b, nb, blk, ATT_KV_HEADS, g, ATT_DH)
    k = ak.reshape(b, t, ATT_KV_HEADS, ATT_DH)
    v = av.reshape(b, t, ATT_KV_HEADS, ATT_DH)

    def band(a):
        ap = jnp.pad(a, ((0, 0), (blk, 0), (0, 0), (0, 0)))
        prev = ap[:, :t].reshape(b, nb, blk, ATT_KV_HEADS, ATT_DH)
        return jnp.concatenate([prev, a.reshape(b, nb, blk, ATT_KV_HEADS, ATT_DH)], axis=2)

    pos = jnp.arange(t, dtype=jnp.int32).reshape(nb, blk)
    kpos = jnp.concatenate([pos - blk, pos], axis=1)
    o = _sink_window_attend(qb, band(k), band(v), pos, kpos, sinks)
    w = min(WINDOW, t)
    return o.reshape(b, t, ATT_HEADS * ATT_DH).astype(aq.dtype), k[:, t - w:], v[:, t - w:]


def _attn_sample(aq, ak, av, ck, cv, sinks):
    b, t, _ = aq.shape
    g = ATT_HEADS // ATT_KV_HEADS
    w = ck.shape[1]
    k = jnp.concatenate([ck.astype(ak.dtype), ak.reshape(b, t, ATT_KV_HEADS, ATT_DH)], axis=1)
    v = jnp.concatenate([cv.astype(av.dtype), av.reshape(b, t, ATT_KV_HEADS, ATT_DH)], axis=1)
    qpos = PAST_LEN + jnp.arange(t, dtype=jnp.int32)
    kpos = PAST_LEN - w + jnp.arange(w + t, dtype=jnp.int32)
    o = _sink_window_attend(aq.reshape(b, t, ATT_KV_HEADS, g, ATT_DH), k, v, qpos, kpos, sinks)
    return o.reshape(b, t, ATT_HEADS * ATT_DH).astype(aq.dtype), k[:, t:], v[:, t:]


def _short_conv_branch(cb, cc, ch, buf, conv_w):
    t = cb.shape[1]
    u = cc * ch
    up = jnp.concatenate([buf.astype(u.dtype), u], axis=1)
    y = up[:, 0:t] * conv_w[0]
    for j in range(1, CONV_W):
        y = y + up[:, j:j + t] * conv_w[j]
    return cb * y, up[:, t:]


def _layer(x, p_l, ret_s0, win_k, win_v, conv_buf,
           g_mix_pre, w_in, conv_w, sinks, w_branch, w_out, g_mix_post,
           g_ffn_pre, w_ff1, w_ff2, g_ffn_post, g_ple, w_ple_gate, w_ple_proj):
    b, t, d = x.shape
    h = _rmsnorm(x, g_mix_pre)
    rq, rk, rv, rg, aq, ak, av, cb, cc, ch, gates = _split_in(h @ w_in)
    o_ret, ret_new = _retention_branch(rq, rk, rv, rg, ret_s0)
    if win_k is None:
        o_att, wk_new, wv_new = _attn_prompt(aq, ak, av, sinks)
    else:
        o_att, wk_new, wv_new = _attn_sample(aq, ak, av, win_k, win_v, sinks)
    o_conv, conv_new = _short_conv_branch(cb, cc, ch, conv_buf, conv_w)
    branches = jnp.stack([o_ret, o_att, o_conv], axis=2)
    proj = jnp.einsum('btnw,nwd->btnd', branches, w_branch)
    gate = jax.nn.sigmoid(gates.reshape(b, t, N_BRANCH, d))
    mixed = jnp.sum(gate * proj, axis=2) @ w_out
    x = x + _rmsnorm(mixed, g_mix_post)
    f = jnp.square(jax.nn.relu(_rmsnorm(x, g_ffn_pre) @ w_ff1)) @ w_ff2
    x = x + _rmsnorm(f, g_ffn_post)
    x = x + jax.nn.sigmoid(_rmsnorm(x, g_ple) @ w_ple_gate) * (p_l @ w_ple_proj)
    return x, (ret_new, wk_new, wv_new, conv_new)


def setup_inputs(seed: int = 0) -> dict:
    key = jax.random.key(seed)
    ks = jax.random.split(key, 24)
    f32 = jnp.float32
    win = min(WINDOW, PAST_LEN)

    def nrm(k, shape, scale):
        return jax.random.normal(k, shape, f32) * scale

    def gain(k):
        return 1.0 + 0.05 * jax.random.normal(k, (DEPTH, D_MODEL), f32)

    return {
        'x_prompt': nrm(ks[0], (BATCH, SEQ, D_MODEL), 1.0),
        'x_sample': nrm(ks[1], (DEC_BATCH, DEC_SEQ, D_MODEL), 1.0),
        'p_prompt': nrm(ks[2], (DEPTH, BATCH, SEQ, D_PLE), 1.0),
        'p_sample': nrm(ks[3], (DEPTH, DEC_BATCH, DEC_SEQ, D_PLE), 1.0),
        'state_ret': nrm(ks[4], (DEPTH, DEC_BATCH, RET_HEADS, RET_DK, RET_DV), 1.0),
        'cache_win_k': nrm(ks[5], (DEPTH, DEC_BATCH, win, ATT_KV_HEADS, ATT_DH), 1.0),
        'cache_win_v': nrm(ks[6], (DEPTH, DEC_BATCH, win, ATT_KV_HEADS, ATT_DH), 1.0),
        'state_conv': nrm(ks[7], (DEPTH, DEC_BATCH, CONV_W - 1, CONV_DIM), 1.0),
        'g_mix_pre': gain(ks[8]),
        'w_in': nrm(ks[9], (DEPTH, D_MODEL, N_IN), D_MODEL ** -0.5),
        'conv_w': nrm(ks[10], (DEPTH, CONV_W, CONV_DIM), CONV_W ** -0.5),
        'attn_sinks': nrm(ks[11], (DEPTH, ATT_HEADS), 0.5),
        'w_branch': nrm(ks[12], (DEPTH, N_BRANCH, BRANCH_W, D_MODEL), BRANCH_W ** -0.5),
        'w_out': nrm(ks[13], (DEPTH, D_MODEL, D_MODEL), D_MODEL ** -0.5),
        'g_mix_post': gain(ks[14]),
        'g_ffn_pre': gain(ks[15]),
        'w_ff1': nrm(ks[16], (DEPTH, D_MODEL, D_FF), D_MODEL ** -0.5),
        'w_ff2': nrm(ks[17], (DEPTH, D_FF, D_MODEL), D_FF ** -0.5),
        'g_ffn_post': gain(ks[18]),
        'g_ple': gain(ks[19]),
        'w_ple_gate': nrm(ks[20], (DEPTH, D_MODEL, D_MODEL), D_MODEL ** -0.5),
        'w_ple_proj': nrm(ks[21], (DEPTH, D_PLE, D_MODEL), D_PLE ** -0.5),
    }


def reference(x_prompt, x_sample, p_prompt, p_sample, state_ret, cache_win_k, cache_win_v, state_conv,
              g_mix_pre, w_in, conv_w, attn_sinks, w_branch, w_out, g_mix_post,
              g_ffn_pre, w_ff1, w_ff2, g_ffn_post, g_ple, w_ple_gate, w_ple_proj):
    yp, ys = x_prompt, x_sample
    bp = x_prompt.shape[0]
    rp, kp, vp, cp = [], [], [], []
    rs, kss, vs, cs = [], [], [], []
    for l in range(DEPTH):
        wl = (g_mix_pre[l], w_in[l], conv_w[l], attn_sinks[l], w_branch[l], w_out[l], g_mix_post[l],
              g_ffn_pre[l], w_ff1[l], w_ff2[l], g_ffn_post[l], g_ple[l], w_ple_gate[l], w_ple_proj[l])
        yp, (r, k, v, c) = _layer(yp, p_prompt[l],
                                  jnp.zeros((bp, RET_HEADS, RET_DK, RET_DV), jnp.float32),
                                  None, None,
                                  jnp.zeros((bp, CONV_W - 1, CONV_DIM), yp.dtype), *wl)
        rp.append(r); kp.append(k); vp.append(v); cp.append(c)
        ys, (r, k, v, c) = _layer(ys, p_sample[l], state_ret[l], cache_win_k[l], cache_win_v[l],
                                  state_conv[l], *wl)
        rs.append(r); kss.append(k); vs.append(v); cs.append(c)
    return (yp, ys,
            jnp.stack(rp), jnp.stack(kp), jnp.stack(vp), jnp.stack(cp),
            jnp.stack(rs), jnp.stack(kss), jnp.stack(vs), jnp.stack(cs))
```

```python
import contextlib
import numpy as np
import concourse.bass as bass
import concourse.mybir as mybir
from concourse.bass_utils import run_bass_kernel_spmd

F32 = mybir.dt.float32
BF = mybir.dt.bfloat16
ALU = mybir.AluOpType
AF = mybir.ActivationFunctionType
AX = mybir.AxisListType

D = 1024
DEPTH = 2
NIN = 6912
EPS = 1e-6
OFF = dict(rq=0, rk=256, rv=512, rg=1024, aq=1536, ak=2048, av=2176, cb=2304, cc=2816, ch=3328, gt=3840)
NEG = -30000.0
import os
STAGE = int(os.environ.get("MK_STAGE", "9"))


class Buf:
    __slots__ = ("name", "w", "r", "x", "sk")

    def __init__(self, name="", x=False):
        self.name = name
        self.w = None
        self.r = {}
        self.sk = None
        self.x = x


class Ctx:
    def __init__(self, nc, n_streams=4):
        self.nc = nc
        self.es = contextlib.ExitStack()
        self.eng = {"pe": nc.tensor, "act": nc.scalar, "dve": nc.vector, "pool": nc.gpsimd, "sp": nc.sync}
        self.sem = {}
        self.cnt = {}
        for k in self.eng:
            self.sem[k] = self.es.enter_context(nc.semaphore("s_" + k))
            self.cnt[k] = 0
        for i in range(n_streams):
            k = "d%d" % i
            self.sem[k] = self.es.enter_context(nc.semaphore("s_" + k))
            self.cnt[k] = 0
        self.seen = {k: {} for k in self.eng}
        self.n_ins = {k: 0 for k in self.eng}
        self.marks = []

    def mark(self, label):
        self.marks.append((label, self.n_ins["pe"]))

    def close(self):
        self.es.close()

    @staticmethod
    def _flat(bs):
        out = []
        for b in bs:
            if isinstance(b, (list, tuple)):
                out.extend(Ctx._flat(b))
            else:
                out.append(b)
        return out

    def _deps(self, e, reads, writes):
        deps = {}
        xr = [b for b in reads if b.x]
        if xr:
            writes = list(writes) + xr
        for b in reads:
            if b.w is not None and deps.get(b.w[0], 0) < b.w[1]:
                deps[b.w[0]] = b.w[1]
        for b in writes:
            if b.w is not None and deps.get(b.w[0], 0) < b.w[1]:
                deps[b.w[0]] = b.w[1]
            for f, v in b.r.items():
                if f != e and deps.get(f, 0) < v:
                    deps[f] = v
        if e == "pe":
            deps.pop("pe", None)
        return deps

    def _wait(self, e, deps):
        for f, v in deps.items():
            if self.seen[e].get(f, 0) < v:
                self.eng[e].wait_ge(self.sem[f], v)
                self.seen[e][f] = v

    def op(self, e, fn, reads=(), writes=(), inc=True):
        reads = self._flat(reads); writes = self._flat(writes)
        self._wait(e, self._deps(e, reads, writes))
        ins = fn(self.eng[e])
        self.n_ins[e] += 1
        if inc:
            ins.then_inc(self.sem[e], 1)
            self.cnt[e] += 1
            idx = self.cnt[e]
        else:
            idx = self.cnt[e] + 1
        for b in reads:
            if b.x:
                b.w = (e, idx)
                b.r = {}
            elif b.r.get(e, 0) < idx:
                b.r[e] = idx
        for b in writes:
            b.w = (e, idx)
            b.r = {}
        return ins

    def dma(self, q, out, in_, reads=(), writes=(), stream=0, own=None, **kw):
        reads = self._flat(reads); writes = self._flat(writes)
        self._wait(q, self._deps(q, reads, writes))
        if own is None:
            own = writes[0] if writes else reads[0]
        if own.sk is None:
            own.sk = {}
        cls = "sw" if q == "pool" else "hw"
        if cls not in own.sk:
            k = "m%d" % len(self.sem)
            own.sk[cls] = k
            self.sem[k] = self.es.enter_context(self.nc.semaphore("s_" + k))
            self.cnt[k] = 0
        k = own.sk[cls]
        ins = self.eng[q].dma_start(out=out, in_=in_, **kw)
        ins.then_inc(self.sem[k], 16)
        self.cnt[k] += 16
        idx = self.cnt[k]
        for b in reads:
            if b.r.get(k, 0) < idx:
                b.r[k] = idx
        for b in writes:
            b.w = (k, idx)
            b.r = {}
        return ins

    def barrier(self):
        for e in self.eng:
            if e == "sp":
                continue
            deps = {f: self.cnt[f] for f in self.cnt if f != e and self.cnt[f] > 0}
            self._wait(e, deps)

    def final_wait(self, e="sp"):
        deps = {f: self.cnt[f] for f in self.cnt if f != e and self.cnt[f] > 0}
        self._wait(e, deps)


def build(NTP, SAMPLE=True, NCORES=8):
    nc = bass.Bass("TRN2", target_bir_lowering=False)
    C = Ctx(nc)
    TP = NTP * 128

    def din(name, shape, dt=F32):
        return nc.dram_tensor(name, list(shape), dt, kind="ExternalInput").ap()

    def dout(name, shape):
        return nc.dram_tensor(name, list(shape), F32, kind="ExternalOutput").ap()

    xp = din("xp", [TP, D]); pp = din("pp", [DEPTH, TP, 256])
    xs = din("xs", [128, D]); ps_ = din("ps", [DEPTH, 128, 256])
    sret = din("sret", [DEPTH, 16, 4, 64, 128])
    ck = din("ck", [DEPTH, 16, 128, 128]); cv = din("cv", [DEPTH, 16, 128, 128])
    scv = din("scv", [DEPTH, 32, 512])
    gTd = din("gT", [128, DEPTH, 5, 8]); cwTd = din("cwT", [128, DEPTH, 4, 3]); sinkd = din("sinkR", [128, DEPTH, 8])
    winF = din("winF", [DEPTH, 13, 128, 8, 512]); winT = din("winT", [DEPTH, 3, 128, 8, 512])
    wbr = din("wbr", [DEPTH, 3, 128, 4, 1024]); wout = din("wout", [DEPTH, 2, 128, 8, 512])
    wff1 = din("wff1", [DEPTH, 8, 128, 8, 512]); wff2 = din("wff2", [DEPTH, 8, 128, 8, 512])
    wpg = din("wpg", [DEPTH, 2, 128, 8, 512]); wpp = din("wpp", [DEPTH, 128, 2, 1024])
    c_ident = din("c_ident", [128, 128]); c_ones = din("c_ones", [128, 128])
    c_dmT = din("c_dmT", [128, 512]); c_dmTs = din("c_dmTs", [128, 512])
    c_qdec = din("c_qdec", [128, 256]); c_qdecs = din("c_qdecs", [128, 512])
    c_kdec = din("c_kdec", [128, 256]); c_kdecs = din("c_kdecs", [128, 256])
    c_cdec = din("c_cdec", [128, 2])
    c_biasP = din("c_biasP", [128, 2048]); c_biasS = din("c_biasS", [128, 2048])
    c_bmQ = din("c_bmQ", [128, 2048]); c_bmQ2 = din("c_bmQ2", [128, 1024]); c_bmV = din("c_bmV", [128, 16])
    sel8d = din("sel8", [128, 8]); negfd = din("negf", [128, 1])
    NG = NTP // 4
    PK = 520
    xs0 = nc.dram_tensor("xscr0", [NG, 128, 8, 512], F32, kind="Internal").ap()
    xs1 = nc.dram_tensor("xscr1", [NG + 1, 128, 8, 512], F32, kind="Internal").ap()
    b_xs0 = [Buf() for _ in range(NG)]; b_xs1 = [Buf() for _ in range(NG + 1)]
    pkg_in = [nc.dram_tensor("pkg_in%d" % l, [128, PK], F32, kind="Internal").ap() for l in range(DEPTH)]
    GS = 2
    pkg_out = [nc.dram_tensor("pkg_out%d" % l, [GS * 128, PK], F32, kind="Internal").ap() for l in range(DEPTH)]

    yp = dout("yp", [TP, D]); ys = dout("ys", [128, D])
    retp = dout("retp", [DEPTH, 4, 64, 128]); wkp = dout("wkp", [DEPTH, 128, 128]); wvp = dout("wvp", [DEPTH, 128, 128])
    cvp = dout("cvp", [DEPTH, 2, 512])
    rets = dout("rets", [DEPTH, 16, 4, 64, 128]); wks = dout("wks", [DEPTH, 16, 128, 128]); wvs = dout("wvs", [DEPTH, 16, 128, 128])
    cvs = dout("cvs", [DEPTH, 32, 512])
    DBG = os.environ.get("MK_DBG", "0") == "1"
    if DBG:
        dbg = dout("dbg", [128, 12, 128])
    DBG2 = os.environ.get("MK_DBG", "0") in ("2", "3")
    if DBG2:
        dbgx = dout("dbgx", [128, 8, 512])

    def sb(name, shape, dt=F32):
        return nc.alloc_sbuf_tensor("sb_" + name, list(shape), dt)

    b_const = Buf("const")
    ident = sb("ident", [128, 128]); identb = sb("identb", [128, 128], BF); onesb = sb("onesb", [128, 128], BF)
    dmT = sb("dmT", [128, 512])
    qdec = sb("qdec", [128, 2, 128])
    kdec = sb("kdec", [128, 256]); cdec = sb("cdec", [128, 2])
    biasP = sb("biasP", [128, 8, 256])
    bmV = sb("bmV", [128, 16], BF)
    gT = sb("gT", [128, DEPTH, 5, 8]); cwT = sb("cwT", [128, DEPTH, 4, 3]); sinkR = sb("sinkR", [128, DEPTH, 8])
    epsb = sb("epsb", [128, 1])
    joinb = sb("joinb", [128, 1])
    sel8 = sb("sel8", [128, 8]); negf = sb("negf", [128, 1])
    C.dma("pool", sel8[:, :], sel8d, writes=[b_const], stream=1)
    C.dma("pool", negf[:, :], negfd, writes=[b_const], stream=1)
    for dst, src in [(ident[:, :], c_ident), (dmT[:, :], c_dmT),
                     (qdec[:, :, :], c_qdec.rearrange("p (a b) -> p a b", a=2)),
                     (kdec[:, :], c_kdec), (cdec[:, :], c_cdec),
                     (biasP[:, :, :], c_biasP.rearrange("p (a b) -> p a b", a=8)),
                     (gT[:, :, :, :], gTd), (cwT[:, :, :, :], cwTd), (sinkR[:, :, :], sinkd)]:
        C.dma("pool", dst, src, writes=[b_const], stream=1)
    for dst, src in [(identb[:, :], c_ident), (onesb[:, :], c_ones),
                     (bmV[:, :], c_bmV)]:
        C.dma("pool", dst, src, writes=[b_const], stream=0)
    C.op("dve", lambda e: e.memset(epsb[:, :], EPS), writes=[b_const])

    xT = sb("xT", [128, 8, 512]); b_xT = [Buf("xT%d" % c) for c in range(8)]
    hT = sb("hT", [128, 8, 512], BF); b_hT = [Buf("hT%d" % c) for c in range(8)]
    yT = sb("yT", [128, 8, 512]); b_yT = [Buf("yT%d" % c) for c in range(8)]
    sq = sb("sq", [128, 2, 512], BF); b_sq = [Buf(), Buf()]
    rstd = sb("rstd", [128, 512]); b_rstd = Buf()
    tmpA = sb("tmpA", [128, 2, 512]); b_tmpA = [Buf(), Buf()]
    oT = sb("oT", [128, 12, 512], BF); b_oT = [Buf("oT0"), Buf("oT1"), Buf("oT2")]
    mixT = sb("mixT", [128, 8, 512], BF); b_mixT = Buf("mixT")
    pT2 = sb("pT", [128, 2, 2, 512], BF); b_pT2 = [Buf("pT0"), Buf("pT1")]
    uT = sb("uT", [128, 4, 516]); b_uT = Buf("uT")
    uhalo = sb("uhalo", [128, DEPTH, 4, 2]); b_uhalo = Buf()
    akT = [sb("akT%d" % l, [128, 640], BF) for l in range(DEPTH)]; b_akT = [Buf(), Buf()]
    avt = [sb("avt%d" % l, [128, 5, 128], BF) for l in range(DEPTH)]; b_avt = [Buf(), Buf()]
    S = [sb("S%d" % l, [128, 2, 128]) for l in range(DEPTH)]; b_S = [Buf(), Buf()]
    Sb = [sb("Sb%d" % l, [128, 2, 128], BF) for l in range(DEPTH)]
    ssb = sb("ssb", [128, 8, 256]); b_ssb = Buf()
    pbf = sb("pbf", [128, 8, 256], BF); b_pbf = Buf()
    pTs = sb("pTs", [128, 8, 2, 128], BF); b_pTs = Buf()
    st8 = sb("st8", [128, 8, 8]); b_st8 = Buf()
    kvf = sb("kvf", [128, 256]); b_kvf = Buf()
    innT = sb("innT", [128, 4, 128], BF); b_innT = Buf()
    ortok = sb("ortok", [128, 512], BF); b_ortok = Buf()
    xin = sb("xin", [128, 1, 1024]); _bx = Buf(); b_xin = [_bx, _bx]
    pin = sb("pin", [128, 1, 256]); _bp = Buf(); b_pin = [_bp, _bp]
    NS = 6
    wring = sb("wring", [128, NS, 4096], BF); b_wr = [Buf("wr%d" % i) for i in range(NS)]
    AR = sb("AR", [128, 16896], BF); b_AR = Buf("AR")

    def arv(off_bytes, nbytes, dt, pat=None, **kw):
        v = AR[:, off_bytes // 2:(off_bytes + nbytes) // 2]
        if dt == F32:
            v = v.bitcast(F32)
        if pat:
            v = v.rearrange(pat, **kw)
        return v

    for l in range(DEPTH):
        C.op("dve", lambda e: e.memset(S[l][:, :, :], 0.0), writes=[b_S[l]])
        C.op("dve", lambda e: e.memset(Sb[l][:, :, :], 0.0), writes=[b_S[l]])
        C.op("dve", lambda e: e.memset(akT[l][:, :], 0.0), writes=[b_akT[l]])
        C.op("dve", lambda e: e.memset(avt[l][:, :, :], 0.0), writes=[b_avt[l]])
    C.op("dve", lambda e: e.memset(uhalo[:, :, :, :], 0.0), writes=[b_uhalo])

    psb = [nc.alloc_psum_tensor("psb%d" % i, [128, 512], F32) for i in range(8)]
    b_ps = [Buf("ps%d" % i, x=True) for i in range(8)]
    bank_i = [0]

    held = set()

    def bank(hold=False):
        i = bank_i[0]
        while i in held:
            i = (i + 1) % 8
        bank_i[0] = (i + 1) % 8
        if hold:
            held.add(i)
        return psb[i], b_ps[i]

    def unhold(bb):
        held.discard(b_ps.index(bb))

    ev_i = [0]

    def evac(out, in_, reads, writes):
        ev_i[0] ^= 1
        if ev_i[0]:
            C.op("act", lambda e: e.copy(out, in_), reads, writes)
        else:
            C.op("dve", lambda e: e.tensor_copy(out, in_), reads, writes)

    wr_i = [0]

    scr_t = {}
    scr_b = {}

    def wload(arr, idx, kc, ncols):
        name, ap = arr
        if name not in scr_t:
            scr_t[name] = nc.dram_tensor("scr_" + name, list(ap.shape), BF, kind="Internal").ap()
        src = ap
        dst = scr_t[name]
        for j in idx:
            src = src[j]
            dst = dst[j]
        src = src[:, :, 0:ncols]
        dst = dst[:, :, 0:ncols]
        i = wr_i[0]
        wr_i[0] = (i + 1) % NS
        v = wring[:, i, 0:kc * ncols].rearrange("p (k n) -> p k n", k=kc)
        key = (name,) + tuple(idx)
        if key not in scr_b:
            scr_b[key] = Buf("scr")
            C.dma("pool", v, src, writes=[b_wr[i]], stream=0)
            C.dma("sp", dst, v, reads=[b_wr[i]], writes=[scr_b[key]], own=b_wr[i])
        else:
            C.dma("sp", v, dst, reads=[scr_b[key]], writes=[b_wr[i]], stream=3)
        return v, b_wr[i]

    A_winF = ("winF", winF); A_winT = ("winT", winT); A_wbr = ("wbr", wbr); A_wout = ("wout", wout)
    A_wff1 = ("wff1", wff1); A_wff2 = ("wff2", wff2); A_wpg = ("wpg", wpg); A_wpp = ("wpp", wpp)

    P = lambda fn, r, w, inc=True: C.op("pe", fn, r, w, inc)
    A = lambda fn, r, w: C.op("act", fn, r, w)
    V = lambda fn, r, w: C.op("dve", fn, r, w)

    def stat_begin():
        bk, bb = bank(hold=True)
        return {"bk": bk, "bb": bb, "n": 0}

    def stat_add(st, srcT, b_src, c, N):
        k = st["n"]
        A(lambda e: e.activation(sq[:, k % 2, :N], srcT[:, c, :N], AF.Square), [b_src[c]], [b_sq[k % 2]])
        P(lambda e: e.matmul(st["bk"][:, :N], onesb[:, :], sq[:, k % 2, :N], start=(k == 0), stop=(k == 7)),
          [b_sq[k % 2], b_const], [st["bb"]])
        st["n"] = k + 1

    def stat_finish(st, N):
        assert st["n"] == 8
        bk, bb = st["bk"], st["bb"]
        A(lambda e: e.activation(rstd[:, :N], bk[:, :N], AF.Ln, bias=epsb[:, 0:1], scale=1.0), [bb, b_const], [b_rstd])
        A(lambda e: e.activation(rstd[:, :N], rstd[:, :N], AF.Exp, scale=-0.5), [b_rstd], [b_rstd])
        unhold(bb)

    def pre_norm(l, gi, N, st=None):
        if st is None:
            st = stat_begin()
            for c in range(8):
                stat_add(st, xT, b_xT, c, N)
        stat_finish(st, N)
        for c in range(8):
            V(lambda e: e.scalar_tensor_tensor(hT[:, c, :N], xT[:, c, :N], gT[:, l, gi, c:c + 1], rstd[:, :N],
                                               ALU.mult, ALU.mult), [b_xT[c], b_rstd, b_const], [b_hT[c]])

    def post_norm_add(l, gi, N, st=None):
        if st is None:
            st = stat_begin()
            for c in range(8):
                stat_add(st, yT, b_yT, c, N)
        stat_finish(st, N)
        st2 = stat_begin()
        for c in range(8):
            V(lambda e: e.scalar_tensor_tensor(tmpA[:, c % 2, :N], yT[:, c, :N], gT[:, l, gi, c:c + 1], rstd[:, :N],
                                               ALU.mult, ALU.mult), [b_yT[c], b_rstd, b_const], [b_tmpA[c % 2]])
            C.op("pool", lambda e: e.tensor_tensor(xT[:, c, :N], xT[:, c, :N], tmpA[:, c % 2, :N], ALU.add),
                 [b_xT[c], b_tmpA[c % 2]], [b_xT[c]])
            stat_add(st2, xT, b_xT, c, N)
        return st2

    def fmm4(wv, wb, N):
        banks = [bank(hold=True) for _ in range(4)]
        for kc in range(8):
            for m in range(4):
                bk, bb = banks[m]
                P(lambda e: e.matmul(bk[:, :N], wv[:, kc, m * 128:(m + 1) * 128], hT[:, kc, :N], start=(kc == 0), stop=(kc == 7)),
                  [wb, b_hT[kc]], [bb])
        return banks

    def fmm(bk, bb, wv, wb, kcs, cols, rhsT, b_rhs, N):
        for kc in range(kcs):
            P(lambda e: e.matmul(bk[:, :N], wv[:, kc, cols], rhsT[:, kc, :N], start=(kc == 0), stop=(kc == kcs - 1)),
              [wb, b_rhs], [bb])

    def process_layer(l, kind, N, first_group, last_group, pslot=0, prefetch=None):
        pT = pT2[:, pslot]; b_pT = b_pT2[pslot]
        nt = N // 128
        smp = (kind == "s")
        zq = arv(0, 9 * N * 2, BF, "p (c n) -> p c n", c=9)
        qd = arv(9216, 2 * N * 2, BF, "p (c n) -> p c n", c=2)
        rkt = arv(11264, nt * 256 * 2, BF, "p (t n) -> p t n", t=nt)
        rvt = arv(13312, nt * 512 * 2, BF, "p (t n) -> p t n", t=nt)
        rgt = arv(17408, nt * 512 * 4, F32, "p (t n) -> p t n", t=nt)
        ccT = arv(25600, 4 * N * 4, F32, "p (c n) -> p c n", c=4)
        hidT = arv(0, 32 * N * 2, BF, "p (c n) -> p c n", c=32)
        b_zq = Buf("zq"); b_qd = Buf("qd"); b_rkt = Buf("rkt"); b_rvt = Buf("rvt"); b_rgt = Buf("rgt"); b_ccT = Buf("ccT")
        b_hid = Buf("hid")
        if smp:
            S0x = [arv(27648, 4096, F32, "p (b e) -> p b e", b=8), arv(19456, 4096, F32, "p (b e) -> p b e", b=8)]
            S0bx = [arv(31744, 2048, BF, "p (b e) -> p b e", b=8), arv(23552, 2048, BF, "p (b e) -> p b e", b=8)]
            b_S0x = [Buf(), Buf()]
            kcTb = lambda b: oT[:, b // 2, 128 + (b % 2) * 128:256 + (b % 2) * 128]
            vcb = lambda b: mixT[:, b // 2, 128 + (b % 2) * 128:256 + (b % 2) * 128]
            aqm4 = yT[:, 0:8, 128:256].bitcast(BF).rearrange("p c (two i) -> p c two i", two=2)
            aqm_b = lambda b: aqm4[:, b // 2, b % 2, :]
            qdm = yT[:, 0:8, 256:320].bitcast(BF)
            qdp = yT[:, 0:4, 320:384].bitcast(BF)
            kinx = [arv(2304, 2048, F32, "p (b s) -> p b s", b=4), arv(6400, 2048, F32, "p (b s) -> p b s", b=4)]
            vinx = [arv(4352, 2048, F32, "p (b s) -> p b s", b=4), arv(8448, 2048, F32, "p (b s) -> p b s", b=4)]
            b_kinx = [Buf(), Buf()]; b_vinx = [Buf(), Buf()]
            b_kcT = Buf(); b_vc = Buf(); b_aqm = Buf(); b_qdm = Buf(); b_qdp = Buf()
        C.barrier()

        if smp:
            b_sc = [Buf("sc%d" % i) for i in range(6)]
            C.dma("pool", xT[:, :, 128:384], c_biasS.rearrange("p (a b) -> p a b", a=8), writes=[b_sc[0]], stream=1)
            C.dma("pool", xT[:, 0:4, 384:512], c_dmTs.rearrange("p (h i) -> p h i", h=4), writes=[b_sc[1]], stream=1)
            C.dma("pool", xT[:, 4:8, 384:512], c_qdecs.rearrange("p (h i) -> p h i", h=4), writes=[b_sc[2]], stream=1)
            C.dma("pool", yT[:, 0:2, 384:512], c_kdecs.rearrange("p (a i) -> p a i", a=2), writes=[b_sc[3]], stream=1)
            C.dma("pool", hT[:, 0:8, 128:384], c_bmQ.rearrange("p (c x) -> p c x", c=8), writes=[b_sc[4]], stream=1)
            C.dma("pool", hT[:, 0:8, 384:512], c_bmQ2.rearrange("p (c x) -> p c x", c=8), writes=[b_sc[5]], stream=1)
            V(lambda e: e.memset(joinb[:, :], 0.0), [b_sc], [b_const])
            for two in range(2):
                C.dma("pool", mixT[:, 0:8, 128 + two * 128:256 + two * 128],
                      cv[l].rearrange("(c two) s f -> two s c f", two=2)[two], writes=[b_vc], stream=0)
            for b4 in range(4):
                kin, vin = kinx[b4 % 2], vinx[b4 % 2]
                b_kin, b_vin = b_kinx[b4 % 2], b_vinx[b4 % 2]
                C.dma("pool", kin, ck[l, b4 * 4:(b4 + 1) * 4].rearrange("b s f -> s b f"), writes=[b_kin], stream=1)
                C.dma("pool", vin, cv[l, b4 * 4:(b4 + 1) * 4].rearrange("b s f -> s b f"), writes=[b_vin], stream=1)
                bk, bb = bank()
                for j in range(4):
                    P(lambda e: e.transpose(bk[:, j * 128:(j + 1) * 128], kin[:, j, :], ident[:, :]), [b_kin, b_const], [bb])
                evac(oT[:, 2 * b4:2 * b4 + 2, 128:384].rearrange("p c (two s) -> p c two s", two=2),
                     bk[:, :].rearrange("p (c two s) -> p c two s", c=2, two=2), [bb], [b_kcT])
                C.dma("pool", wks[l, b4 * 4:(b4 + 1) * 4, 0:120, :].rearrange("b s f -> s b f"), kin[8:128, :, :], reads=[b_kin], stream=2)
                C.dma("pool", wvs[l, b4 * 4:(b4 + 1) * 4, 0:120, :].rearrange("b s f -> s b f"), vin[8:128, :, :], reads=[b_vin], stream=2)

        if STAGE < 1:
            return
        C.mark("%s%d prenorm" % (kind, l))
        pre_norm(l, 0, N)
        C.mark("%s%d Fproj" % (kind, l))

        if os.environ.get("MK_SUB", "1") == "0":
            return
        def fblock(bi, ncols=512):
            return wload(A_winF, (l, bi), 8, ncols)

        wv, wb = fblock(0)
        SUB = os.environ.get("MK_SUB", "1")
        if SUB == "a":
            return
        pre = fmm4(wv, wb, N)
        for m in range(4):
            bk, bb = pre[m]
            unhold(bb)
            if SUB == "b":
                continue
            evac(zq[:, m, :N], bk[:, :N], [bb], [b_zq])
            if SUB == "c":
                continue
            if m < 2 and not smp:
                for t in range(nt):
                    V(lambda e: e.tensor_tensor(qd[:, m, t * 128:(t + 1) * 128], bk[:, t * 128:(t + 1) * 128],
                                                qdec[:, m, :], ALU.mult), [bb, b_const], [b_qd])
        if SUB in "abcd":
            return
        wv, wb = fblock(1)
        for m in range(4):
            bk, bb = bank()
            fmm(bk, bb, wv, wb, 8, slice(m * 128, (m + 1) * 128), hT, b_hT, N)
            evac(zq[:, 4 + m, :N], bk[:, :N], [bb], [b_zq])
        if SUB == "e":
            return
        wv, wb = fblock(2, 128)
        bk, bb = bank()
        fmm(bk, bb, wv, wb, 8, slice(0, 128), hT, b_hT, N)
        evac(akT[l][:, 128:128 + N], bk[:, :N], [bb], [b_akT[l]])

        if STAGE < 2:
            return
        C.mark("%s%d conv" % (kind, l))
        if smp:
            uv = uT[:, :, 0:160].rearrange("p c (b t) -> p c b t", b=16)
            C.dma("pool", xin[0:32, 0, 0:512], scv[l, :, :], writes=[b_xin[0]], stream=1)
            bk, bb = bank()
            for c in range(4):
                P(lambda e: e.transpose(bk[:, c * 32:(c + 1) * 32], xin[0:32, 0, c * 128:(c + 1) * 128], ident[0:32, 0:32]),
                  [b_xin[0], b_const], [bb])
            V(lambda e: e.tensor_copy(uv[:, :, :, 0:2], bk[:, 0:128].rearrange("p (c b j) -> p c b j", c=4, b=16)),
              [bb], [b_uT])
            ucur = lambda c: uv[:, c, :, 2:10]
            ush = lambda c, j: uv[:, c, :, j:j + 8]
            v3 = lambda ap: ap.rearrange("p (b t) -> p b t", b=16)
        else:
            V(lambda e: e.tensor_copy(uT[:, :, 0:2], uhalo[:, l, :, :]), [b_uhalo], [b_uT])
            ucur = lambda c: uT[:, c, 2:2 + N]
            ush = lambda c, j: uT[:, c, j:j + N]
            v3 = lambda ap: ap
        def conv_cc():
            wv, wb = fblock(3)
            for c in range(4):
                bk, bb = bank()
                fmm(bk, bb, wv, wb, 8, slice(c * 128, (c + 1) * 128), hT, b_hT, N)
                evac(ccT[:, c, :N], bk[:, :N], [bb], [b_ccT])

        def conv_ch():
            wv, wb = fblock(4)
            for c in range(4):
                bk, bb = bank()
                fmm(bk, bb, wv, wb, 8, slice(c * 128, (c + 1) * 128), hT, b_hT, N)
                V(lambda e: e.tensor_tensor(ucur(c), v3(ccT[:, c, :N]), v3(bk[:, :N]), ALU.mult), [bb, b_ccT], [b_uT])

        def conv_cb():
          wv, wb = fblock(5)
          for c in range(4):
              ycv = tmpA[:, c % 2, :N]
              V(lambda e: e.tensor_scalar(v3(ycv), ush(c, 0), cwT[:, l, c, 0:1], None, ALU.mult),
                [b_uT, b_const], [b_tmpA[c % 2]])
              for j in (1, 2):
                  V(lambda e: e.scalar_tensor_tensor(v3(ycv), ush(c, j), cwT[:, l, c, j:j + 1], v3(ycv), ALU.mult, ALU.add),
                    [b_uT, b_const, b_tmpA[c % 2]], [b_tmpA[c % 2]])
              bk, bb = bank()
              fmm(bk, bb, wv, wb, 8, slice(c * 128, (c + 1) * 128), hT, b_hT, N)
              V(lambda e: e.tensor_tensor(oT[:, 8 + c, :N], bk[:, :N], ycv, ALU.mult), [bb, b_tmpA[c % 2]], [b_oT[2]])
          if smp:
              bk, bb = bank()
              V(lambda e: e.tensor_copy(tmpA[:, 0, 0:128].rearrange("p (c b j) -> p c b j", c=4, b=16), uv[:, :, :, 8:10]),
                [b_uT], [b_tmpA[0]])
              for c in range(4):
                  P(lambda e: e.transpose(bk[0:32, c * 128:(c + 1) * 128], tmpA[:, 0, c * 32:(c + 1) * 32], ident[:, :]),
                    [b_tmpA[0], b_const], [bb])
              V(lambda e: e.tensor_copy(xin[0:32, 0, 512:1024], bk[0:32, 0:512]), [bb], [b_xin[1]])
              C.dma("pool", cvs[l, :, :], xin[0:32, 0, 512:1024], reads=[b_xin[1]], stream=2)
          else:
              V(lambda e: e.tensor_copy(uhalo[:, l, :, :], uT[:, :, N:N + 2]), [b_uT], [b_uhalo])
              if last_group:
                  bk, bb = bank()
                  V(lambda e: e.tensor_copy(tmpA[:, 0, 0:8].rearrange("p (c j) -> p c j", c=4), uT[:, :, N:N + 2]),
                    [b_uT], [b_tmpA[0]])
                  P(lambda e: e.transpose(bk[0:8, 0:128], tmpA[:, 0, 0:8], ident[:, :]), [b_tmpA[0], b_const], [bb])
                  V(lambda e: e.tensor_copy(xin[0:8, 0, 0:128], bk[0:8, 0:128]), [bb], [b_xin[1]])
                  for c in range(4):
                      C.dma("pool", cvp[l, :, c * 128:(c + 1) * 128], xin[2 * c:2 * c + 2, 0, 0:128], reads=[b_xin[1]], stream=2)

        if smp:
            conv_cc(); conv_ch(); conv_cb()

        if STAGE < 3:
            return
        C.mark("%s%d Tproj" % (kind, l))
        for bi in range(3):
            wv, wb = wload(A_winT, (l, bi), 8, 512)
            for t in range(nt):
                bk, bb = bank()
                for kc in range(8):
                    P(lambda e: e.matmul(bk[:, :], hT[:, kc, t * 128:(t + 1) * 128], wv[:, kc, :], start=(kc == 0), stop=(kc == 7)),
                      [wb, b_hT], [bb])
                if bi == 0:
                    if smp:
                        V(lambda e: e.tensor_tensor(rkt[:, t, :].rearrange("p (a i) -> p a i", a=2),
                                                    bk[:, 0:256].rearrange("p (a i) -> p a i", a=2), yT[:, 0:2, 384:512], ALU.mult),
                          [bb, b_const], [b_rkt])
                    else:
                        V(lambda e: e.tensor_tensor(rkt[:, t, :], bk[:, 0:256], kdec[:, :], ALU.mult), [bb, b_const], [b_rkt])
                    A(lambda e: e.copy(avt[l][:, 1 + t, :], bk[:, 384:512]), [bb], [b_avt[l]])
                    if smp or (last_group and t == nt - 1):
                        A(lambda e: e.copy(kvf[:, :], bk[:, 256:512]), [bb], [b_kvf])
                        if smp:
                            for b in range(16):
                                C.dma("pool", wks[l, b, 120:128, :], kvf[b * 8:(b + 1) * 8, 0:128], reads=[b_kvf], stream=2)
                                C.dma("pool", wvs[l, b, 120:128, :], kvf[b * 8:(b + 1) * 8, 128:256], reads=[b_kvf], stream=2)
                        else:
                            C.dma("pool", wkp[l, :, :], kvf[:, 0:128], reads=[b_kvf], stream=2)
                            C.dma("pool", wvp[l, :, :], kvf[:, 128:256], reads=[b_kvf], stream=2)
                elif bi == 1:
                    evac(rvt[:, t, :], bk[:, :], [bb], [b_rvt])
                else:
                    A(lambda e: e.activation(rgt[:, t, :], bk[:, :], AF.Silu), [bb], [b_rgt])

        if STAGE < 4:
            return
        C.mark("%s%d mixers" % (kind, l))
        if smp:
            wv, wb = fblock(12)
            for h in range(4):
                bk, bb = bank()
                fmm(bk, bb, wv, wb, 8, slice(h * 128, (h + 1) * 128), hT, b_hT, N)
                V(lambda e: e.tensor_tensor(qdp[:, h, :], bk[:, :N], xT[:, 4 + h, 384:512], ALU.mult), [bb, b_const], [b_qdp])

        NK = 256
        bias = xT[:, :, 128:384] if smp else biasP
        koff = 0
        nb = 2
        mx = st8[:, 1, :]; ng = st8[:, 2, :]; rs = st8[:, 3, :]; es = st8[:, 4, :]

        def tile_ret(t):
            tc = slice(t * 128, (t + 1) * 128)
            first_tile = (not smp) and first_group and t == 0
            bIs = [bank(), bank()]
            for h in range(4):
                hp, s = h // 2, h % 2
                pr = slice(s * 64, s * 64 + 64)
                bI, bbI = bIs[s]
                P(lambda e: e.matmul(bI[:, hp * 128:(hp + 1) * 128], zq[pr, 2 + hp, tc], zq[pr, hp, tc], start=True, stop=True),
                  [b_zq], [bbI])
            if smp:
                dmv = xT[:, 0:4, 384:512].rearrange("p (hp s) i -> p s hp i", s=2)
            else:
                dmv = dmT[:, :].rearrange("p (hp s i) -> p s hp i", hp=2, s=2)
            inv = innT[:, :, :].rearrange("p (hp s) i -> p s hp i", s=2)
            for s in range(2):
                bI, bbI = bIs[s]
                V(lambda e: e.tensor_tensor(inv[:, s], bI[:, 0:256].rearrange("p (hp i) -> p hp i", hp=2), dmv[:, s], ALU.mult),
                  [bbI, b_const], [b_innT])
            bO, bbO = bank(hold=True)
            if not smp:
                for h in range(4):
                    hp, s = h // 2, h % 2
                    pr = slice(s * 64, s * 64 + 64)
                    P(lambda e: e.matmul(bO[:, h * 128:(h + 1) * 128], innT[:, h, :], rvt[:, t, h * 128:(h + 1) * 128],
                                         start=True, stop=False), [b_innT, b_rvt], [bbO])
                    P(lambda e: e.matmul(bO[:, h * 128:(h + 1) * 128], qd[pr, hp, tc], Sb[l][pr, hp, :],
                                         start=False, stop=True), [b_qd, b_S[l]], [bbO])
                bS, bbS = bank()
                for h in range(4):
                    hp, s = h // 2, h % 2
                    pr = slice(s * 64, s * 64 + 64)
                    P(lambda e: e.matmul(bS[pr, hp * 128:(hp + 1) * 128], rkt[:, t, h * 64:(h + 1) * 64],
                                         rvt[:, t, h * 128:(h + 1) * 128], start=True, stop=True), [b_rkt, b_rvt], [bbS])
                for hp in range(2):
                    V(lambda e: e.scalar_tensor_tensor(S[l][:, hp, :], S[l][:, hp, :], cdec[:, hp:hp + 1],
                                                       bS[:, hp * 128:(hp + 1) * 128], ALU.mult, ALU.add),
                      [bbS, b_S[l], b_const], [b_S[l]])
                A(lambda e: e.copy(Sb[l][:, :, :], S[l][:, :, :]), [b_S[l]], [b_S[l]])
                if last_group and t == nt - 1:
                    for s in range(2):
                        C.dma("pool", retp[l].rearrange("(hp s) d e -> s d hp e", s=2)[s],
                              S[l][s * 64:(s + 1) * 64, :, :], reads=[b_S[l]], stream=2)
            else:
                def load_state(h):
                    for bh in range(2):
                        C.dma("pool", S0x[h % 2][bh * 64:(bh + 1) * 64, :, :],
                              sret[l, bh * 8:(bh + 1) * 8, h, :, :].rearrange("b d e -> d b e"), writes=[b_S0x[h % 2]], stream=1)

                load_state(0)
                for h in range(4):
                    S0, S0b, b_S0 = S0x[h % 2], S0bx[h % 2], b_S0x[h % 2]
                    if h + 1 < 4:
                        load_state(h + 1)
                    A(lambda e: e.copy(S0b[:, :, :], S0[:, :, :]), [b_S0], [b_S0])
                    V(lambda e: e.tensor_tensor(qdm, qdp[:, h:h + 1, :].broadcast_to([128, 8, 128]), hT[:, 0:8, 384:512], ALU.mult),
                      [b_qdp, b_const], [b_qdm])
                    P(lambda e: e.matmul(bO[:, h * 128:(h + 1) * 128], innT[:, h, :], rvt[:, t, h * 128:(h + 1) * 128],
                                         start=True, stop=False), [b_innT, b_rvt], [bbO])
                    for b in range(16):
                        bh, b2 = b // 8, b % 8
                        pr = slice(bh * 64, bh * 64 + 64)
                        P(lambda e: e.matmul(bO[:, h * 128:(h + 1) * 128], qdm[pr, b2, :], S0b[pr, b2, :],
                                             start=False, stop=(b == 15)), [b_qdm, b_S0], [bbO], inc=(b in (7, 15)))
                        if b == 7:
                            C.eng["pe"].wait_ge(C.sem["pe"], C.cnt["pe"])
                    vbd = pbf[:, :, :].rearrange("p h k -> p (h k)").rearrange("p (b e) -> p b e", b=16)
                    V(lambda e: e.tensor_tensor(vbd, rvt[:, t, h * 128:(h + 1) * 128].rearrange("p (o e) -> p o e", o=1).broadcast_to([128, 16, 128]),
                                                bmV[:, :].rearrange("p (b o) -> p b o", o=1).broadcast_to([128, 16, 128]), ALU.mult),
                      [b_rvt, b_const], [b_pbf])
                    bS1, bbS1 = bank()
                    bS2, bbS2 = bank()
                    assert bbS1 is not bbO and bbS2 is not bbO
                    for bh in range(2):
                        pr = slice(bh * 64, bh * 64 + 64)
                        for q4, (bSx, bbSx) in enumerate(((bS1, bbS1), (bS2, bbS2))):
                            P(lambda e: e.matmul(bSx[pr, :], rkt[:, t, h * 64:(h + 1) * 64],
                                                 vbd[:, bh * 8 + q4 * 4: bh * 8 + q4 * 4 + 4, :], start=True, stop=True),
                              [b_rkt, b_pbf], [bbSx])
                    c8 = float(np.float32(np.exp(np.float32(8.0) * np.log1p(-np.exp2(np.float32(-5.0 - h))))))
                    for q4, (bSx, bbSx) in enumerate(((bS1, bbS1), (bS2, bbS2))):
                        V(lambda e: e.scalar_tensor_tensor(S0[:, q4 * 4:(q4 + 1) * 4, :], S0[:, q4 * 4:(q4 + 1) * 4, :], c8,
                                                           bSx[:, :].rearrange("p (b e) -> p b e", b=4), ALU.mult, ALU.add),
                          [bbSx, b_S0], [b_S0])
                    for bh in range(2):
                        C.dma("pool", rets[l, bh * 8:(bh + 1) * 8, h, :, :].rearrange("b d e -> d b e"),
                              S0[bh * 64:(bh + 1) * 64, :, :], reads=[b_S0], stream=2)
            ssqc = st8[:, 0, 0:4]
            V(lambda e: e.memset(st8[:, 0, 0:4], 0.0), [], [b_st8])
            for h in range(4):
                A(lambda e: e.activation(ssb[:, 0, 0:128], bO[:, h * 128:(h + 1) * 128], AF.Square,
                                         accum_out=st8[:, 0, h:h + 1]), [bbO], [b_st8, b_ssb])
            A(lambda e: e.activation(st8[:, 0, 4:8], ssqc, AF.Ln, bias=epsb[:, 0:1], scale=1.0 / 128.0), [b_st8, b_const], [b_st8])
            A(lambda e: e.activation(st8[:, 0, 4:8], st8[:, 0, 4:8], AF.Exp, scale=-0.5), [b_st8], [b_st8])
            for h in range(4):
                V(lambda e: e.scalar_tensor_tensor(ortok[:, h * 128:(h + 1) * 128], bO[:, h * 128:(h + 1) * 128],
                                                   st8[:, 0, 4 + h:5 + h], rgt[:, t, h * 128:(h + 1) * 128], ALU.mult, ALU.mult),
                  [bbO, b_st8, b_rgt], [b_ortok])
            unhold(bbO)
            bT, bbT = bank()
            bTb = bT[:, :].bitcast(BF)
            for h in range(4):
                P(lambda e: e.transpose(bTb[:, h * 128:(h + 1) * 128], ortok[:, h * 128:(h + 1) * 128], identb[:, :]),
                  [b_ortok, b_const], [bbT])
            evac(oT[:, 0:4, tc], bTb[:, 0:512].rearrange("p (c n) -> p c n", c=4), [bbT], [b_oT[0]])


        def tile_att_scores(t):
            tc = slice(t * 128, (t + 1) * 128)
            first_tile = (not smp) and first_group and t == 0
            scb = []
            for half in range(2):
                b1, bb1 = bank(hold=True); b2, bb2 = bank(hold=True)
                scb.append(((b1, bb1), (b2, bb2)))
            for h in [0, 4, 1, 5, 2, 6, 3, 7]:
                c, s = h % 4, h // 4
                pr = slice(s * 64, s * 64 + 64)
                bkk, bbk = scb[h // 4][(h % 4) // 2]
                oc = (h % 2) * 256
                if smp:
                    if s == 0:
                        V(lambda e: e.tensor_tensor(aqm4, zq[:, 4 + c:5 + c, 0:128].rearrange("p (a b) i -> p a b i", a=1).broadcast_to([128, 8, 2, 128]),
                                                    hT[:, 0:8, 128:384].rearrange("p c (two i) -> p c two i", two=2), ALU.mult),
                          [b_zq, b_const], [b_aqm])
                    for b in range(16):
                        P(lambda e: e.matmul(bkk[:, oc:oc + 128], aqm_b(b)[pr, :], kcTb(b)[pr, :], start=(b == 0), stop=(b == 15)),
                          [b_aqm, b_kcT], [bbk], inc=(b == 15))
                    P(lambda e: e.matmul(bkk[:, oc + 128:oc + 256], zq[pr, 4 + c, tc], akT[l][pr, 128:256], start=True, stop=True),
                      [b_zq, b_akT[l]], [bbk])
                else:
                    P(lambda e: e.matmul(bkk[:, oc:oc + NK], zq[pr, 4 + c, tc], akT[l][pr, (t + 2) * 128 - NK:(t + 2) * 128],
                                         start=True, stop=True), [b_zq, b_akT[l]], [bbk])
            for h2 in range(4):
                bkk, bbk = scb[h2 // 2][h2 % 2]
                V(lambda e: e.scalar_tensor_tensor(ssb[:, 2 * h2:2 * h2 + 2, 0:NK],
                                                   bkk[:, :].rearrange("p (h k) -> p h k", h=2)[:, :, 0:NK], 0.125,
                                                   bias[:, 2 * h2:2 * h2 + 2, koff:256], ALU.mult, ALU.add),
                  [bbk, b_const], [b_ssb])
            for half in range(2):
                unhold(scb[half][0][1]); unhold(scb[half][1][1])
            if first_tile:
                V(lambda e: e.tensor_scalar(ssb[:, :, 0:128], ssb[:, :, 0:128], negf[:, 0:1], None, ALU.add),
                  [b_ssb, b_const], [b_ssb])
            V(lambda e: e.tensor_reduce(mx, ssb[:, :, 0:NK], AX.X, ALU.max), [b_ssb], [b_st8])
            V(lambda e: e.tensor_tensor(mx, mx, sinkR[:, l, :], ALU.max), [b_st8, b_const], [b_st8])
            V(lambda e: e.tensor_scalar(ng, mx, -1.0, None, ALU.mult), [b_st8], [b_st8])
            V(lambda e: e.memset(rs, 0.0), [], [b_st8])

        def tile_att_exp(t):
            for h in range(8):
                A(lambda e: e.activation(pbf[:, h, 0:NK], ssb[:, h, 0:NK], AF.Exp, bias=ng[:, h:h + 1], scale=1.0,
                                         accum_out=rs[:, h:h + 1]), [b_ssb, b_st8], [b_pbf, b_st8])

        def tile_att_norm(t):
            V(lambda e: e.tensor_tensor(es, sinkR[:, l, :], ng, ALU.add), [b_st8, b_const], [b_st8])
            A(lambda e: e.activation(es, es, AF.Exp), [b_st8], [b_st8])
            V(lambda e: e.tensor_tensor(rs, rs, es, ALU.add), [b_st8], [b_st8])
            V(lambda e: e.reciprocal(rs, rs), [b_st8], [b_st8])
            V(lambda e: e.tensor_tensor(pbf[:, :, 0:NK], pbf[:, :, 0:NK],
                                        rs.rearrange("p (h o) -> p h o", o=1).broadcast_to([128, 8, NK]), ALU.mult),
              [b_pbf, b_st8], [b_pbf])

        def tile_att_out(t):
            tc = slice(t * 128, (t + 1) * 128)
            first_tile = (not smp) and first_group and t == 0
            for half in range(2):
                bT, bbT = bank()
                bTb = bT[:, :].bitcast(BF)
                for hh in range(4):
                    h = half * 4 + hh
                    for blk in range(nb):
                        P(lambda e: e.transpose(bTb[:, (hh * 2 + blk) * 128:(hh * 2 + blk + 1) * 128],
                                                pbf[:, h, blk * 128:(blk + 1) * 128], identb[:, :]), [b_pbf, b_const], [bbT])
                evac(pTs[:, half * 4:(half + 1) * 4, 0:nb, :],
                     bTb[:, :].rearrange("p (h b q) -> p h b q", h=4, b=2)[:, :, 0:nb, :], [bbT], [b_pTs])
            bV, bbV = bank()
            for h in range(8):
                kv = h // 4
                po = slice((h % 2) * 64, (h % 2) * 64 + 64)
                oc = (h // 2) * 128
                if smp:
                    P(lambda e: e.matmul(bV[po, oc:oc + 128], avt[l][:, 1 + t, kv * 64:(kv + 1) * 64], pTs[:, h, 1, :],
                                         start=True, stop=False), [b_avt[l], b_pTs], [bbV], inc=False)
                    for b in range(16):
                        P(lambda e: e.matmul(bV[po, oc + b * 8:oc + (b + 1) * 8], vcb(b)[:, kv * 64:(kv + 1) * 64],
                                             pTs[:, h, 0, b * 8:(b + 1) * 8], start=False, stop=(b == 15)),
                          [b_vc, b_pTs], [bbV], inc=(b == 15))
                else:
                    for blk in range(nb):
                        slot = t + blk if nb == 2 else t + 1
                        P(lambda e: e.matmul(bV[po, oc:oc + 128], avt[l][:, slot, kv * 64:(kv + 1) * 64], pTs[:, h, blk, :],
                                             start=(blk == 0), stop=(blk == nb - 1)), [b_avt[l], b_pTs], [bbV])
            evac(oT[:, 4:8, tc], bV[:, :].rearrange("p (c n) -> p c n", c=4), [bbV], [b_oT[1]])

        ntl = nt if STAGE >= 5 else 0
        if smp:
            for t in range(ntl):
                tile_ret(t); tile_att_scores(t); tile_att_exp(t); tile_att_norm(t); tile_att_out(t)
        else:
            convq = [conv_cc, conv_ch, conv_cb]
            for t in range(ntl):
                tile_att_scores(t)
                if t > 0:
                    tile_att_out(t - 1)
                tile_att_exp(t)
                tile_ret(t)
                tile_att_norm(t)
                if convq:
                    convq.pop(0)()
            if ntl:
                tile_att_out(ntl - 1)
            while convq:
                convq.pop(0)()
        if not smp:
            A(lambda e: e.copy(akT[l][:, 0:128], akT[l][:, N:N + 128]), [b_akT[l]], [b_akT[l]])
            A(lambda e: e.copy(avt[l][:, 0, :], avt[l][:, nt, :]), [b_avt[l]], [b_avt[l]])

        if DBG and smp and l == 0:
            for c in range(12):
                V(lambda e: e.tensor_copy(tmpA[:, 0, 0:128], oT[:, c, 0:128]), [b_oT[c // 4]], [b_tmpA[0]])
                C.dma("pool", dbg[:, c, :], tmpA[:, 0, 0:128], reads=[b_tmpA[0]], stream=2)
        if STAGE < 6:
            return
        C.mark("%s%d gating" % (kind, l))
        for n in range(3):
            wbv, wbb = wload(A_wbr, (l, n), 4, 1024)
            for half in range(2):
                wv, wb = fblock(6 + n * 2 + half)
                for m in range(4):
                    dc = half * 4 + m
                    bA, bbA = bank()
                    fmm(bA, bbA, wv, wb, 8, slice(m * 128, (m + 1) * 128), hT, b_hT, N)
                    bB, bbB = bank()
                    fmm(bB, bbB, wbv, wbb, 4, slice(dc * 128, (dc + 1) * 128), oT[:, n * 4:(n + 1) * 4, :], b_oT[n], N)
                    sg = tmpA[:, dc % 2, :N]
                    A(lambda e: e.activation(sg, bA[:, :N], AF.Sigmoid), [bbA], [b_tmpA[dc % 2]])
                    if n == 0:
                        V(lambda e: e.tensor_tensor(yT[:, dc, :N], sg, bB[:, :N], ALU.mult), [b_tmpA[dc % 2], bbB], [b_yT[dc]])
                    else:
                        V(lambda e: e.tensor_tensor(sg, sg, bB[:, :N], ALU.mult), [b_tmpA[dc % 2], bbB], [b_tmpA[dc % 2]])
                        if n == 1:
                            V(lambda e: e.tensor_tensor(yT[:, dc, :N], yT[:, dc, :N], sg, ALU.add), [b_tmpA[dc % 2], b_yT[dc]], [b_yT[dc]])
                        else:
                            V(lambda e: e.tensor_tensor(mixT[:, dc, :N], yT[:, dc, :N], sg, ALU.add), [b_tmpA[dc % 2], b_yT[dc]], [b_mixT])
        C.mark("%s%d wout+norm" % (kind, l))
        st = stat_begin()
        for half in range(2):
            wv, wb = wload(A_wout, (l, half), 8, 512)
            for m in range(4):
                bk, bb = bank()
                fmm(bk, bb, wv, wb, 8, slice(m * 128, (m + 1) * 128), mixT, b_mixT, N)
                evac(yT[:, half * 4 + m, :N], bk[:, :N], [bb], [b_yT[half * 4 + m]])
                if half * 4 + m > 0:
                    stat_add(st, yT, b_yT, half * 4 + m - 1, N)
        stat_add(st, yT, b_yT, 7, N)
        st_x = post_norm_add(l, 1, N, st)

        if STAGE < 7:
            return
        C.mark("%s%d ffn" % (kind, l))
        C.barrier()
        pre_norm(l, 2, N, st_x)
        if prefetch is not None:
            prefetch()
        for blk in range(8):
            wv, wb = wload(A_wff1, (l, blk), 8, 512)
            pre = fmm4(wv, wb, N) if blk == 0 else None
            for m in range(4):
                if pre is not None:
                    bk, bb = pre[m]
                    unhold(bb)
                else:
                    bk, bb = bank()
                    fmm(bk, bb, wv, wb, 8, slice(m * 128, (m + 1) * 128), hT, b_hT, N)
                r = tmpA[:, m % 2, :N]
                A(lambda e: e.activation(r, bk[:, :N], AF.Relu), [bb], [b_tmpA[m % 2]])
                V(lambda e: e.tensor_tensor(hidT[:, blk * 4 + m, :N], r, r, ALU.mult), [b_tmpA[m % 2]], [b_hid])
        for half in range(2):
            acc = [bank(hold=True) for _ in range(4)]
            for kg in range(4):
                wv, wb = wload(A_wff2, (l, half * 4 + kg), 8, 512)
                for kc in range(8):
                    for m in range(4):
                        bk, bb = acc[m]
                        P(lambda e: e.matmul(bk[:, :N], wv[:, kc, m * 128:(m + 1) * 128], hidT[:, kg * 8 + kc, :N],
                                             start=(kg == 0 and kc == 0), stop=(kg == 3 and kc == 7)), [wb, b_hid], [bb])
            for m in range(4):
                bk, bb = acc[m]
                evac(yT[:, half * 4 + m, :N], bk[:, :N], [bb], [b_yT[half * 4 + m]])
                unhold(bb)
        post_norm_add_ffn = None
        st = stat_begin()
        for c in range(8):
            stat_add(st, yT, b_yT, c, N)
        st_x = post_norm_add(l, 3, N, st)

        if STAGE < 8:
            return
        C.mark("%s%d ple" % (kind, l))
        pre_norm(l, 4, N, st_x)
        wpv, wpb = wload(A_wpp, (l,), 2, 1024)
        for half in range(2):
            wv, wb = wload(A_wpg, (l, half), 8, 512)
            pre = fmm4(wv, wb, N) if half == 0 else None
            for m in range(4):
                dc = half * 4 + m
                if pre is not None:
                    bA, bbA = pre[m]
                    unhold(bbA)
                else:
                    bA, bbA = bank()
                    fmm(bA, bbA, wv, wb, 8, slice(m * 128, (m + 1) * 128), hT, b_hT, N)
                bB, bbB = bank()
                fmm(bB, bbB, wpv, wpb, 2, slice(dc * 128, (dc + 1) * 128), pT, b_pT, N)
                sg = tmpA[:, dc % 2, :N]
                A(lambda e: e.activation(sg, bA[:, :N], AF.Sigmoid), [bbA], [b_tmpA[dc % 2]])
                V(lambda e: e.tensor_tensor(sg, sg, bB[:, :N], ALU.mult), [b_tmpA[dc % 2], bbB], [b_tmpA[dc % 2]])
                V(lambda e: e.tensor_tensor(yT[:, dc, :N], xT[:, dc, :N], sg, ALU.add), [b_tmpA[dc % 2], b_xT[dc]], [b_yT[dc]])

    def load_x(src, nt):
        for t in range(nt):
            C.dma("pool", xin[:, 0, :], src[t * 128:(t + 1) * 128, :], writes=[b_xin[t % 2]], stream=1)
            for hf in range(2):
                bk, bb = bank()
                for j in range(4):
                    c = hf * 4 + j
                    P(lambda e: e.transpose(bk[:, j * 128:(j + 1) * 128], xin[:, 0, c * 128:(c + 1) * 128], ident[:, :]),
                      [b_xin[t % 2], b_const], [bb])
                evac(xT[:, hf * 4:(hf + 1) * 4, t * 128:(t + 1) * 128], bk[:, :].rearrange("p (c n) -> p c n", c=4), [bb], [b_xT[hf * 4:(hf + 1) * 4]])

    def load_p(src, nt, pslot=0):
        pT = pT2[:, pslot]; b_pT = b_pT2[pslot]
        for t in range(nt):
            C.dma("pool", pin[:, 0, :], src[t * 128:(t + 1) * 128, :], writes=[b_pin[t % 2]], stream=1)
            bk, bb = bank()
            for j in range(2):
                P(lambda e: e.transpose(bk[:, j * 128:(j + 1) * 128], pin[:, 0, j * 128:(j + 1) * 128], ident[:, :]),
                  [b_pin[t % 2], b_const], [bb])
            evac(pT[:, :, t * 128:(t + 1) * 128], bk[:, 0:256].rearrange("p (c n) -> p c n", c=2), [bb], [b_pT])

    def store_x(dst, nt, xT=None, b_xT=None):
        xT = yT if xT is None else xT
        b_xT = b_yT if b_xT is None else b_xT
        for t in range(nt):
            for hf in range(2):
                bk, bb = bank()
                for j in range(4):
                    c = hf * 4 + j
                    P(lambda e: e.transpose(bk[:, j * 128:(j + 1) * 128], xT[:, c, t * 128:(t + 1) * 128], ident[:, :]),
                      [b_xT[c], b_const], [bb])
                evac(xin[:, 0, hf * 512:(hf + 1) * 512], bk[:, :], [bb], [b_xin[t % 2]])
            C.dma("pool", dst[t * 128:(t + 1) * 128, :], xin[:, 0, :], reads=[b_xin[t % 2]], stream=2)

    ngp = NTP // 4

    def phase1(l, g, last, next_load=None):
        C.mark("phase1 %d" % l)
        N, nt = 512, 4
        rkt = arv(11264, nt * 256 * 2, BF, "p (t n) -> p t n", t=nt)
        rvt = arv(13312, nt * 512 * 2, BF, "p (t n) -> p t n", t=nt)
        pkg = arv(20480, PK * 4, F32)
        b_rkt = Buf(); b_rvt = Buf()
        C.barrier()
        pre_norm(l, 0, N)
        if next_load is not None:
            next_load()
        for bi in range(2):
            wv, wb = wload(A_winT, (l, bi), 8, 512)
            for t in range(nt):
                bk, bb = bank()
                for kc in range(8):
                    P(lambda e: e.matmul(bk[:, :], hT[:, kc, t * 128:(t + 1) * 128], wv[:, kc, :], start=(kc == 0), stop=(kc == 7)),
                      [wb, b_hT], [bb])
                if bi == 0:
                    V(lambda e: e.tensor_tensor(rkt[:, t, :], bk[:, 0:256], kdec[:, :], ALU.mult), [bb, b_const], [b_rkt])
                    if last and t == nt - 1:
                        A(lambda e: e.copy(pkg[:, 384:512], bk[:, 384:512]), [bb], [b_pkg])
                else:
                    evac(rvt[:, t, :], bk[:, :], [bb], [b_rvt])
        for t in range(nt):
            bS, bbS = bank()
            for h in range(4):
                hp, s_ = h // 2, h % 2
                pr = slice(s_ * 64, s_ * 64 + 64)
                P(lambda e: e.matmul(bS[pr, hp * 128:(hp + 1) * 128], rkt[:, t, h * 64:(h + 1) * 64],
                                     rvt[:, t, h * 128:(h + 1) * 128], start=True, stop=True), [b_rkt, b_rvt], [bbS])
            for hp in range(2):
                V(lambda e: e.scalar_tensor_tensor(S[l][:, hp, :], S[l][:, hp, :], cdec[:, hp:hp + 1],
                                                   bS[:, hp * 128:(hp + 1) * 128], ALU.mult, ALU.add),
                  [bbS, b_S[l], b_const], [b_S[l]])
        if last:
            tcl = slice(384, 512)
            wv, wb = wload(A_winF, (l, 2), 8, 128)
            bk, bb = bank()
            for kc in range(8):
                P(lambda e: e.matmul(bk[:, 0:128], wv[:, kc, 0:128], hT[:, kc, tcl], start=(kc == 0), stop=(kc == 7)), [wb, b_hT], [bb])
            A(lambda e: e.copy(pkg[:, 256:384], bk[:, 0:128]), [bb], [b_pkg])
            cc2 = tmpA[:, 0, 0:8]
            wv, wb = wload(A_winF, (l, 3), 8, 512)
            for c in range(4):
                bk, bb = bank()
                for kc in range(8):
                    P(lambda e: e.matmul(bk[:, 0:128], wv[:, kc, c * 128:(c + 1) * 128], hT[:, kc, tcl], start=(kc == 0), stop=(kc == 7)),
                      [wb, b_hT], [bb])
                A(lambda e: e.copy(cc2[:, 2 * c:2 * c + 2], bk[:, 126:128]), [bb], [b_tmpA[0]])
            wv, wb = wload(A_winF, (l, 4), 8, 512)
            for c in range(4):
                bk, bb = bank()
                for kc in range(8):
                    P(lambda e: e.matmul(bk[:, 0:128], wv[:, kc, c * 128:(c + 1) * 128], hT[:, kc, tcl], start=(kc == 0), stop=(kc == 7)),
                      [wb, b_hT], [bb])
                V(lambda e: e.tensor_tensor(pkg[:, 512 + 2 * c:514 + 2 * c], cc2[:, 2 * c:2 * c + 2], bk[:, 126:128], ALU.mult),
                  [bb, b_tmpA[0]], [b_pkg])
            V(lambda e: e.tensor_copy(pkg[:, 0:256], S[l][:, :, :].rearrange("p a e -> p (a e)")), [b_S[l]], [b_pkg])

    def exchange(l):
        C.mark("exchange %d" % l)
        pkg = arv(20480, PK * 4, F32)
        C.dma("pool", pkg_in[l], pkg, reads=[b_pkg], writes=[b_pki[l]], own=b_pkg)
        C._wait("pool", C._deps("pool", [b_pki[l]], [b_pko[l]]))
        ins = nc.gpsimd.collective_compute("AllGather", ALU.bypass,
                                           replica_groups=[[2 * i, 2 * i + 1] for i in range(NCORES // 2)],
                                           ins=[pkg_in[l].opt()], outs=[pkg_out[l].opt()])
        k = "cc%d" % l
        C.sem[k] = C.es.enter_context(nc.semaphore("s_" + k)); C.cnt[k] = 1
        ins.then_inc(C.sem[k])
        b_pki[l].r[k] = 1
        b_pko[l].w = (k, 1); b_pko[l].r = {}
        C.barrier()
        b_g = Buf()
        gbuf = arv(0, 4 * PK * 4, F32, "p (r n) -> p r n", r=4)
        for r0 in range(0, GS, 4):
            nr = min(4, GS - r0)
            C.dma("pool", gbuf[:, 0:nr, :], pkg_out[l][r0 * 128:(r0 + nr) * 128, :].rearrange("(r p) n -> p r n", p=128),
                  reads=[b_pko[l]], writes=[b_g])
            for r in range(nr):
                if r0 + r == 0:
                    V(lambda e: e.tensor_scalar(pkg, gbuf[:, r, :], sel8[:, 0:1], None, ALU.mult), [b_g, b_const], [b_pkg])
                else:
                    V(lambda e: e.scalar_tensor_tensor(pkg, gbuf[:, r, :], sel8[:, r0 + r:r0 + r + 1], pkg, ALU.mult, ALU.add),
                      [b_g, b_const, b_pkg], [b_pkg])
        V(lambda e: e.tensor_copy(S[l][:, :, :].rearrange("p a e -> p (a e)"), pkg[:, 0:256]), [b_pkg], [b_S[l]])
        A(lambda e: e.copy(Sb[l][:, :, :].rearrange("p a e -> p (a e)"), pkg[:, 0:256]), [b_pkg], [b_S[l]])
        A(lambda e: e.copy(akT[l][:, 0:128], pkg[:, 256:384]), [b_pkg], [b_akT[l]])
        A(lambda e: e.copy(avt[l][:, 0, :], pkg[:, 384:512]), [b_pkg], [b_avt[l]])
        V(lambda e: e.tensor_copy(uhalo[:, l, :, :].rearrange("p c j -> p (c j)"), pkg[:, 512:520]), [b_pkg], [b_uhalo])
        C.barrier()

    b_pkg = Buf("pkg"); b_pki = [Buf(), Buf()]; b_pko = [Buf(), Buf()]

    def xT_load(src, bsrc):
        C.dma("act", xT[:, :, :], src, reads=[bsrc], writes=[b_xT], own=b_xT[0])

    def xT_save(dst, bdst, N=512, src=None, bsrc=None):
        src = xT if src is None else src
        bsrc = b_xT if bsrc is None else bsrc
        C.dma("pool", dst[:, :, 0:N], src[:, :, 0:N], reads=[bsrc], writes=[bdst], own=bsrc[0])

    passes = []
    for l in range(DEPTH):
        for g in range(ngp):
            passes.append((l, "p", g))
        if SAMPLE:
            passes.append((l, "s", 0))
    p_loaded = set()
    p_staged = set()

    def stage_p(i):
        if i >= len(passes) or i in p_loaded:
            return
        l_, k_, g_ = passes[i]
        if k_ != "p":
            return
        C.dma("pool", xin[:, 0, :].rearrange("p (t f) -> p t f", t=4),
              pp[l_, g_ * 512:(g_ + 1) * 512, :].rearrange("(t p) f -> p t f", p=128), writes=[_bx], stream=1)
        p_staged.add(i)

    def stage_ps(i):
        if i >= len(passes) or i in p_loaded or passes[i][1] != "s":
            return
        C.dma("pool", pin[:, 0, :], ps_[passes[i][0]], writes=[_bp], stream=1)
        p_staged.add(i)

    def finish_p(i):
        if i not in p_staged:
            return
        pT = pT2[:, i % 2]; b_pT = b_pT2[i % 2]
        if passes[i][1] == "s":
            bk, bb = bank()
            for j in range(2):
                P(lambda e: e.transpose(bk[:, j * 128:(j + 1) * 128], pin[:, 0, j * 128:(j + 1) * 128], ident[:, :]),
                  [_bp, b_const], [bb])
            evac(pT[:, :, 0:128], bk[:, 0:256].rearrange("p (c n) -> p c n", c=2), [bb], [b_pT])
            p_loaded.add(i)
            return
        for t in range(4):
            bk, bb = bank()
            for j in range(2):
                P(lambda e: e.transpose(bk[:, j * 128:(j + 1) * 128], xin[:, 0, t * 256 + j * 128:t * 256 + (j + 1) * 128], ident[:, :]),
                  [_bx, b_const], [bb])
            evac(pT[:, :, t * 128:(t + 1) * 128], bk[:, 0:256].rearrange("p (c n) -> p c n", c=2), [bb], [b_pT])
        p_loaded.add(i)

    def ensure_p(i):
        if i >= len(passes) or i in p_loaded:
            return
        l_, k_, g_ = passes[i]
        if k_ == "p":
            load_p(pp[l_, g_ * 512:(g_ + 1) * 512, :], 4, i % 2)
        else:
            load_p(ps_[l_], 1, i % 2)
        p_loaded.add(i)

    pi = 0
    for l in range(DEPTH):
        for g in range(ngp):
            if l == 0:
                load_x(xp[g * 512:(g + 1) * 512, :], 4)
                xT_save(xs0[g], b_xs0[g])
                nl = None
            else:
                if g == 0:
                    xT_load(xs1[g], b_xs1[g])
                nl = (lambda gg=g + 1: xT_load(xs1[gg], b_xs1[gg])) if g + 1 < ngp else None
            phase1(l, g, last=(g == ngp - 1), next_load=nl)
        exchange(l)
        for g in range(ngp):
            C.mark("io %d" % l)
            if l == 0:
                xT_load(xs0[g], b_xs0[g])
            else:
                xT_load(xs1[g], b_xs1[g])
            ensure_p(pi)
            if g < ngp - 1:
                stage_p(pi + 1)
            else:
                stage_ps(pi + 1)
            process_layer(l, "p", 512, first_group=(g == 0), last_group=(g == ngp - 1), pslot=pi % 2,
                          prefetch=(lambda i=pi + 1: finish_p(i)))
            pi += 1
            C.mark("io %d" % l)
            if l == 0:
                xT_save(xs1[g], b_xs1[g], src=yT, bsrc=b_yT)
            else:
                store_x(yp[g * 512:(g + 1) * 512, :], 4)
        if SAMPLE:
            if l == 0:
                load_x(xs, 1)
            else:
                C.dma("pool", xT[:, :, 0:128], xs1[ngp][:, :, 0:128], reads=[b_xs1[ngp]], writes=[b_xT], own=b_xT[0])
            ensure_p(pi)
            process_layer(l, "s", 128, first_group=False, last_group=False, pslot=pi % 2)
            pi += 1
            if l == 0:
                xT_save(xs1[ngp], b_xs1[ngp], 128, src=yT, bsrc=b_yT)
            else:
                store_x(ys, 1)
    C.mark("end")
    C.final_wait("sp")
    C.close()
    if os.environ.get("MK_MARKS"):
        import json
        json.dump(C.marks, open(os.environ["MK_MARKS"], "w"))
    return nc, C


def host_consts():
    f = np.float32
    hh = np.arange(4, dtype=f)
    lg = np.log1p(-np.exp2(-5.0 - hh)).astype(f)
    i = np.arange(128, dtype=f)
    c = {}
    c["c_ident"] = np.eye(128, dtype=f)
    c["c_ones"] = np.full((128, 128), 1.0 / 1024.0, f)
    diff = i[None, :] - i[:, None]
    dm = np.where(diff[None] >= 0, np.exp(lg[:, None, None] * np.maximum(diff[None], 0.0)), 0.0).astype(f) * f(0.125)
    c["c_dmT"] = np.ascontiguousarray(dm.transpose(1, 0, 2)).reshape(128, 512)
    seq = (np.arange(128) // 8)
    tt = (np.arange(128) % 8).astype(f)
    same = (seq[:, None] == seq[None, :])
    dts = tt[None, :] - tt[:, None]
    dms = np.where(same[None] & (dts[None] >= 0), np.exp(lg[:, None, None] * np.maximum(dts[None], 0.0)), 0.0).astype(f) * f(0.125)
    c["c_dmTs"] = np.ascontiguousarray(dms.transpose(1, 0, 2)).reshape(128, 512)
    qd = np.exp(lg[:, None] * (i[None, :] + 1.0)).astype(f)
    q2 = np.zeros((128, 2, 128), f)
    for hp in range(2):
        for s in range(2):
            q2[s * 64:(s + 1) * 64, hp, :] = qd[2 * hp + s][None, :]
    c["c_qdec"] = q2.reshape(128, 256)
    qds = np.exp(lg[:, None] * (tt[None, :] + 1.0)).astype(f)
    c["c_qdecs"] = np.ascontiguousarray(np.broadcast_to(qds[None], (128, 4, 128))).reshape(128, 512)
    kd = np.exp(lg[:, None] * (127.0 - i[None, :])).astype(f) * f(0.125)
    c["c_kdec"] = np.ascontiguousarray(np.repeat(kd.T[:, :, None], 64, axis=2)).reshape(128, 256)
    kds = np.exp(lg[:, None] * (7.0 - tt[None, :])).astype(f) * f(0.125)
    c["c_kdecs"] = np.ascontiguousarray(np.repeat(kds.T[:, :, None], 64, axis=2)).reshape(128, 256)
    cd = np.exp(lg * f(128.0)).astype(f)
    c2 = np.zeros((128, 2), f)
    for hp in range(2):
        for s in range(2):
            c2[s * 64:(s + 1) * 64, hp] = cd[2 * hp + s]
    c["c_cdec"] = c2
    slopes = np.exp2(-8.0 * (np.arange(8, dtype=f) + 1.0) / 8.0).astype(f)
    q = np.arange(128)[:, None]; kk = np.arange(256)[None, :]
    dist = (128 + q - kk)
    allowed = (dist >= 0) & (dist < 128)
    bp = np.where(allowed[:, None, :], -slopes[None, :, None] * dist[:, None, :].astype(f), f(NEG)).astype(f)
    c["c_biasP"] = np.ascontiguousarray(bp).reshape(128, 2048)
    ti = (np.arange(128) % 8)[:, None]; si = (np.arange(128) // 8)[:, None]
    kc_ = np.arange(128)[None, :]
    dist_c = 128 + ti - kc_
    al_c = dist_c < 128
    tj = (np.arange(128) % 8)[None, :]; sj = (np.arange(128) // 8)[None, :]
    dist_n = ti - tj
    al_n = (si == sj) & (dist_n >= 0)
    dist_s = np.concatenate([dist_c, dist_n], axis=1)
    al_s = np.concatenate([al_c, al_n], axis=1)
    bs = np.where(al_s[:, None, :], -slopes[None, :, None] * dist_s[:, None, :].astype(f), f(NEG)).astype(f)
    c["c_biasS"] = np.ascontiguousarray(bs).reshape(128, 2048)
    bmq = (np.arange(16)[:, None] == seq[None, :]).astype(f)
    c["c_bmQ"] = np.ascontiguousarray(np.broadcast_to(bmq[None], (128, 16, 128))).reshape(128, 2048)
    bq2 = np.zeros((128, 8, 128), f)
    for bh in range(2):
        bq2[bh * 64:(bh + 1) * 64] = (np.arange(8)[:, None] + bh * 8 == seq[None, :]).astype(f)[None]
    c["c_bmQ2"] = bq2.reshape(128, 1024)
    c["c_bmV"] = (seq[:, None] == np.arange(16)[None, :]).astype(f)
    return c


def blk(w, cols=None):
    if cols is not None:
        w = w[:, cols]
    K, n = w.shape
    return np.ascontiguousarray(w.reshape(K // 128, 128, n).transpose(1, 0, 2))


def host_weights(w_in, w_branch, w_out, w_ff1, w_ff2, w_ple_gate, w_ple_proj):
    r = lambda a, n: np.arange(a, a + n)
    aqperm = np.concatenate([np.concatenate([r(OFF["aq"] + c * 64, 64), r(OFF["aq"] + (c + 4) * 64, 64)]) for c in range(4)])
    rqdup = np.concatenate([np.concatenate([r(OFF["rq"] + h * 64, 64), r(OFF["rq"] + h * 64, 64)]) for h in range(4)])
    fcols = [np.concatenate([r(OFF["rq"], 256), r(OFF["rk"], 256)]), aqperm, np.tile(r(OFF["ak"], 128), 4),
             r(OFF["cc"], 512), r(OFF["ch"], 512), r(OFF["cb"], 512)]
    for n in range(3):
        for half in range(2):
            fcols.append(r(OFF["gt"] + n * 1024 + half * 512, 512))
    fcols.append(rqdup)
    tcols = [np.concatenate([r(OFF["rk"], 256), r(OFF["ak"], 128), r(OFF["av"], 128)]), r(OFF["rv"], 512), r(OFF["rg"], 512)]
    out = {}
    out["winF"] = np.stack([np.stack([blk(w_in[l], cc) for cc in fcols]) for l in range(DEPTH)])
    out["winT"] = np.stack([np.stack([blk(w_in[l], cc) for cc in tcols]) for l in range(DEPTH)])
    out["wbr"] = np.stack([np.stack([blk(w_branch[l, n]) for n in range(3)]) for l in range(DEPTH)])
    out["wout"] = np.stack([np.stack([blk(w_out[l][:, h * 512:(h + 1) * 512]) for h in range(2)]) for l in range(DEPTH)])
    out["wff1"] = np.stack([np.stack([blk(w_ff1[l][:, b * 512:(b + 1) * 512]) for b in range(8)]) for l in range(DEPTH)])
    out["wff2"] = np.stack([np.stack([blk(w_ff2[l][kg * 1024:(kg + 1) * 1024, h * 512:(h + 1) * 512])
                                      for h in range(2) for kg in range(4)]) for l in range(DEPTH)])
    out["wpg"] = np.stack([np.stack([blk(w_ple_gate[l][:, h * 512:(h + 1) * 512]) for h in range(2)]) for l in range(DEPTH)])
    out["wpp"] = np.stack([blk(w_ple_proj[l]) for l in range(DEPTH)])
    return out


_CACHE = {}


def run(inputs, NTP, n_cores, seq_of_core, samp_of_core, SAMPLE=True):
    key = (NTP, SAMPLE, n_cores)
    if key not in _CACHE:
        _CACHE[key] = build(NTP, SAMPLE, n_cores)
    nc, C = _CACHE[key]
    f = np.float32
    g = lambda k: np.asarray(inputs[k], dtype=f)
    shared = host_consts()
    shared.update(host_weights(g("w_in"), g("w_branch"), g("w_out"), g("w_ff1"), g("w_ff2"), g("w_ple_gate"), g("w_ple_proj")))
    gs = np.stack([g("g_mix_pre"), g("g_mix_post"), g("g_ffn_pre"), g("g_ffn_post"), g("g_ple")], axis=1)
    shared["gT"] = np.ascontiguousarray(gs.reshape(DEPTH, 5, 8, 128).transpose(3, 0, 1, 2))
    shared["cwT"] = np.ascontiguousarray(g("conv_w").reshape(DEPTH, 3, 4, 128).transpose(3, 0, 2, 1))
    shared["sinkR"] = np.ascontiguousarray(np.broadcast_to(g("attn_sinks")[None], (128, DEPTH, 8)))
    TP = NTP * 128
    xpr, ppr, xsm, psm = g("x_prompt"), g("p_prompt"), g("x_sample"), g("p_sample")
    sr, ckk, cvv, scc = g("state_ret"), g("cache_win_k"), g("cache_win_v"), g("state_conv")
    in_maps = []
    for c in range(n_cores):
        sq_, t0 = seq_of_core[c]
        b0 = samp_of_core[c]
        m = dict(shared)
        m["xp"] = np.ascontiguousarray(xpr[sq_, t0:t0 + TP])
        m["pp"] = np.ascontiguousarray(ppr[:, sq_, t0:t0 + TP])
        m["xs"] = np.ascontiguousarray(xsm[b0:b0 + 16].reshape(128, D))
        m["ps"] = np.ascontiguousarray(psm[:, b0:b0 + 16].reshape(DEPTH, 128, 256))
        m["sret"] = np.ascontiguousarray(sr[:, b0:b0 + 16])
        m["ck"] = np.ascontiguousarray(ckk[:, b0:b0 + 16].reshape(DEPTH, 16, 128, 128))
        m["cv"] = np.ascontiguousarray(cvv[:, b0:b0 + 16].reshape(DEPTH, 16, 128, 128))
        m["scv"] = np.ascontiguousarray(scc[:, b0:b0 + 16].reshape(DEPTH, 32, 512))
        sel = np.zeros((128, 8), f)
        if c % 2 == 1:
            sel[:, 0] = 1.0
        m["sel8"] = sel
        m["negf"] = np.full((128, 1), NEG if c % 2 == 0 else 0.0, f)
        in_maps.append(m)
    res = run_bass_kernel_spmd(nc, in_maps, core_ids=list(range(n_cores)))
    return res.results


def kernel(**inputs):
    NTP = 16
    n = 8
    seq_of_core = [(c // 2, (c % 2) * 2048) for c in range(n)]
    samp_of_core = [16 * c for c in range(n)]
    R = run(inputs, NTP, n, seq_of_core, samp_of_core)
    f = np.float32
    yp = np.stack([np.concatenate([R[2 * i]["yp"], R[2 * i + 1]["yp"]], axis=0) for i in range(4)]).astype(f)
    ys = np.concatenate([R[c]["ys"].reshape(16, 8, D) for c in range(n)]).astype(f)
    odd = [1, 3, 5, 7]
    retp = np.stack([R[c]["retp"] for c in odd], axis=1).astype(f)
    wkp = np.stack([R[c]["wkp"].reshape(DEPTH, 128, 2, 64) for c in odd], axis=1).astype(f)
    wvp = np.stack([R[c]["wvp"].reshape(DEPTH, 128, 2, 64) for c in odd], axis=1).astype(f)
    cvp = np.stack([R[c]["cvp"] for c in odd], axis=1).astype(f)
    rets = np.concatenate([R[c]["rets"] for c in range(n)], axis=1).astype(f)
    wks = np.concatenate([R[c]["wks"].reshape(DEPTH, 16, 128, 2, 64) for c in range(n)], axis=1).astype(f)
    wvs = np.concatenate([R[c]["wvs"].reshape(DEPTH, 16, 128, 2, 64) for c in range(n)], axis=1).astype(f)
    cvs = np.concatenate([R[c]["cvs"].reshape(DEPTH, 16, 2, 512) for c in range(n)], axis=1).astype(f)
    return (yp, ys, retp, wkp, wvp, cvp, rets, wks, wvs, cvs)
```

```python
import contextlib
import numpy as np
import concourse.bass as bass
import concourse.mybir as mybir
from concourse.bass_utils import run_bass_kernel_spmd

F32 = mybir.dt.float32
BF = mybir.dt.bfloat16
ALU = mybir.AluOpType
AF = mybir.ActivationFunctionType
AX = mybir.AxisListType

D = 1024
DEPTH = 2
NIN = 6912
EPS = 1e-6
OFF = dict(rq=0, rk=256, rv=512, rg=1024, aq=1536, ak=2048, av=2176, cb=2304, cc=2816, ch=3328, gt=3840)
NEG = -30000.0
import os
STAGE = int(os.environ.get("MK_STAGE", "9"))


class Buf:
    __slots__ = ("name", "w", "r", "x", "sk")

    def __init__(self, name="", x=False):
        self.name = name
        self.w = None
        self.r = {}
        self.sk = None
        self.x = x


class Ctx:
    def __init__(self, nc, n_streams=4):
        self.nc = nc
        self.es = contextlib.ExitStack()
        self.eng = {"pe": nc.tensor, "act": nc.scalar, "dve": nc.vector, "pool": nc.gpsimd, "sp": nc.sync}
        self.sem = {}
        self.cnt = {}
        for k in self.eng:
            self.sem[k] = self.es.enter_context(nc.semaphore("s_" + k))
            self.cnt[k] = 0
        for i in range(n_streams):
            k = "d%d" % i
            self.sem[k] = self.es.enter_context(nc.semaphore("s_" + k))
            self.cnt[k] = 0
        self.seen = {k: {} for k in self.eng}
        self.n_ins = {k: 0 for k in self.eng}
        self.marks = []

    def mark(self, label):
        self.marks.append((label, self.n_ins["pe"]))

    def close(self):
        self.es.close()

    @staticmethod
    def _flat(bs):
        out = []
        for b in bs:
            if isinstance(b, (list, tuple)):
                out.extend(Ctx._flat(b))
            else:
                out.append(b)
        return out

    def _deps(self, e, reads, writes):
        deps = {}
        xr = [b for b in reads if b.x]
        if xr:
            writes = list(writes) + xr
        for b in reads:
            if b.w is not None and deps.get(b.w[0], 0) < b.w[1]:
                deps[b.w[0]] = b.w[1]
        for b in writes:
            if b.w is not None and deps.get(b.w[0], 0) < b.w[1]:
                deps[b.w[0]] = b.w[1]
            for f, v in b.r.items():
                if f != e and deps.get(f, 0) < v:
                    deps[f] = v
        if e == "pe":
            deps.pop("pe", None)
        return deps

    def _wait(self, e, deps):
        for f, v in deps.items():
            if self.seen[e].get(f, 0) < v:
                self.eng[e].wait_ge(self.sem[f], v)
                self.seen[e][f] = v

    def op(self, e, fn, reads=(), writes=(), inc=True):
        reads = self._flat(reads); writes = self._flat(writes)
        self._wait(e, self._deps(e, reads, writes))
        ins = fn(self.eng[e])
        self.n_ins[e] += 1
        if inc:
            ins.then_inc(self.sem[e], 1)
            self.cnt[e] += 1
            idx = self.cnt[e]
        else:
            idx = self.cnt[e] + 1
        for b in reads:
            if b.x:
                b.w = (e, idx)
                b.r = {}
            elif b.r.get(e, 0) < idx:
                b.r[e] = idx
        for b in writes:
            b.w = (e, idx)
            b.r = {}
        return ins

    def dma(self, q, out, in_, reads=(), writes=(), stream=0, own=None, **kw):
        reads = self._flat(reads); writes = self._flat(writes)
        self._wait(q, self._deps(q, reads, writes))
        if own is None:
            own = writes[0] if writes else reads[0]
        if own.sk is None:
            own.sk = {}
        cls = "sw" if q == "pool" else "hw"
        if cls not in own.sk:
            k = "m%d" % len(self.sem)
            own.sk[cls] = k
            self.sem[k] = self.es.enter_context(self.nc.semaphore("s_" + k))
            self.cnt[k] = 0
        k = own.sk[cls]
        ins = self.eng[q].dma_start(out=out, in_=in_, **kw)
        ins.then_inc(self.sem[k], 16)
        self.cnt[k] += 16
        idx = self.cnt[k]
        for b in reads:
            if b.r.get(k, 0) < idx:
                b.r[k] = idx
        for b in writes:
            b.w = (k, idx)
            b.r = {}
        return ins

    def barrier(self):
        for e in self.eng:
            if e == "sp":
                continue
            deps = {f: self.cnt[f] for f in self.cnt if f != e and self.cnt[f] > 0}
            self._wait(e, deps)

    def final_wait(self, e="sp"):
        deps = {f: self.cnt[f] for f in self.cnt if f != e and self.cnt[f] > 0}
        self._wait(e, deps)


def build(NTP, SAMPLE=True, NCORES=8):
    nc = bass.Bass("TRN2", target_bir_lowering=False)
    C = Ctx(nc)
    TP = NTP * 128

    def din(name, shape, dt=F32):
        return nc.dram_tensor(name, list(shape), dt, kind="ExternalInput").ap()

    def dout(name, shape):
        return nc.dram_tensor(name, list(shape), F32, kind="ExternalOutput").ap()

    xp = din("xp", [TP, D]); pp = din("pp", [DEPTH, TP, 256])
    xs = din("xs", [128, D]); ps_ = din("ps", [DEPTH, 128, 256])
    sret = din("sret", [DEPTH, 16, 4, 64, 128])
    ck = din("ck", [DEPTH, 16, 128, 128]); cv = din("cv", [DEPTH, 16, 128, 128])
    scv = din("scv", [DEPTH, 32, 512])
    gTd = din("gT", [128, DEPTH, 5, 8]); cwTd = din("cwT", [128, DEPTH, 4, 3]); sinkd = din("sinkR", [128, DEPTH, 8])
    winF = din("winF", [DEPTH, 13, 128, 8, 512]); winT = din("winT", [DEPTH, 3, 128, 8, 512])
    wbr = din("wbr", [DEPTH, 3, 128, 4, 1024]); wout = din("wout", [DEPTH, 2, 128, 8, 512])
    wff1 = din("wff1", [DEPTH, 8, 128, 8, 512]); wff2 = din("wff2", [DEPTH, 8, 128, 8, 512])
    wpg = din("wpg", [DEPTH, 2, 128, 8, 512]); wpp = din("wpp", [DEPTH, 128, 2, 1024])
    c_ident = din("c_ident", [128, 128]); c_ones = din("c_ones", [128, 128])
    c_dmT = din("c_dmT", [128, 512]); c_dmTs = din("c_dmTs", [128, 512])
    c_qdec = din("c_qdec", [128, 256]); c_qdecs = din("c_qdecs", [128, 512])
    c_kdec = din("c_kdec", [128, 256]); c_kdecs = din("c_kdecs", [128, 256])
    c_cdec = din("c_cdec", [128, 2])
    c_biasP = din("c_biasP", [128, 2048]); c_biasS = din("c_biasS", [128, 2048])
    c_bmQ = din("c_bmQ", [128, 2048]); c_bmQ2 = din("c_bmQ2", [128, 1024]); c_bmV = din("c_bmV", [128, 16])
    sel8d = din("sel8", [128, 8]); negfd = din("negf", [128, 1])
    NG = NTP // 4
    PK = 520
    xs0 = nc.dram_tensor("xscr0", [NG, 128, 8, 512], F32, kind="Internal").ap()
    xs1 = nc.dram_tensor("xscr1", [NG + 1, 128, 8, 512], F32, kind="Internal").ap()
    b_xs0 = [Buf() for _ in range(NG)]; b_xs1 = [Buf() for _ in range(NG + 1)]
    pkg_in = [nc.dram_tensor("pkg_in%d" % l, [128, PK], F32, kind="Internal").ap() for l in range(DEPTH)]
    GS = 2
    pkg_out = [nc.dram_tensor("pkg_out%d" % l, [GS * 128, PK], F32, kind="Internal").ap() for l in range(DEPTH)]

    yp = dout("yp", [TP, D]); ys = dout("ys", [128, D])
    retp = dout("retp", [DEPTH, 4, 64, 128]); wkp = dout("wkp", [DEPTH, 128, 128]); wvp = dout("wvp", [DEPTH, 128, 128])
    cvp = dout("cvp", [DEPTH, 2, 512])
    rets = dout("rets", [DEPTH, 16, 4, 64, 128]); wks = dout("wks", [DEPTH, 16, 128, 128]); wvs = dout("wvs", [DEPTH, 16, 128, 128])
    cvs = dout("cvs", [DEPTH, 32, 512])
    DBG = os.environ.get("MK_DBG", "0") == "1"
    if DBG:
        dbg = dout("dbg", [128, 12, 128])
    DBG2 = os.environ.get("MK_DBG", "0") in ("2", "3")
    if DBG2:
        dbgx = dout("dbgx", [128, 8, 512])

    def sb(name, shape, dt=F32):
        return nc.alloc_sbuf_tensor("sb_" + name, list(shape), dt)

    b_const = Buf("const")
    ident = sb("ident", [128, 128]); identb = sb("identb", [128, 128], BF); onesb = sb("onesb", [128, 128], BF)
    dmT = sb("dmT", [128, 512])
    qdec = sb("qdec", [128, 2, 128])
    kdec = sb("kdec", [128, 256]); cdec = sb("cdec", [128, 2])
    biasP = sb("biasP", [128, 8, 256])
    bmV = sb("bmV", [128, 16], BF)
    gT = sb("gT", [128, DEPTH, 5, 8]); cwT = sb("cwT", [128, DEPTH, 4, 3]); sinkR = sb("sinkR", [128, DEPTH, 8])
    epsb = sb("epsb", [128, 1])
    joinb = sb("joinb", [128, 1])
    sel8 = sb("sel8", [128, 8]); negf = sb("negf", [128, 1])
    C.dma("pool", sel8[:, :], sel8d, writes=[b_const], stream=1)
    C.dma("pool", negf[:, :], negfd, writes=[b_const], stream=1)
    for dst, src in [(ident[:, :], c_ident), (dmT[:, :], c_dmT),
                     (qdec[:, :, :], c_qdec.rearrange("p (a b) -> p a b", a=2)),
                     (kdec[:, :], c_kdec), (cdec[:, :], c_cdec),
                     (biasP[:, :, :], c_biasP.rearrange("p (a b) -> p a b", a=8)),
                     (gT[:, :, :, :], gTd), (cwT[:, :, :, :], cwTd), (sinkR[:, :, :], sinkd)]:
        C.dma("pool", dst, src, writes=[b_const], stream=1)
    for dst, src in [(identb[:, :], c_ident), (onesb[:, :], c_ones),
                     (bmV[:, :], c_bmV)]:
        C.dma("pool", dst, src, writes=[b_const], stream=0)
    C.op("dve", lambda e: e.memset(epsb[:, :], EPS), writes=[b_const])

    xT = sb("xT", [128, 8, 512]); b_xT = [Buf("xT%d" % c) for c in range(8)]
    hT = sb("hT", [128, 8, 512], BF); b_hT = [Buf("hT%d" % c) for c in range(8)]
    yT = sb("yT", [128, 8, 512]); b_yT = [Buf("yT%d" % c) for c in range(8)]
    sq = sb("sq", [128, 2, 512], BF); b_sq = [Buf(), Buf()]
    rstd = sb("rstd", [128, 512]); b_rstd = Buf()
    tmpA = sb("tmpA", [128, 2, 512]); b_tmpA = [Buf(), Buf()]
    oT = sb("oT", [128, 12, 512], BF); b_oT = [Buf("oT0"), Buf("oT1"), Buf("oT2")]
    mixT = sb("mixT", [128, 8, 512], BF); b_mixT = Buf("mixT")
    pT2 = sb("pT", [128, 2, 2, 512], BF); b_pT2 = [Buf("pT0"), Buf("pT1")]
    uT = sb("uT", [128, 4, 516]); b_uT = Buf("uT")
    uhalo = sb("uhalo", [128, DEPTH, 4, 2]); b_uhalo = Buf()
    akT = [sb("akT%d" % l, [128, 640], BF) for l in range(DEPTH)]; b_akT = [Buf(), Buf()]
    avt = [sb("avt%d" % l, [128, 5, 128], BF) for l in range(DEPTH)]; b_avt = [Buf(), Buf()]
    S = [sb("S%d" % l, [128, 2, 128]) for l in range(DEPTH)]; b_S = [Buf(), Buf()]
    Sb = [sb("Sb%d" % l, [128, 2, 128], BF) for l in range(DEPTH)]
    ssb = sb("ssb", [128, 8, 256]); b_ssb = Buf()
    pbf = sb("pbf", [128, 8, 256], BF); b_pbf = Buf()
    pTs = sb("pTs", [128, 8, 2, 128], BF); b_pTs = Buf()
    st8 = sb("st8", [128, 8, 8]); b_st8 = Buf()
    kvf = sb("kvf", [128, 256]); b_kvf = Buf()
    innT = sb("innT", [128, 4, 128], BF); b_innT = Buf()
    ortok = sb("ortok", [128, 512], BF); b_ortok = Buf()
    xin = sb("xin", [128, 1, 1024]); _bx = Buf(); b_xin = [_bx, _bx]
    pin = sb("pin", [128, 1, 256]); _bp = Buf(); b_pin = [_bp, _bp]
    NS = 6
    wring = sb("wring", [128, NS, 4096], BF); b_wr = [Buf("wr%d" % i) for i in range(NS)]
    AR = sb("AR", [128, 16896], BF); b_AR = Buf("AR")

    def arv(off_bytes, nbytes, dt, pat=None, **kw):
        v = AR[:, off_bytes // 2:(off_bytes + nbytes) // 2]
        if dt == F32:
            v = v.bitcast(F32)
        if pat:
            v = v.rearrange(pat, **kw)
        return v

    for l in range(DEPTH):
        C.op("dve", lambda e: e.memset(S[l][:, :, :], 0.0), writes=[b_S[l]])
        C.op("dve", lambda e: e.memset(Sb[l][:, :, :], 0.0), writes=[b_S[l]])
        C.op("dve", lambda e: e.memset(akT[l][:, :], 0.0), writes=[b_akT[l]])
        C.op("dve", lambda e: e.memset(avt[l][:, :, :], 0.0), writes=[b_avt[l]])
    C.op("dve", lambda e: e.memset(uhalo[:, :, :, :], 0.0), writes=[b_uhalo])

    psb = [nc.alloc_psum_tensor("psb%d" % i, [128, 512], F32) for i in range(8)]
    b_ps = [Buf("ps%d" % i, x=True) for i in range(8)]
    bank_i = [0]

    held = set()

    def bank(hold=False):
        i = bank_i[0]
        while i in held:
            i = (i + 1) % 8
        bank_i[0] = (i + 1) % 8
        if hold:
            held.add(i)
        return psb[i], b_ps[i]

    def unhold(bb):
        held.discard(b_ps.index(bb))

    ev_i = [0]

    def evac(out, in_, reads, writes):
        ev_i[0] = (ev_i[0] + 1) % 3
        if ev_i[0]:
            C.op("act", lambda e: e.copy(out, in_), reads, writes)
        else:
            C.op("dve", lambda e: e.tensor_copy(out, in_), reads, writes)

    wr_i = [0]

    scr_t = {}
    scr_b = {}

    def wload(arr, idx, kc, ncols):
        name, ap = arr
        if name not in scr_t:
            scr_t[name] = nc.dram_tensor("scr_" + name, list(ap.shape), BF, kind="Internal").ap()
        src = ap
        dst = scr_t[name]
        for j in idx:
            src = src[j]
            dst = dst[j]
        src = src[:, :, 0:ncols]
        dst = dst[:, :, 0:ncols]
        i = wr_i[0]
        wr_i[0] = (i + 1) % NS
        v = wring[:, i, 0:kc * ncols].rearrange("p (k n) -> p k n", k=kc)
        key = (name,) + tuple(idx)
        if key not in scr_b:
            scr_b[key] = Buf("scr")
            C.dma("pool", v, src, writes=[b_wr[i]], stream=0)
            C.dma("sp", dst, v, reads=[b_wr[i]], writes=[scr_b[key]], own=b_wr[i])
        else:
            C.dma("sp", v, dst, reads=[scr_b[key]], writes=[b_wr[i]], stream=3)
        return v, b_wr[i]

    A_winF = ("winF", winF); A_winT = ("winT", winT); A_wbr = ("wbr", wbr); A_wout = ("wout", wout)
    A_wff1 = ("wff1", wff1); A_wff2 = ("wff2", wff2); A_wpg = ("wpg", wpg); A_wpp = ("wpp", wpp)

    P = lambda fn, r, w, inc=True: C.op("pe", fn, r, w, inc)
    A = lambda fn, r, w: C.op("act", fn, r, w)
    V = lambda fn, r, w: C.op("dve", fn, r, w)

    def stat_begin():
        bk, bb = bank(hold=True)
        return {"bk": bk, "bb": bb, "n": 0}

    def stat_add(st, srcT, b_src, c, N):
        k = st["n"]
        A(lambda e: e.activation(sq[:, k % 2, :N], srcT[:, c, :N], AF.Square), [b_src[c]], [b_sq[k % 2]])
        P(lambda e: e.matmul(st["bk"][:, :N], onesb[:, :], sq[:, k % 2, :N], start=(k == 0), stop=(k == 7)),
          [b_sq[k % 2], b_const], [st["bb"]])
        st["n"] = k + 1

    def stat_finish(st, N):
        assert st["n"] == 8
        bk, bb = st["bk"], st["bb"]
        A(lambda e: e.activation(rstd[:, :N], bk[:, :N], AF.Ln, bias=epsb[:, 0:1], scale=1.0), [bb, b_const], [b_rstd])
        A(lambda e: e.activation(rstd[:, :N], rstd[:, :N], AF.Exp, scale=-0.5), [b_rstd], [b_rstd])
        unhold(bb)

    def pre_norm(l, gi, N, st=None):
        if st is None:
            st = stat_begin()
            for c in range(8):
                stat_add(st, xT, b_xT, c, N)
        stat_finish(st, N)
        for c in range(8):
            V(lambda e: e.scalar_tensor_tensor(hT[:, c, :N], xT[:, c, :N], gT[:, l, gi, c:c + 1], rstd[:, :N],
                                               ALU.mult, ALU.mult), [b_xT[c], b_rstd, b_const], [b_hT[c]])

    def post_norm_add(l, gi, N, st=None):
        if st is None:
            st = stat_begin()
            for c in range(8):
                stat_add(st, yT, b_yT, c, N)
        stat_finish(st, N)
        st2 = stat_begin()
        for c in range(8):
            V(lambda e: e.scalar_tensor_tensor(tmpA[:, c % 2, :N], yT[:, c, :N], gT[:, l, gi, c:c + 1], rstd[:, :N],
                                               ALU.mult, ALU.mult), [b_yT[c], b_rstd, b_const], [b_tmpA[c % 2]])
            C.op("pool", lambda e: e.tensor_tensor(xT[:, c, :N], xT[:, c, :N], tmpA[:, c % 2, :N], ALU.add),
                 [b_xT[c], b_tmpA[c % 2]], [b_xT[c]])
            stat_add(st2, xT, b_xT, c, N)
        return st2

    def fmm4(wv, wb, N):
        banks = [bank(hold=True) for _ in range(4)]
        for kc in range(8):
            for m in range(4):
                bk, bb = banks[m]
                P(lambda e: e.matmul(bk[:, :N], wv[:, kc, m * 128:(m + 1) * 128], hT[:, kc, :N], start=(kc == 0), stop=(kc == 7)),
                  [wb, b_hT[kc]], [bb])
        return banks

    def fmm(bk, bb, wv, wb, kcs, cols, rhsT, b_rhs, N):
        for kc in range(kcs):
            P(lambda e: e.matmul(bk[:, :N], wv[:, kc, cols], rhsT[:, kc, :N], start=(kc == 0), stop=(kc == kcs - 1)),
              [wb, b_rhs], [bb])

    def process_layer(l, kind, N, first_group, last_group, pslot=0, prefetch=None):
        pT = pT2[:, pslot]; b_pT = b_pT2[pslot]
        nt = N // 128
        smp = (kind == "s")
        zq = arv(0, 9 * N * 2, BF, "p (c n) -> p c n", c=9)
        qd = arv(9216, 2 * N * 2, BF, "p (c n) -> p c n", c=2)
        rkt = arv(11264, nt * 256 * 2, BF, "p (t n) -> p t n", t=nt)
        rvt = arv(13312, nt * 512 * 2, BF, "p (t n) -> p t n", t=nt)
        rgt = arv(17408, nt * 512 * 4, F32, "p (t n) -> p t n", t=nt)
        ccT = arv(25600, 4 * N * 4, F32, "p (c n) -> p c n", c=4)
        hidT = arv(0, 32 * N * 2, BF, "p (c n) -> p c n", c=32)
        b_zq = Buf("zq"); b_qd = Buf("qd"); b_rkt = Buf("rkt"); b_rvt = Buf("rvt"); b_rgt = Buf("rgt"); b_ccT = Buf("ccT")
        b_hid = Buf("hid")
        if smp:
            S0x = [arv(27648, 4096, F32, "p (b e) -> p b e", b=8), arv(19456, 4096, F32, "p (b e) -> p b e", b=8)]
            S0bx = [arv(31744, 2048, BF, "p (b e) -> p b e", b=8), arv(23552, 2048, BF, "p (b e) -> p b e", b=8)]
            b_S0x = [Buf(), Buf()]
            kcTb = lambda b: oT[:, b // 2, 128 + (b % 2) * 128:256 + (b % 2) * 128]
            vcb = lambda b: mixT[:, b // 2, 128 + (b % 2) * 128:256 + (b % 2) * 128]
            aqm4 = yT[:, 0:8, 128:256].bitcast(BF).rearrange("p c (two i) -> p c two i", two=2)
            aqm_b = lambda b: aqm4[:, b // 2, b % 2, :]
            qdm = yT[:, 0:8, 256:320].bitcast(BF)
            qdp = yT[:, 0:4, 320:384].bitcast(BF)
            kinx = [arv(2304, 2048, F32, "p (b s) -> p b s", b=4), arv(6400, 2048, F32, "p (b s) -> p b s", b=4)]
            vinx = [arv(4352, 2048, F32, "p (b s) -> p b s", b=4), arv(8448, 2048, F32, "p (b s) -> p b s", b=4)]
            b_kinx = [Buf(), Buf()]; b_vinx = [Buf(), Buf()]
            b_kcT = Buf(); b_vc = Buf(); b_aqm = Buf(); b_qdm = Buf(); b_qdp = Buf()
        C.barrier()

        if smp:
            b_sc = [Buf("sc%d" % i) for i in range(6)]
            C.dma("pool", xT[:, :, 128:384], c_biasS.rearrange("p (a b) -> p a b", a=8), writes=[b_sc[0]], stream=1)
            C.dma("pool", xT[:, 0:4, 384:512], c_dmTs.rearrange("p (h i) -> p h i", h=4), writes=[b_sc[1]], stream=1)
            C.dma("pool", xT[:, 4:8, 384:512], c_qdecs.rearrange("p (h i) -> p h i", h=4), writes=[b_sc[2]], stream=1)
            C.dma("pool", yT[:, 0:2, 384:512], c_kdecs.rearrange("p (a i) -> p a i", a=2), writes=[b_sc[3]], stream=1)
            C.dma("pool", hT[:, 0:8, 128:384], c_bmQ.rearrange("p (c x) -> p c x", c=8), writes=[b_sc[4]], stream=1)
            C.dma("pool", hT[:, 0:8, 384:512], c_bmQ2.rearrange("p (c x) -> p c x", c=8), writes=[b_sc[5]], stream=1)
            V(lambda e: e.memset(joinb[:, :], 0.0), [b_sc], [b_const])
            for two in range(2):
                C.dma("pool", mixT[:, 0:8, 128 + two * 128:256 + two * 128],
                      cv[l].rearrange("(c two) s f -> two s c f", two=2)[two], writes=[b_vc], stream=0)
            for b4 in range(4):
                kin, vin = kinx[b4 % 2], vinx[b4 % 2]
                b_kin, b_vin = b_kinx[b4 % 2], b_vinx[b4 % 2]
                C.dma("pool", kin, ck[l, b4 * 4:(b4 + 1) * 4].rearrange("b s f -> s b f"), writes=[b_kin], stream=1)
                C.dma("pool", vin, cv[l, b4 * 4:(b4 + 1) * 4].rearrange("b s f -> s b f"), writes=[b_vin], stream=1)
                bk, bb = bank()
                for j in range(4):
                    P(lambda e: e.transpose(bk[:, j * 128:(j + 1) * 128], kin[:, j, :], ident[:, :]), [b_kin, b_const], [bb])
                evac(oT[:, 2 * b4:2 * b4 + 2, 128:384].rearrange("p c (two s) -> p c two s", two=2),
                     bk[:, :].rearrange("p (c two s) -> p c two s", c=2, two=2), [bb], [b_kcT])
                C.dma("pool", wks[l, b4 * 4:(b4 + 1) * 4, 0:120, :].rearrange("b s f -> s b f"), kin[8:128, :, :], reads=[b_kin], stream=2)
                C.dma("pool", wvs[l, b4 * 4:(b4 + 1) * 4, 0:120, :].rearrange("b s f -> s b f"), vin[8:128, :, :], reads=[b_vin], stream=2)

        if STAGE < 1:
            return
        C.mark("%s%d prenorm" % (kind, l))
        pre_norm(l, 0, N)
        C.mark("%s%d Fproj" % (kind, l))

        if os.environ.get("MK_SUB", "1") == "0":
            return
        def fblock(bi, ncols=512):
            return wload(A_winF, (l, bi), 8, ncols)

        wv, wb = fblock(0)
        SUB = os.environ.get("MK_SUB", "1")
        if SUB == "a":
            return
        pre = fmm4(wv, wb, N)
        for m in range(4):
            bk, bb = pre[m]
            unhold(bb)
            if SUB == "b":
                continue
            evac(zq[:, m, :N], bk[:, :N], [bb], [b_zq])
            if SUB == "c":
                continue
            if m < 2 and not smp:
                for t in range(nt):
                    V(lambda e: e.tensor_tensor(qd[:, m, t * 128:(t + 1) * 128], bk[:, t * 128:(t + 1) * 128],
                                                qdec[:, m, :], ALU.mult), [bb, b_const], [b_qd])
        if SUB in "abcd":
            return
        wv, wb = fblock(1)
        for m in range(4):
            bk, bb = bank()
            fmm(bk, bb, wv, wb, 8, slice(m * 128, (m + 1) * 128), hT, b_hT, N)
            evac(zq[:, 4 + m, :N], bk[:, :N], [bb], [b_zq])
        if SUB == "e":
            return
        wv, wb = fblock(2, 128)
        bk, bb = bank()
        fmm(bk, bb, wv, wb, 8, slice(0, 128), hT, b_hT, N)
        evac(akT[l][:, 128:128 + N], bk[:, :N], [bb], [b_akT[l]])

        if STAGE < 2:
            return
        C.mark("%s%d conv" % (kind, l))
        if smp:
            uv = uT[:, :, 0:160].rearrange("p c (b t) -> p c b t", b=16)
            C.dma("pool", xin[0:32, 0, 0:512], scv[l, :, :], writes=[b_xin[0]], stream=1)
            bk, bb = bank()
            for c in range(4):
                P(lambda e: e.transpose(bk[:, c * 32:(c + 1) * 32], xin[0:32, 0, c * 128:(c + 1) * 128], ident[0:32, 0:32]),
                  [b_xin[0], b_const], [bb])
            V(lambda e: e.tensor_copy(uv[:, :, :, 0:2], bk[:, 0:128].rearrange("p (c b j) -> p c b j", c=4, b=16)),
              [bb], [b_uT])
            ucur = lambda c: uv[:, c, :, 2:10]
            ush = lambda c, j: uv[:, c, :, j:j + 8]
            v3 = lambda ap: ap.rearrange("p (b t) -> p b t", b=16)
        else:
            V(lambda e: e.tensor_copy(uT[:, :, 0:2], uhalo[:, l, :, :]), [b_uhalo], [b_uT])
            ucur = lambda c: uT[:, c, 2:2 + N]
            ush = lambda c, j: uT[:, c, j:j + N]
            v3 = lambda ap: ap
        def conv_cc():
            wv, wb = fblock(3)
            for c in range(4):
                bk, bb = bank()
                fmm(bk, bb, wv, wb, 8, slice(c * 128, (c + 1) * 128), hT, b_hT, N)
                evac(ccT[:, c, :N], bk[:, :N], [bb], [b_ccT])

        def conv_ch():
            wv, wb = fblock(4)
            for c in range(4):
                bk, bb = bank()
                fmm(bk, bb, wv, wb, 8, slice(c * 128, (c + 1) * 128), hT, b_hT, N)
                V(lambda e: e.tensor_tensor(ucur(c), v3(ccT[:, c, :N]), v3(bk[:, :N]), ALU.mult), [bb, b_ccT], [b_uT])

        def conv_cb():
          wv, wb = fblock(5)
          for c in range(4):
              ycv = tmpA[:, c % 2, :N]
              V(lambda e: e.tensor_scalar(v3(ycv), ush(c, 0), cwT[:, l, c, 0:1], None, ALU.mult),
                [b_uT, b_const], [b_tmpA[c % 2]])
              for j in (1, 2):
                  V(lambda e: e.scalar_tensor_tensor(v3(ycv), ush(c, j), cwT[:, l, c, j:j + 1], v3(ycv), ALU.mult, ALU.add),
                    [b_uT, b_const, b_tmpA[c % 2]], [b_tmpA[c % 2]])
              bk, bb = bank()
              fmm(bk, bb, wv, wb, 8, slice(c * 128, (c + 1) * 128), hT, b_hT, N)
              V(lambda e: e.tensor_tensor(oT[:, 8 + c, :N], bk[:, :N], ycv, ALU.mult), [bb, b_tmpA[c % 2]], [b_oT[2]])
          if smp:
              bk, bb = bank()
              V(lambda e: e.tensor_copy(tmpA[:, 0, 0:128].rearrange("p (c b j) -> p c b j", c=4, b=16), uv[:, :, :, 8:10]),
                [b_uT], [b_tmpA[0]])
              for c in range(4):
                  P(lambda e: e.transpose(bk[0:32, c * 128:(c + 1) * 128], tmpA[:, 0, c * 32:(c + 1) * 32], ident[:, :]),
                    [b_tmpA[0], b_const], [bb])
              V(lambda e: e.tensor_copy(xin[0:32, 0, 512:1024], bk[0:32, 0:512]), [bb], [b_xin[1]])
              C.dma("pool", cvs[l, :, :], xin[0:32, 0, 512:1024], reads=[b_xin[1]], stream=2)
          else:
              V(lambda e: e.tensor_copy(uhalo[:, l, :, :], uT[:, :, N:N + 2]), [b_uT], [b_uhalo])
              if last_group:
                  bk, bb = bank()
                  V(lambda e: e.tensor_copy(tmpA[:, 0, 0:8].rearrange("p (c j) -> p c j", c=4), uT[:, :, N:N + 2]),
                    [b_uT], [b_tmpA[0]])
                  P(lambda e: e.transpose(bk[0:8, 0:128], tmpA[:, 0, 0:8], ident[:, :]), [b_tmpA[0], b_const], [bb])
                  V(lambda e: e.tensor_copy(xin[0:8, 0, 0:128], bk[0:8, 0:128]), [bb], [b_xin[1]])
                  for c in range(4):
                      C.dma("pool", cvp[l, :, c * 128:(c + 1) * 128], xin[2 * c:2 * c + 2, 0, 0:128], reads=[b_xin[1]], stream=2)

        if smp:
            conv_cc(); conv_ch(); conv_cb()

        if STAGE < 3:
            return
        C.mark("%s%d Tproj" % (kind, l))
        for bi in range(3):
            wv, wb = wload(A_winT, (l, bi), 8, 512)
            for t in range(nt):
                bk, bb = bank()
                for kc in range(8):
                    P(lambda e: e.matmul(bk[:, :], hT[:, kc, t * 128:(t + 1) * 128], wv[:, kc, :], start=(kc == 0), stop=(kc == 7)),
                      [wb, b_hT], [bb])
                if bi == 0:
                    if smp:
                        V(lambda e: e.tensor_tensor(rkt[:, t, :].rearrange("p (a i) -> p a i", a=2),
                                                    bk[:, 0:256].rearrange("p (a i) -> p a i", a=2), yT[:, 0:2, 384:512], ALU.mult),
                          [bb, b_const], [b_rkt])
                    else:
                        V(lambda e: e.tensor_tensor(rkt[:, t, :], bk[:, 0:256], kdec[:, :], ALU.mult), [bb, b_const], [b_rkt])
                    A(lambda e: e.copy(avt[l][:, 1 + t, :], bk[:, 384:512]), [bb], [b_avt[l]])
                    if smp or (last_group and t == nt - 1):
                        A(lambda e: e.copy(kvf[:, :], bk[:, 256:512]), [bb], [b_kvf])
                        if smp:
                            for b in range(16):
                                C.dma("pool", wks[l, b, 120:128, :], kvf[b * 8:(b + 1) * 8, 0:128], reads=[b_kvf], stream=2)
                                C.dma("pool", wvs[l, b, 120:128, :], kvf[b * 8:(b + 1) * 8, 128:256], reads=[b_kvf], stream=2)
                        else:
                            C.dma("pool", wkp[l, :, :], kvf[:, 0:128], reads=[b_kvf], stream=2)
                            C.dma("pool", wvp[l, :, :], kvf[:, 128:256], reads=[b_kvf], stream=2)
                elif bi == 1:
                    evac(rvt[:, t, :], bk[:, :], [bb], [b_rvt])
                else:
                    A(lambda e: e.activation(rgt[:, t, :], bk[:, :], AF.Silu), [bb], [b_rgt])

        if STAGE < 4:
            return
        C.mark("%s%d mixers" % (kind, l))
        if smp:
            wv, wb = fblock(12)
            for h in range(4):
                bk, bb = bank()
                fmm(bk, bb, wv, wb, 8, slice(h * 128, (h + 1) * 128), hT, b_hT, N)
                V(lambda e: e.tensor_tensor(qdp[:, h, :], bk[:, :N], xT[:, 4 + h, 384:512], ALU.mult), [bb, b_const], [b_qdp])

        NK = 256
        bias = xT[:, :, 128:384] if smp else biasP
        koff = 0
        nb = 2
        mx = st8[:, 1, :]; ng = st8[:, 2, :]; rs = st8[:, 3, :]; es = st8[:, 4, :]

        def tile_ret(t):
            tc = slice(t * 128, (t + 1) * 128)
            first_tile = (not smp) and first_group and t == 0
            bIs = [bank(), bank()]
            for h in range(4):
                hp, s = h // 2, h % 2
                pr = slice(s * 64, s * 64 + 64)
                bI, bbI = bIs[s]
                P(lambda e: e.matmul(bI[:, hp * 128:(hp + 1) * 128], zq[pr, 2 + hp, tc], zq[pr, hp, tc], start=True, stop=True),
                  [b_zq], [bbI])
            if smp:
                dmv = xT[:, 0:4, 384:512].rearrange("p (hp s) i -> p s hp i", s=2)
            else:
                dmv = dmT[:, :].rearrange("p (hp s i) -> p s hp i", hp=2, s=2)
            inv = innT[:, :, :].rearrange("p (hp s) i -> p s hp i", s=2)
            for s in range(2):
                bI, bbI = bIs[s]
                V(lambda e: e.tensor_tensor(inv[:, s], bI[:, 0:256].rearrange("p (hp i) -> p hp i", hp=2), dmv[:, s], ALU.mult),
                  [bbI, b_const], [b_innT])
            bO, bbO = bank(hold=True)
            if not smp:
                for h in range(4):
                    hp, s = h // 2, h % 2
                    pr = slice(s * 64, s * 64 + 64)
                    P(lambda e: e.matmul(bO[:, h * 128:(h + 1) * 128], innT[:, h, :], rvt[:, t, h * 128:(h + 1) * 128],
                                         start=True, stop=False), [b_innT, b_rvt], [bbO])
                    P(lambda e: e.matmul(bO[:, h * 128:(h + 1) * 128], qd[pr, hp, tc], Sb[l][pr, hp, :],
                                         start=False, stop=True), [b_qd, b_S[l]], [bbO])
                bS, bbS = bank()
                for h in range(4):
                    hp, s = h // 2, h % 2
                    pr = slice(s * 64, s * 64 + 64)
                    P(lambda e: e.matmul(bS[pr, hp * 128:(hp + 1) * 128], rkt[:, t, h * 64:(h + 1) * 64],
                                         rvt[:, t, h * 128:(h + 1) * 128], start=True, stop=True), [b_rkt, b_rvt], [bbS])
                for hp in range(2):
                    V(lambda e: e.scalar_tensor_tensor(S[l][:, hp, :], S[l][:, hp, :], cdec[:, hp:hp + 1],
                                                       bS[:, hp * 128:(hp + 1) * 128], ALU.mult, ALU.add),
                      [bbS, b_S[l], b_const], [b_S[l]])
                A(lambda e: e.copy(Sb[l][:, :, :], S[l][:, :, :]), [b_S[l]], [b_S[l]])
                if last_group and t == nt - 1:
                    for s in range(2):
                        C.dma("pool", retp[l].rearrange("(hp s) d e -> s d hp e", s=2)[s],
                              S[l][s * 64:(s + 1) * 64, :, :], reads=[b_S[l]], stream=2)
            else:
                def load_state(h):
                    for bh in range(2):
                        C.dma("pool", S0x[h % 2][bh * 64:(bh + 1) * 64, :, :],
                              sret[l, bh * 8:(bh + 1) * 8, h, :, :].rearrange("b d e -> d b e"), writes=[b_S0x[h % 2]], stream=1)

                load_state(0)
                for h in range(4):
                    S0, S0b, b_S0 = S0x[h % 2], S0bx[h % 2], b_S0x[h % 2]
                    if h + 1 < 4:
                        load_state(h + 1)
                    A(lambda e: e.copy(S0b[:, :, :], S0[:, :, :]), [b_S0], [b_S0])
                    V(lambda e: e.tensor_tensor(qdm, qdp[:, h:h + 1, :].broadcast_to([128, 8, 128]), hT[:, 0:8, 384:512], ALU.mult),
                      [b_qdp, b_const], [b_qdm])
                    P(lambda e: e.matmul(bO[:, h * 128:(h + 1) * 128], innT[:, h, :], rvt[:, t, h * 128:(h + 1) * 128],
                                         start=True, stop=False), [b_innT, b_rvt], [bbO])
                    for b in range(16):
                        bh, b2 = b // 8, b % 8
                        pr = slice(bh * 64, bh * 64 + 64)
                        P(lambda e: e.matmul(bO[:, h * 128:(h + 1) * 128], qdm[pr, b2, :], S0b[pr, b2, :],
                                             start=False, stop=(b == 15)), [b_qdm, b_S0], [bbO], inc=(b in (7, 15)))
                        if b == 7:
                            C.eng["pe"].wait_ge(C.sem["pe"], C.cnt["pe"])
                    vbd = pbf[:, :, :].rearrange("p h k -> p (h k)").rearrange("p (b e) -> p b e", b=16)
                    V(lambda e: e.tensor_tensor(vbd, rvt[:, t, h * 128:(h + 1) * 128].rearrange("p (o e) -> p o e", o=1).broadcast_to([128, 16, 128]),
                                                bmV[:, :].rearrange("p (b o) -> p b o", o=1).broadcast_to([128, 16, 128]), ALU.mult),
                      [b_rvt, b_const], [b_pbf])
                    bS1, bbS1 = bank()
                    bS2, bbS2 = bank()
                    assert bbS1 is not bbO and bbS2 is not bbO
                    for bh in range(2):
                        pr = slice(bh * 64, bh * 64 + 64)
                        for q4, (bSx, bbSx) in enumerate(((bS1, bbS1), (bS2, bbS2))):
                            P(lambda e: e.matmul(bSx[pr, :], rkt[:, t, h * 64:(h + 1) * 64],
                                                 vbd[:, bh * 8 + q4 * 4: bh * 8 + q4 * 4 + 4, :], start=True, stop=True),
                              [b_rkt, b_pbf], [bbSx])
                    c8 = float(np.float32(np.exp(np.float32(8.0) * np.log1p(-np.exp2(np.float32(-5.0 - h))))))
                    for q4, (bSx, bbSx) in enumerate(((bS1, bbS1), (bS2, bbS2))):
                        V(lambda e: e.scalar_tensor_tensor(S0[:, q4 * 4:(q4 + 1) * 4, :], S0[:, q4 * 4:(q4 + 1) * 4, :], c8,
                                                           bSx[:, :].rearrange("p (b e) -> p b e", b=4), ALU.mult, ALU.add),
                          [bbSx, b_S0], [b_S0])
                    for bh in range(2):
                        C.dma("pool", rets[l, bh * 8:(bh + 1) * 8, h, :, :].rearrange("b d e -> d b e"),
                              S0[bh * 64:(bh + 1) * 64, :, :], reads=[b_S0], stream=2)
            ssqc = st8[:, 0, 0:4]
            V(lambda e: e.memset(st8[:, 0, 0:4], 0.0), [], [b_st8])
            for h in range(4):
                A(lambda e: e.activation(ssb[:, 0, 0:128], bO[:, h * 128:(h + 1) * 128], AF.Square,
                                         accum_out=st8[:, 0, h:h + 1]), [bbO], [b_st8, b_ssb])
            A(lambda e: e.activation(st8[:, 0, 4:8], ssqc, AF.Ln, bias=epsb[:, 0:1], scale=1.0 / 128.0), [b_st8, b_const], [b_st8])
            A(lambda e: e.activation(st8[:, 0, 4:8], st8[:, 0, 4:8], AF.Exp, scale=-0.5), [b_st8], [b_st8])
            for h in range(4):
                V(lambda e: e.scalar_tensor_tensor(ortok[:, h * 128:(h + 1) * 128], bO[:, h * 128:(h + 1) * 128],
                                                   st8[:, 0, 4 + h:5 + h], rgt[:, t, h * 128:(h + 1) * 128], ALU.mult, ALU.mult),
                  [bbO, b_st8, b_rgt], [b_ortok])
            unhold(bbO)
            bT, bbT = bank()
            bTb = bT[:, :].bitcast(BF)
            for h in range(4):
                P(lambda e: e.transpose(bTb[:, h * 128:(h + 1) * 128], ortok[:, h * 128:(h + 1) * 128], identb[:, :]),
                  [b_ortok, b_const], [bbT])
            evac(oT[:, 0:4, tc], bTb[:, 0:512].rearrange("p (c n) -> p c n", c=4), [bbT], [b_oT[0]])


        def tile_att_scores(t):
            tc = slice(t * 128, (t + 1) * 128)
            first_tile = (not smp) and first_group and t == 0
            scb = []
            for half in range(2):
                b1, bb1 = bank(hold=True); b2, bb2 = bank(hold=True)
                scb.append(((b1, bb1), (b2, bb2)))
            for h in [0, 4, 1, 5, 2, 6, 3, 7]:
                c, s = h % 4, h // 4
                pr = slice(s * 64, s * 64 + 64)
                bkk, bbk = scb[h // 4][(h % 4) // 2]
                oc = (h % 2) * 256
                if smp:
                    if s == 0:
                        V(lambda e: e.tensor_tensor(aqm4, zq[:, 4 + c:5 + c, 0:128].rearrange("p (a b) i -> p a b i", a=1).broadcast_to([128, 8, 2, 128]),
                                                    hT[:, 0:8, 128:384].rearrange("p c (two i) -> p c two i", two=2), ALU.mult),
                          [b_zq, b_const], [b_aqm])
                    for b in range(16):
                        P(lambda e: e.matmul(bkk[:, oc:oc + 128], aqm_b(b)[pr, :], kcTb(b)[pr, :], start=(b == 0), stop=(b == 15)),
                          [b_aqm, b_kcT], [bbk], inc=(b == 15))
                    P(lambda e: e.matmul(bkk[:, oc + 128:oc + 256], zq[pr, 4 + c, tc], akT[l][pr, 128:256], start=True, stop=True),
                      [b_zq, b_akT[l]], [bbk])
                else:
                    P(lambda e: e.matmul(bkk[:, oc:oc + NK], zq[pr, 4 + c, tc], akT[l][pr, (t + 2) * 128 - NK:(t + 2) * 128],
                                         start=True, stop=True), [b_zq, b_akT[l]], [bbk])
            for h2 in range(4):
                bkk, bbk = scb[h2 // 2][h2 % 2]
                V(lambda e: e.scalar_tensor_tensor(ssb[:, 2 * h2:2 * h2 + 2, 0:NK],
                                                   bkk[:, :].rearrange("p (h k) -> p h k", h=2)[:, :, 0:NK], 0.125,
                                                   bias[:, 2 * h2:2 * h2 + 2, koff:256], ALU.mult, ALU.add),
                  [bbk, b_const], [b_ssb])
            for half in range(2):
                unhold(scb[half][0][1]); unhold(scb[half][1][1])
            if first_tile:
                V(lambda e: e.tensor_scalar(ssb[:, :, 0:128], ssb[:, :, 0:128], negf[:, 0:1], None, ALU.add),
                  [b_ssb, b_const], [b_ssb])
            V(lambda e: e.tensor_reduce(mx, ssb[:, :, 0:NK], AX.X, ALU.max), [b_ssb], [b_st8])
            V(lambda e: e.tensor_tensor(mx, mx, sinkR[:, l, :], ALU.max), [b_st8, b_const], [b_st8])
            V(lambda e: e.tensor_scalar(ng, mx, -1.0, None, ALU.mult), [b_st8], [b_st8])
            V(lambda e: e.memset(rs, 0.0), [], [b_st8])

        def tile_att_exp(t):
            for h in range(8):
                A(lambda e: e.activation(pbf[:, h, 0:NK], ssb[:, h, 0:NK], AF.Exp, bias=ng[:, h:h + 1], scale=1.0,
                                         accum_out=rs[:, h:h + 1]), [b_ssb, b_st8], [b_pbf, b_st8])

        def tile_att_norm(t):
            V(lambda e: e.tensor_tensor(es, sinkR[:, l, :], ng, ALU.add), [b_st8, b_const], [b_st8])
            A(lambda e: e.activation(es, es, AF.Exp), [b_st8], [b_st8])
            V(lambda e: e.tensor_tensor(rs, rs, es, ALU.add), [b_st8], [b_st8])
            V(lambda e: e.reciprocal(rs, rs), [b_st8], [b_st8])
            V(lambda e: e.tensor_tensor(pbf[:, :, 0:NK], pbf[:, :, 0:NK],
                                        rs.rearrange("p (h o) -> p h o", o=1).broadcast_to([128, 8, NK]), ALU.mult),
              [b_pbf, b_st8], [b_pbf])

        def tile_att_out(t):
            tc = slice(t * 128, (t + 1) * 128)
            first_tile = (not smp) and first_group and t == 0
            for half in range(2):
                bT, bbT = bank()
                bTb = bT[:, :].bitcast(BF)
                for hh in range(4):
                    h = half * 4 + hh
                    for blk in range(nb):
                        P(lambda e: e.transpose(bTb[:, (hh * 2 + blk) * 128:(hh * 2 + blk + 1) * 128],
                                                pbf[:, h, blk * 128:(blk + 1) * 128], identb[:, :]), [b_pbf, b_const], [bbT])
                evac(pTs[:, half * 4:(half + 1) * 4, 0:nb, :],
                     bTb[:, :].rearrange("p (h b q) -> p h b q", h=4, b=2)[:, :, 0:nb, :], [bbT], [b_pTs])
            bV, bbV = bank()
            for h in range(8):
                kv = h // 4
                po = slice((h % 2) * 64, (h % 2) * 64 + 64)
                oc = (h // 2) * 128
                if smp:
                    P(lambda e: e.matmul(bV[po, oc:oc + 128], avt[l][:, 1 + t, kv * 64:(kv + 1) * 64], pTs[:, h, 1, :],
                                         start=True, stop=False), [b_avt[l], b_pTs], [bbV], inc=False)
                    for b in range(16):
                        P(lambda e: e.matmul(bV[po, oc + b * 8:oc + (b + 1) * 8], vcb(b)[:, kv * 64:(kv + 1) * 64],
                                             pTs[:, h, 0, b * 8:(b + 1) * 8], start=False, stop=(b == 15)),
                          [b_vc, b_pTs], [bbV], inc=(b == 15))
                else:
                    for blk in range(nb):
                        slot = t + blk if nb == 2 else t + 1
                        P(lambda e: e.matmul(bV[po, oc:oc + 128], avt[l][:, slot, kv * 64:(kv + 1) * 64], pTs[:, h, blk, :],
                                             start=(blk == 0), stop=(blk == nb - 1)), [b_avt[l], b_pTs], [bbV])
            evac(oT[:, 4:8, tc], bV[:, :].rearrange("p (c n) -> p c n", c=4), [bbV], [b_oT[1]])

        ntl = nt if STAGE >= 5 else 0
        if smp:
            for t in range(ntl):
                tile_ret(t); tile_att_scores(t); tile_att_exp(t); tile_att_norm(t); tile_att_out(t)
        else:
            convq = [conv_cc, conv_ch, conv_cb]
            for t in range(ntl):
                tile_att_scores(t)
                if t > 0:
                    tile_att_out(t - 1)
                tile_att_exp(t)
                tile_ret(t)
                tile_att_norm(t)
                if convq:
                    convq.pop(0)()
            if ntl:
                tile_att_out(ntl - 1)
            while convq:
                convq.pop(0)()
        if not smp:
            A(lambda e: e.copy(akT[l][:, 0:128], akT[l][:, N:N + 128]), [b_akT[l]], [b_akT[l]])
            A(lambda e: e.copy(avt[l][:, 0, :], avt[l][:, nt, :]), [b_avt[l]], [b_avt[l]])

        if DBG and smp and l == 0:
            for c in range(12):
                V(lambda e: e.tensor_copy(tmpA[:, 0, 0:128], oT[:, c, 0:128]), [b_oT[c // 4]], [b_tmpA[0]])
                C.dma("pool", dbg[:, c, :], tmpA[:, 0, 0:128], reads=[b_tmpA[0]], stream=2)
        if STAGE < 6:
            return
        C.mark("%s%d gating" % (kind, l))
        for n in range(3):
            wbv, wbb = wload(A_wbr, (l, n), 4, 1024)
            for half in range(2):
                wv, wb = fblock(6 + n * 2 + half)
                for m in range(4):
                    dc = half * 4 + m
                    bA, bbA = bank()
                    fmm(bA, bbA, wv, wb, 8, slice(m * 128, (m + 1) * 128), hT, b_hT, N)
                    bB, bbB = bank()
                    fmm(bB, bbB, wbv, wbb, 4, slice(dc * 128, (dc + 1) * 128), oT[:, n * 4:(n + 1) * 4, :], b_oT[n], N)
                    sg = tmpA[:, dc % 2, :N]
                    A(lambda e: e.activation(sg, bA[:, :N], AF.Sigmoid), [bbA], [b_tmpA[dc % 2]])
                    if n == 0:
                        V(lambda e: e.tensor_tensor(yT[:, dc, :N], sg, bB[:, :N], ALU.mult), [b_tmpA[dc % 2], bbB], [b_yT[dc]])
                    else:
                        V(lambda e: e.tensor_tensor(sg, sg, bB[:, :N], ALU.mult), [b_tmpA[dc % 2], bbB], [b_tmpA[dc % 2]])
                        if n == 1:
                            V(lambda e: e.tensor_tensor(yT[:, dc, :N], yT[:, dc, :N], sg, ALU.add), [b_tmpA[dc % 2], b_yT[dc]], [b_yT[dc]])
                        else:
                            V(lambda e: e.tensor_tensor(mixT[:, dc, :N], yT[:, dc, :N], sg, ALU.add), [b_tmpA[dc % 2], b_yT[dc]], [b_mixT])
        C.mark("%s%d wout+norm" % (kind, l))
        st = stat_begin()
        for half in range(2):
            wv, wb = wload(A_wout, (l, half), 8, 512)
            for m in range(4):
                bk, bb = bank()
                fmm(bk, bb, wv, wb, 8, slice(m * 128, (m + 1) * 128), mixT, b_mixT, N)
                evac(yT[:, half * 4 + m, :N], bk[:, :N], [bb], [b_yT[half * 4 + m]])
                if half * 4 + m > 0:
                    stat_add(st, yT, b_yT, half * 4 + m - 1, N)
        stat_add(st, yT, b_yT, 7, N)
        st_x = post_norm_add(l, 1, N, st)

        if STAGE < 7:
            return
        C.mark("%s%d ffn" % (kind, l))
        C.barrier()
        pre_norm(l, 2, N, st_x)
        if prefetch is not None:
            prefetch()
        for blk in range(8):
            wv, wb = wload(A_wff1, (l, blk), 8, 512)
            pre = fmm4(wv, wb, N) if blk == 0 else None
            for m in range(4):
                if pre is not None:
                    bk, bb = pre[m]
                    unhold(bb)
                else:
                    bk, bb = bank()
                    fmm(bk, bb, wv, wb, 8, slice(m * 128, (m + 1) * 128), hT, b_hT, N)
                r = tmpA[:, m % 2, :N]
                A(lambda e: e.activation(r, bk[:, :N], AF.Relu), [bb], [b_tmpA[m % 2]])
                V(lambda e: e.tensor_tensor(hidT[:, blk * 4 + m, :N], r, r, ALU.mult), [b_tmpA[m % 2]], [b_hid])
        for half in range(2):
            acc = [bank(hold=True) for _ in range(4)]
            for kg in range(4):
                wv, wb = wload(A_wff2, (l, half * 4 + kg), 8, 512)
                for kc in range(8):
                    for m in range(4):
                        bk, bb = acc[m]
                        P(lambda e: e.matmul(bk[:, :N], wv[:, kc, m * 128:(m + 1) * 128], hidT[:, kg * 8 + kc, :N],
                                             start=(kg == 0 and kc == 0), stop=(kg == 3 and kc == 7)), [wb, b_hid], [bb])
            for m in range(4):
                bk, bb = acc[m]
                evac(yT[:, half * 4 + m, :N], bk[:, :N], [bb], [b_yT[half * 4 + m]])
                unhold(bb)
        post_norm_add_ffn = None
        st = stat_begin()
        for c in range(8):
            stat_add(st, yT, b_yT, c, N)
        st_x = post_norm_add(l, 3, N, st)

        if STAGE < 8:
            return
        C.mark("%s%d ple" % (kind, l))
        pre_norm(l, 4, N, st_x)
        wpv, wpb = wload(A_wpp, (l,), 2, 1024)
        for half in range(2):
            wv, wb = wload(A_wpg, (l, half), 8, 512)
            pre = fmm4(wv, wb, N) if half == 0 else None
            for m in range(4):
                dc = half * 4 + m
                if pre is not None:
                    bA, bbA = pre[m]
                    unhold(bbA)
                else:
                    bA, bbA = bank()
                    fmm(bA, bbA, wv, wb, 8, slice(m * 128, (m + 1) * 128), hT, b_hT, N)
                bB, bbB = bank()
                fmm(bB, bbB, wpv, wpb, 2, slice(dc * 128, (dc + 1) * 128), pT, b_pT, N)
                sg = tmpA[:, dc % 2, :N]
                A(lambda e: e.activation(sg, bA[:, :N], AF.Sigmoid), [bbA], [b_tmpA[dc % 2]])
                V(lambda e: e.tensor_tensor(sg, sg, bB[:, :N], ALU.mult), [b_tmpA[dc % 2], bbB], [b_tmpA[dc % 2]])
                V(lambda e: e.tensor_tensor(yT[:, dc, :N], xT[:, dc, :N], sg, ALU.add), [b_tmpA[dc % 2], b_xT[dc]], [b_yT[dc]])

    def load_x(src, nt):
        for t in range(nt):
            C.dma("pool", xin[:, 0, :], src[t * 128:(t + 1) * 128, :], writes=[b_xin[t % 2]], stream=1)
            for hf in range(2):
                bk, bb = bank()
                for j in range(4):
                    c = hf * 4 + j
                    P(lambda e: e.transpose(bk[:, j * 128:(j + 1) * 128], xin[:, 0, c * 128:(c + 1) * 128], ident[:, :]),
                      [b_xin[t % 2], b_const], [bb])
                evac(xT[:, hf * 4:(hf + 1) * 4, t * 128:(t + 1) * 128], bk[:, :].rearrange("p (c n) -> p c n", c=4), [bb], [b_xT[hf * 4:(hf + 1) * 4]])

    def load_p(src, nt, pslot=0):
        pT = pT2[:, pslot]; b_pT = b_pT2[pslot]
        for t in range(nt):
            C.dma("pool", pin[:, 0, :], src[t * 128:(t + 1) * 128, :], writes=[b_pin[t % 2]], stream=1)
            bk, bb = bank()
            for j in range(2):
                P(lambda e: e.transpose(bk[:, j * 128:(j + 1) * 128], pin[:, 0, j * 128:(j + 1) * 128], ident[:, :]),
                  [b_pin[t % 2], b_const], [bb])
            evac(pT[:, :, t * 128:(t + 1) * 128], bk[:, 0:256].rearrange("p (c n) -> p c n", c=2), [bb], [b_pT])

    def store_x(dst, nt, xT=None, b_xT=None):
        xT = yT if xT is None else xT
        b_xT = b_yT if b_xT is None else b_xT
        for t in range(nt):
            for hf in range(2):
                bk, bb = bank()
                for j in range(4):
                    c = hf * 4 + j
                    P(lambda e: e.transpose(bk[:, j * 128:(j + 1) * 128], xT[:, c, t * 128:(t + 1) * 128], ident[:, :]),
                      [b_xT[c], b_const], [bb])
                evac(xin[:, 0, hf * 512:(hf + 1) * 512], bk[:, :], [bb], [b_xin[t % 2]])
            C.dma("pool", dst[t * 128:(t + 1) * 128, :], xin[:, 0, :], reads=[b_xin[t % 2]], stream=2)

    ngp = NTP // 4

    def phase1(l, g, last, next_load=None):
        C.mark("phase1 %d" % l)
        N, nt = 512, 4
        rkt = arv(11264, nt * 256 * 2, BF, "p (t n) -> p t n", t=nt)
        rvt = arv(13312, nt * 512 * 2, BF, "p (t n) -> p t n", t=nt)
        pkg = arv(20480, PK * 4, F32)
        b_rkt = Buf(); b_rvt = Buf()
        C.barrier()
        pre_norm(l, 0, N)
        if next_load is not None:
            next_load()
        for bi in range(2):
            wv, wb = wload(A_winT, (l, bi), 8, 512)
            for t in range(nt):
                bk, bb = bank()
                for kc in range(8):
                    P(lambda e: e.matmul(bk[:, :], hT[:, kc, t * 128:(t + 1) * 128], wv[:, kc, :], start=(kc == 0), stop=(kc == 7)),
                      [wb, b_hT], [bb])
                if bi == 0:
                    V(lambda e: e.tensor_tensor(rkt[:, t, :], bk[:, 0:256], kdec[:, :], ALU.mult), [bb, b_const], [b_rkt])
                    if last and t == nt - 1:
                        A(lambda e: e.copy(pkg[:, 384:512], bk[:, 384:512]), [bb], [b_pkg])
                else:
                    evac(rvt[:, t, :], bk[:, :], [bb], [b_rvt])
        for t in range(nt):
            bS, bbS = bank()
            for h in range(4):
                hp, s_ = h // 2, h % 2
                pr = slice(s_ * 64, s_ * 64 + 64)
                P(lambda e: e.matmul(bS[pr, hp * 128:(hp + 1) * 128], rkt[:, t, h * 64:(h + 1) * 64],
                                     rvt[:, t, h * 128:(h + 1) * 128], start=True, stop=True), [b_rkt, b_rvt], [bbS])
            for hp in range(2):
                V(lambda e: e.scalar_tensor_tensor(S[l][:, hp, :], S[l][:, hp, :], cdec[:, hp:hp + 1],
                                                   bS[:, hp * 128:(hp + 1) * 128], ALU.mult, ALU.add),
                  [bbS, b_S[l], b_const], [b_S[l]])
        if last:
            tcl = slice(384, 512)
            wv, wb = wload(A_winF, (l, 2), 8, 128)
            bk, bb = bank()
            for kc in range(8):
                P(lambda e: e.matmul(bk[:, 0:128], wv[:, kc, 0:128], hT[:, kc, tcl], start=(kc == 0), stop=(kc == 7)), [wb, b_hT], [bb])
            A(lambda e: e.copy(pkg[:, 256:384], bk[:, 0:128]), [bb], [b_pkg])
            cc2 = tmpA[:, 0, 0:8]
            wv, wb = wload(A_winF, (l, 3), 8, 512)
            for c in range(4):
                bk, bb = bank()
                for kc in range(8):
                    P(lambda e: e.matmul(bk[:, 0:128], wv[:, kc, c * 128:(c + 1) * 128], hT[:, kc, tcl], start=(kc == 0), stop=(kc == 7)),
                      [wb, b_hT], [bb])
                A(lambda e: e.copy(cc2[:, 2 * c:2 * c + 2], bk[:, 126:128]), [bb], [b_tmpA[0]])
            wv, wb = wload(A_winF, (l, 4), 8, 512)
            for c in range(4):
                bk, bb = bank()
                for kc in range(8):
                    P(lambda e: e.matmul(bk[:, 0:128], wv[:, kc, c * 128:(c + 1) * 128], hT[:, kc, tcl], start=(kc == 0), stop=(kc == 7)),
                      [wb, b_hT], [bb])
                V(lambda e: e.tensor_tensor(pkg[:, 512 + 2 * c:514 + 2 * c], cc2[:, 2 * c:2 * c + 2], bk[:, 126:128], ALU.mult),
                  [bb, b_tmpA[0]], [b_pkg])
            V(lambda e: e.tensor_copy(pkg[:, 0:256], S[l][:, :, :].rearrange("p a e -> p (a e)")), [b_S[l]], [b_pkg])

    def exchange(l):
        C.mark("exchange %d" % l)
        pkg = arv(20480, PK * 4, F32)
        C.dma("pool", pkg_in[l], pkg, reads=[b_pkg], writes=[b_pki[l]], own=b_pkg)
        C._wait("pool", C._deps("pool", [b_pki[l]], [b_pko[l]]))
        ins = nc.gpsimd.collective_compute("AllGather", ALU.bypass,
                                           replica_groups=[[2 * i, 2 * i + 1] for i in range(NCORES // 2)],
                                           ins=[pkg_in[l].opt()], outs=[pkg_out[l].opt()])
        k = "cc%d" % l
        C.sem[k] = C.es.enter_context(nc.semaphore("s_" + k)); C.cnt[k] = 1
        ins.then_inc(C.sem[k])
        b_pki[l].r[k] = 1
        b_pko[l].w = (k, 1); b_pko[l].r = {}
        C.barrier()
        b_g = Buf()
        gbuf = arv(0, 4 * PK * 4, F32, "p (r n) -> p r n", r=4)
        for r0 in range(0, GS, 4):
            nr = min(4, GS - r0)
            C.dma("pool", gbuf[:, 0:nr, :], pkg_out[l][r0 * 128:(r0 + nr) * 128, :].rearrange("(r p) n -> p r n", p=128),
                  reads=[b_pko[l]], writes=[b_g])
            for r in range(nr):
                if r0 + r == 0:
                    V(lambda e: e.tensor_scalar(pkg, gbuf[:, r, :], sel8[:, 0:1], None, ALU.mult), [b_g, b_const], [b_pkg])
                else:
                    V(lambda e: e.scalar_tensor_tensor(pkg, gbuf[:, r, :], sel8[:, r0 + r:r0 + r + 1], pkg, ALU.mult, ALU.add),
                      [b_g, b_const, b_pkg], [b_pkg])
        V(lambda e: e.tensor_copy(S[l][:, :, :].rearrange("p a e -> p (a e)"), pkg[:, 0:256]), [b_pkg], [b_S[l]])
        A(lambda e: e.copy(Sb[l][:, :, :].rearrange("p a e -> p (a e)"), pkg[:, 0:256]), [b_pkg], [b_S[l]])
        A(lambda e: e.copy(akT[l][:, 0:128], pkg[:, 256:384]), [b_pkg], [b_akT[l]])
        A(lambda e: e.copy(avt[l][:, 0, :], pkg[:, 384:512]), [b_pkg], [b_avt[l]])
        V(lambda e: e.tensor_copy(uhalo[:, l, :, :].rearrange("p c j -> p (c j)"), pkg[:, 512:520]), [b_pkg], [b_uhalo])
        C.barrier()

    b_pkg = Buf("pkg"); b_pki = [Buf(), Buf()]; b_pko = [Buf(), Buf()]

    def xT_load(src, bsrc):
        C.dma("act", xT[:, :, :], src, reads=[bsrc], writes=[b_xT], own=b_xT[0])

    def xT_save(dst, bdst, N=512, src=None, bsrc=None):
        src = xT if src is None else src
        bsrc = b_xT if bsrc is None else bsrc
        C.dma("pool", dst[:, :, 0:N], src[:, :, 0:N], reads=[bsrc], writes=[bdst], own=bsrc[0])

    passes = []
    for l in range(DEPTH):
        for g in range(ngp):
            passes.append((l, "p", g))
        if SAMPLE:
            passes.append((l, "s", 0))
    p_loaded = set()
    p_staged = set()

    def stage_p(i):
        if i >= len(passes) or i in p_loaded:
            return
        l_, k_, g_ = passes[i]
        if k_ != "p":
            return
        C.dma("pool", xin[:, 0, :].rearrange("p (t f) -> p t f", t=4),
              pp[l_, g_ * 512:(g_ + 1) * 512, :].rearrange("(t p) f -> p t f", p=128), writes=[_bx], stream=1)
        p_staged.add(i)

    def finish_p(i):
        if i not in p_staged:
            return
        pT = pT2[:, i % 2]; b_pT = b_pT2[i % 2]
        for t in range(4):
            bk, bb = bank()
            for j in range(2):
                P(lambda e: e.transpose(bk[:, j * 128:(j + 1) * 128], xin[:, 0, t * 256 + j * 128:t * 256 + (j + 1) * 128], ident[:, :]),
                  [_bx, b_const], [bb])
            evac(pT[:, :, t * 128:(t + 1) * 128], bk[:, 0:256].rearrange("p (c n) -> p c n", c=2), [bb], [b_pT])
        p_loaded.add(i)

    def ensure_p(i):
        if i >= len(passes) or i in p_loaded:
            return
        l_, k_, g_ = passes[i]
        if k_ == "p":
            load_p(pp[l_, g_ * 512:(g_ + 1) * 512, :], 4, i % 2)
        else:
            load_p(ps_[l_], 1, i % 2)
        p_loaded.add(i)

    pi = 0
    for l in range(DEPTH):
        for g in range(ngp):
            if l == 0:
                load_x(xp[g * 512:(g + 1) * 512, :], 4)
                xT_save(xs0[g], b_xs0[g])
                nl = None
            else:
                if g == 0:
                    xT_load(xs1[g], b_xs1[g])
                nl = (lambda gg=g + 1: xT_load(xs1[gg], b_xs1[gg])) if g + 1 < ngp else None
            phase1(l, g, last=(g == ngp - 1), next_load=nl)
        exchange(l)
        for g in range(ngp):
            C.mark("io %d" % l)
            if l == 0:
                xT_load(xs0[g], b_xs0[g])
            else:
                xT_load(xs1[g], b_xs1[g])
            ensure_p(pi)
            if g < ngp - 1:
                stage_p(pi + 1)
            process_layer(l, "p", 512, first_group=(g == 0), last_group=(g == ngp - 1), pslot=pi % 2,
                          prefetch=(lambda i=pi + 1: finish_p(i)))
            pi += 1
            C.mark("io %d" % l)
            if l == 0:
                xT_save(xs1[g], b_xs1[g], src=yT, bsrc=b_yT)
            else:
                store_x(yp[g * 512:(g + 1) * 512, :], 4)
        if SAMPLE:
            if l == 0:
                load_x(xs, 1)
            else:
                C.dma("pool", xT[:, :, 0:128], xs1[ngp][:, :, 0:128], reads=[b_xs1[ngp]], writes=[b_xT], own=b_xT[0])
            ensure_p(pi)
            process_layer(l, "s", 128, first_group=False, last_group=False, pslot=pi % 2)
            pi += 1
            if l == 0:
                xT_save(xs1[ngp], b_xs1[ngp], 128, src=yT, bsrc=b_yT)
            else:
                store_x(ys, 1)
    C.mark("end")
    C.final_wait("sp")
    C.close()
    if os.environ.get("MK_MARKS"):
        import json
        json.dump(C.marks, open(os.environ["MK_MARKS"], "w"))
    return nc, C


def host_consts():
    f = np.float32
    hh = np.arange(4, dtype=f)
    lg = np.log1p(-np.exp2(-5.0 - hh)).astype(f)
    i = np.arange(128, dtype=f)
    c = {}
    c["c_ident"] = np.eye(128, dtype=f)
    c["c_ones"] = np.full((128, 128), 1.0 / 1024.0, f)
    diff = i[None, :] - i[:, None]
    dm = np.where(diff[None] >= 0, np.exp(lg[:, None, None] * np.maximum(diff[None], 0.0)), 0.0).astype(f) * f(0.125)
    c["c_dmT"] = np.ascontiguousarray(dm.transpose(1, 0, 2)).reshape(128, 512)
    seq = (np.arange(128) // 8)
    tt = (np.arange(128) % 8).astype(f)
    same = (seq[:, None] == seq[None, :])
    dts = tt[None, :] - tt[:, None]
    dms = np.where(same[None] & (dts[None] >= 0), np.exp(lg[:, None, None] * np.maximum(dts[None], 0.0)), 0.0).astype(f) * f(0.125)
    c["c_dmTs"] = np.ascontiguousarray(dms.transpose(1, 0, 2)).reshape(128, 512)
    qd = np.exp(lg[:, None] * (i[None, :] + 1.0)).astype(f)
    q2 = np.zeros((128, 2, 128), f)
    for hp in range(2):
        for s in range(2):
            q2[s * 64:(s + 1) * 64, hp, :] = qd[2 * hp + s][None, :]
    c["c_qdec"] = q2.reshape(128, 256)
    qds = np.exp(lg[:, None] * (tt[None, :] + 1.0)).astype(f)
    c["c_qdecs"] = np.ascontiguousarray(np.broadcast_to(qds[None], (128, 4, 128))).reshape(128, 512)
    kd = np.exp(lg[:, None] * (127.0 - i[None, :])).astype(f) * f(0.125)
    c["c_kdec"] = np.ascontiguousarray(np.repeat(kd.T[:, :, None], 64, axis=2)).reshape(128, 256)
    kds = np.exp(lg[:, None] * (7.0 - tt[None, :])).astype(f) * f(0.125)
    c["c_kdecs"] = np.ascontiguousarray(np.repeat(kds.T[:, :, None], 64, axis=2)).reshape(128, 256)
    cd = np.exp(lg * f(128.0)).astype(f)
    c2 = np.zeros((128, 2), f)
    for hp in range(2):
        for s in range(2):
            c2[s * 64:(s + 1) * 64, hp] = cd[2 * hp + s]
    c["c_cdec"] = c2
    slopes = np.exp2(-8.0 * (np.arange(8, dtype=f) + 1.0) / 8.0).astype(f)
    q = np.arange(128)[:, None]; kk = np.arange(256)[None, :]
    dist = (128 + q - kk)
    allowed = (dist >= 0) & (dist < 128)
    bp = np.where(allowed[:, None, :], -slopes[None, :, None] * dist[:, None, :].astype(f), f(NEG)).astype(f)
    c["c_biasP"] = np.ascontiguousarray(bp).reshape(128, 2048)
    ti = (np.arange(128) % 8)[:, None]; si = (np.arange(128) // 8)[:, None]
    kc_ = np.arange(128)[None, :]
    dist_c = 128 + ti - kc_
    al_c = dist_c < 128
    tj = (np.arange(128) % 8)[None, :]; sj = (np.arange(128) // 8)[None, :]
    dist_n = ti - tj
    al_n = (si == sj) & (dist_n >= 0)
    dist_s = np.concatenate([dist_c, dist_n], axis=1)
    al_s = np.concatenate([al_c, al_n], axis=1)
    bs = np.where(al_s[:, None, :], -slopes[None, :, None] * dist_s[:, None, :].astype(f), f(NEG)).astype(f)
    c["c_biasS"] = np.ascontiguousarray(bs).reshape(128, 2048)
    bmq = (np.arange(16)[:, None] == seq[None, :]).astype(f)
    c["c_bmQ"] = np.ascontiguousarray(np.broadcast_to(bmq[None], (128, 16, 128))).reshape(128, 2048)
    bq2 = np.zeros((128, 8, 128), f)
    for bh in range(2):
        bq2[bh * 64:(bh + 1) * 64] = (np.arange(8)[:, None] + bh * 8 == seq[None, :]).astype(f)[None]
    c["c_bmQ2"] = bq2.reshape(128, 1024)
    c["c_bmV"] = (seq[:, None] == np.arange(16)[None, :]).astype(f)
    return c


def blk(w, cols=None):
    if cols is not None:
        w = w[:, cols]
    K, n = w.shape
    return np.ascontiguousarray(w.reshape(K // 128, 128, n).transpose(1, 0, 2))


def host_weights(w_in, w_branch, w_out, w_ff1, w_ff2, w_ple_gate, w_ple_proj):
    r = lambda a, n: np.arange(a, a + n)
    aqperm = np.concatenate([np.concatenate([r(OFF["aq"] + c * 64, 64), r(OFF["aq"] + (c + 4) * 64, 64)]) for c in range(4)])
    rqdup = np.concatenate([np.concatenate([r(OFF["rq"] + h * 64, 64), r(OFF["rq"] + h * 64, 64)]) for h in range(4)])
    fcols = [np.concatenate([r(OFF["rq"], 256), r(OFF["rk"], 256)]), aqperm, np.tile(r(OFF["ak"], 128), 4),
             r(OFF["cc"], 512), r(OFF["ch"], 512), r(OFF["cb"], 512)]
    for n in range(3):
        for half in range(2):
            fcols.append(r(OFF["gt"] + n * 1024 + half * 512, 512))
    fcols.append(rqdup)
    tcols = [np.concatenate([r(OFF["rk"], 256), r(OFF["ak"], 128), r(OFF["av"], 128)]), r(OFF["rv"], 512), r(OFF["rg"], 512)]
    out = {}
    out["winF"] = np.stack([np.stack([blk(w_in[l], cc) for cc in fcols]) for l in range(DEPTH)])
    out["winT"] = np.stack([np.stack([blk(w_in[l], cc) for cc in tcols]) for l in range(DEPTH)])
    out["wbr"] = np.stack([np.stack([blk(w_branch[l, n]) for n in range(3)]) for l in range(DEPTH)])
    out["wout"] = np.stack([np.stack([blk(w_out[l][:, h * 512:(h + 1) * 512]) for h in range(2)]) for l in range(DEPTH)])
    out["wff1"] = np.stack([np.stack([blk(w_ff1[l][:, b * 512:(b + 1) * 512]) for b in range(8)]) for l in range(DEPTH)])
    out["wff2"] = np.stack([np.stack([blk(w_ff2[l][kg * 1024:(kg + 1) * 1024, h * 512:(h + 1) * 512])
                                      for h in range(2) for kg in range(4)]) for l in range(DEPTH)])
    out["wpg"] = np.stack([np.stack([blk(w_ple_gate[l][:, h * 512:(h + 1) * 512]) for h in range(2)]) for l in range(DEPTH)])
    out["wpp"] = np.stack([blk(w_ple_proj[l]) for l in range(DEPTH)])
    return out


_CACHE = {}


def run(inputs, NTP, n_cores, seq_of_core, samp_of_core, SAMPLE=True):
    key = (NTP, SAMPLE, n_cores)
    if key not in _CACHE:
        _CACHE[key] = build(NTP, SAMPLE, n_cores)
    nc, C = _CACHE[key]
    f = np.float32
    g = lambda k: np.asarray(inputs[k], dtype=f)
    shared = host_consts()
    shared.update(host_weights(g("w_in"), g("w_branch"), g("w_out"), g("w_ff1"), g("w_ff2"), g("w_ple_gate"), g("w_ple_proj")))
    gs = np.stack([g("g_mix_pre"), g("g_mix_post"), g("g_ffn_pre"), g("g_ffn_post"), g("g_ple")], axis=1)
    shared["gT"] = np.ascontiguousarray(gs.reshape(DEPTH, 5, 8, 128).transpose(3, 0, 1, 2))
    shared["cwT"] = np.ascontiguousarray(g("conv_w").reshape(DEPTH, 3, 4, 128).transpose(3, 0, 2, 1))
    shared["sinkR"] = np.ascontiguousarray(np.broadcast_to(g("attn_sinks")[None], (128, DEPTH, 8)))
    TP = NTP * 128
    xpr, ppr, xsm, psm = g("x_prompt"), g("p_prompt"), g("x_sample"), g("p_sample")
    sr, ckk, cvv, scc = g("state_ret"), g("cache_win_k"), g("cache_win_v"), g("state_conv")
    in_maps = []
    for c in range(n_cores):
        sq_, t0 = seq_of_core[c]
        b0 = samp_of_core[c]
        m = dict(shared)
        m["xp"] = np.ascontiguousarray(xpr[sq_, t0:t0 + TP])
        m["pp"] = np.ascontiguousarray(ppr[:, sq_, t0:t0 + TP])
        m["xs"] = np.ascontiguousarray(xsm[b0:b0 + 16].reshape(128, D))
        m["ps"] = np.ascontiguousarray(psm[:, b0:b0 + 16].reshape(DEPTH, 128, 256))
        m["sret"] = np.ascontiguousarray(sr[:, b0:b0 + 16])
        m["ck"] = np.ascontiguousarray(ckk[:, b0:b0 + 16].reshape(DEPTH, 16, 128, 128))
        m["cv"] = np.ascontiguousarray(cvv[:, b0:b0 + 16].reshape(DEPTH, 16, 128, 128))
        m["scv"] = np.ascontiguousarray(scc[:, b0:b0 + 16].reshape(DEPTH, 32, 512))
        sel = np.zeros((128, 8), f)
        if c % 2 == 1:
            sel[:, 0] = 1.0
        m["sel8"] = sel
        m["negf"] = np.full((128, 1), NEG if c % 2 == 0 else 0.0, f)
        in_maps.append(m)
    res = run_bass_kernel_spmd(nc, in_maps, core_ids=list(range(n_cores)))
    return res.results


def kernel(**inputs):
    NTP = 16
    n = 8
    seq_of_core = [(c // 2, (c % 2) * 2048) for c in range(n)]
    samp_of_core = [16 * c for c in range(n)]
    R = run(inputs, NTP, n, seq_of_core, samp_of_core)
    f = np.float32
    yp = np.stack([np.concatenate([R[2 * i]["yp"], R[2 * i + 1]["yp"]], axis=0) for i in range(4)]).astype(f)
    ys = np.concatenate([R[c]["ys"].reshape(16, 8, D) for c in range(n)]).astype(f)
    odd = [1, 3, 5, 7]
    retp = np.stack([R[c]["retp"] for c in odd], axis=1).astype(f)
    wkp = np.stack([R[c]["wkp"].reshape(DEPTH, 128, 2, 64) for c in odd], axis=1).astype(f)
    wvp = np.stack([R[c]["wvp"].reshape(DEPTH, 128, 2, 64) for c in odd], axis=1).astype(f)
    cvp = np.stack([R[c]["cvp"] for c in odd], axis=1).astype(f)
    rets = np.concatenate([R[c]["rets"] for c in range(n)], axis=1).astype(f)
    wks = np.concatenate([R[c]["wks"].reshape(DEPTH, 16, 128, 2, 64) for c in range(n)], axis=1).astype(f)
    wvs = np.concatenate([R[c]["wvs"].reshape(DEPTH, 16, 128, 2, 64) for c in range(n)], axis=1).astype(f)
    cvs = np.concatenate([R[c]["cvs"].reshape(DEPTH, 16, 2, 512) for c in range(n)], axis=1).astype(f)
    return (yp, ys, retp, wkp, wvp, cvp, rets, wks, wvs, cvs)
```

```python
import contextlib
import numpy as np
import concourse.bass as bass
import concourse.mybir as mybir
from concourse.bass_utils import run_bass_kernel_spmd

F32 = mybir.dt.float32
BF = mybir.dt.bfloat16
ALU = mybir.AluOpType
AF = mybir.ActivationFunctionType
AX = mybir.AxisListType

D = 1024
DEPTH = 2
NIN = 6912
EPS = 1e-6
OFF = dict(rq=0, rk=256, rv=512, rg=1024, aq=1536, ak=2048, av=2176, cb=2304, cc=2816, ch=3328, gt=3840)
NEG = -30000.0
import os
STAGE = int(os.environ.get("MK_STAGE", "9"))


class Buf:
    __slots__ = ("name", "w", "r", "x", "sk")

    def __init__(self, name="", x=False):
        self.name = name
        self.w = None
        self.r = {}
        self.sk = None
        self.x = x


class Ctx:
    def __init__(self, nc, n_streams=4):
        self.nc = nc
        self.es = contextlib.ExitStack()
        self.eng = {"pe": nc.tensor, "act": nc.scalar, "dve": nc.vector, "pool": nc.gpsimd, "sp": nc.sync}
        self.sem = {}
        self.cnt = {}
        for k in self.eng:
            self.sem[k] = self.es.enter_context(nc.semaphore("s_" + k))
            self.cnt[k] = 0
        for i in range(n_streams):
            k = "d%d" % i
            self.sem[k] = self.es.enter_context(nc.semaphore("s_" + k))
            self.cnt[k] = 0
        self.seen = {k: {} for k in self.eng}
        self.n_ins = {k: 0 for k in self.eng}
        self.marks = []

    def mark(self, label):
        self.marks.append((label, self.n_ins["pe"]))

    def close(self):
        self.es.close()

    @staticmethod
    def _flat(bs):
        out = []
        for b in bs:
            if isinstance(b, (list, tuple)):
                out.extend(Ctx._flat(b))
            else:
                out.append(b)
        return out

    def _deps(self, e, reads, writes):
        deps = {}
        xr = [b for b in reads if b.x]
        if xr:
            writes = list(writes) + xr
        for b in reads:
            if b.w is not None and deps.get(b.w[0], 0) < b.w[1]:
                deps[b.w[0]] = b.w[1]
        for b in writes:
            if b.w is not None and deps.get(b.w[0], 0) < b.w[1]:
                deps[b.w[0]] = b.w[1]
            for f, v in b.r.items():
                if f != e and deps.get(f, 0) < v:
                    deps[f] = v
        if e == "pe":
            deps.pop("pe", None)
        return deps

    def _wait(self, e, deps):
        for f, v in deps.items():
            if self.seen[e].get(f, 0) < v:
                self.eng[e].wait_ge(self.sem[f], v)
                self.seen[e][f] = v

    def op(self, e, fn, reads=(), writes=(), inc=True):
        reads = self._flat(reads); writes = self._flat(writes)
        self._wait(e, self._deps(e, reads, writes))
        ins = fn(self.eng[e])
        self.n_ins[e] += 1
        if inc:
            ins.then_inc(self.sem[e], 1)
            self.cnt[e] += 1
            idx = self.cnt[e]
        else:
            idx = self.cnt[e] + 1
        for b in reads:
            if b.x:
                b.w = (e, idx)
                b.r = {}
            elif b.r.get(e, 0) < idx:
                b.r[e] = idx
        for b in writes:
            b.w = (e, idx)
            b.r = {}
        return ins

    def dma(self, q, out, in_, reads=(), writes=(), stream=0, own=None, **kw):
        reads = self._flat(reads); writes = self._flat(writes)
        self._wait(q, self._deps(q, reads, writes))
        if own is None:
            own = writes[0] if writes else reads[0]
        if own.sk is None:
            own.sk = {}
        cls = "sw" if q == "pool" else "hw"
        if cls not in own.sk:
            k = "m%d" % len(self.sem)
            own.sk[cls] = k
            self.sem[k] = self.es.enter_context(self.nc.semaphore("s_" + k))
            self.cnt[k] = 0
        k = own.sk[cls]
        ins = self.eng[q].dma_start(out=out, in_=in_, **kw)
        ins.then_inc(self.sem[k], 16)
        self.cnt[k] += 16
        idx = self.cnt[k]
        for b in reads:
            if b.r.get(k, 0) < idx:
                b.r[k] = idx
        for b in writes:
            b.w = (k, idx)
            b.r = {}
        return ins

    def barrier(self):
        for e in self.eng:
            if e == "sp":
                continue
            deps = {f: self.cnt[f] for f in self.cnt if f != e and self.cnt[f] > 0}
            self._wait(e, deps)

    def final_wait(self, e="sp"):
        deps = {f: self.cnt[f] for f in self.cnt if f != e and self.cnt[f] > 0}
        self._wait(e, deps)


def build(NTP, SAMPLE=True, NCORES=8):
    nc = bass.Bass("TRN2", target_bir_lowering=False)
    C = Ctx(nc)
    TP = NTP * 128

    def din(name, shape, dt=F32):
        return nc.dram_tensor(name, list(shape), dt, kind="ExternalInput").ap()

    def dout(name, shape):
        return nc.dram_tensor(name, list(shape), F32, kind="ExternalOutput").ap()

    xp = din("xp", [TP, D]); pp = din("pp", [DEPTH, TP, 256])
    xs = din("xs", [128, D]); ps_ = din("ps", [DEPTH, 128, 256])
    sret = din("sret", [DEPTH, 16, 4, 64, 128])
    ck = din("ck", [DEPTH, 16, 128, 128]); cv = din("cv", [DEPTH, 16, 128, 128])
    scv = din("scv", [DEPTH, 32, 512])
    gTd = din("gT", [128, DEPTH, 5, 8]); cwTd = din("cwT", [128, DEPTH, 4, 3]); sinkd = din("sinkR", [128, DEPTH, 8])
    winF = din("winF", [DEPTH, 13, 128, 8, 512]); winT = din("winT", [DEPTH, 3, 128, 8, 512])
    wbr = din("wbr", [DEPTH, 3, 128, 4, 1024]); wout = din("wout", [DEPTH, 2, 128, 8, 512])
    wff1 = din("wff1", [DEPTH, 8, 128, 8, 512]); wff2 = din("wff2", [DEPTH, 8, 128, 8, 512])
    wpg = din("wpg", [DEPTH, 2, 128, 8, 512]); wpp = din("wpp", [DEPTH, 128, 2, 1024])
    c_ident = din("c_ident", [128, 128]); c_ones = din("c_ones", [128, 128])
    c_dmT = din("c_dmT", [128, 512]); c_dmTs = din("c_dmTs", [128, 512])
    c_qdec = din("c_qdec", [128, 256]); c_qdecs = din("c_qdecs", [128, 512])
    c_kdec = din("c_kdec", [128, 256]); c_kdecs = din("c_kdecs", [128, 256])
    c_cdec = din("c_cdec", [128, 2])
    c_biasP = din("c_biasP", [128, 2048]); c_biasS = din("c_biasS", [128, 2048])
    c_bmQ = din("c_bmQ", [128, 2048]); c_bmQ2 = din("c_bmQ2", [128, 1024]); c_bmV = din("c_bmV", [128, 16])
    sel8d = din("sel8", [128, 8]); negfd = din("negf", [128, 1])
    NG = NTP // 4
    PK = 520
    xs0 = nc.dram_tensor("xscr0", [NG, 128, 8, 512], F32, kind="Internal").ap()
    xs1 = nc.dram_tensor("xscr1", [NG + 1, 128, 8, 512], F32, kind="Internal").ap()
    b_xs0 = [Buf() for _ in range(NG)]; b_xs1 = [Buf() for _ in range(NG + 1)]
    pkg_in = [nc.dram_tensor("pkg_in%d" % l, [128, PK], F32, kind="Internal").ap() for l in range(DEPTH)]
    GS = 2
    pkg_out = [nc.dram_tensor("pkg_out%d" % l, [GS * 128, PK], F32, kind="Internal").ap() for l in range(DEPTH)]

    yp = dout("yp", [TP, D]); ys = dout("ys", [128, D])
    retp = dout("retp", [DEPTH, 4, 64, 128]); wkp = dout("wkp", [DEPTH, 128, 128]); wvp = dout("wvp", [DEPTH, 128, 128])
    cvp = dout("cvp", [DEPTH, 2, 512])
    rets = dout("rets", [DEPTH, 16, 4, 64, 128]); wks = dout("wks", [DEPTH, 16, 128, 128]); wvs = dout("wvs", [DEPTH, 16, 128, 128])
    cvs = dout("cvs", [DEPTH, 32, 512])
    DBG = os.environ.get("MK_DBG", "0") == "1"
    if DBG:
        dbg = dout("dbg", [128, 12, 128])
    DBG2 = os.environ.get("MK_DBG", "0") in ("2", "3")
    if DBG2:
        dbgx = dout("dbgx", [128, 8, 512])

    def sb(name, shape, dt=F32):
        return nc.alloc_sbuf_tensor("sb_" + name, list(shape), dt)

    b_const = Buf("const")
    ident = sb("ident", [128, 128]); identb = sb("identb", [128, 128], BF); onesb = sb("onesb", [128, 128], BF)
    dmT = sb("dmT", [128, 512])
    qdec = sb("qdec", [128, 2, 128])
    kdec = sb("kdec", [128, 256]); cdec = sb("cdec", [128, 2])
    biasP = sb("biasP", [128, 8, 256])
    bmV = sb("bmV", [128, 16], BF)
    gT = sb("gT", [128, DEPTH, 5, 8]); cwT = sb("cwT", [128, DEPTH, 4, 3]); sinkR = sb("sinkR", [128, DEPTH, 8])
    epsb = sb("epsb", [128, 1])
    joinb = sb("joinb", [128, 1])
    sel8 = sb("sel8", [128, 8]); negf = sb("negf", [128, 1])
    C.dma("pool", sel8[:, :], sel8d, writes=[b_const], stream=1)
    C.dma("pool", negf[:, :], negfd, writes=[b_const], stream=1)
    for dst, src in [(ident[:, :], c_ident), (dmT[:, :], c_dmT),
                     (qdec[:, :, :], c_qdec.rearrange("p (a b) -> p a b", a=2)),
                     (kdec[:, :], c_kdec), (cdec[:, :], c_cdec),
                     (biasP[:, :, :], c_biasP.rearrange("p (a b) -> p a b", a=8)),
                     (gT[:, :, :, :], gTd), (cwT[:, :, :, :], cwTd), (sinkR[:, :, :], sinkd)]:
        C.dma("pool", dst, src, writes=[b_const], stream=1)
    for dst, src in [(identb[:, :], c_ident), (onesb[:, :], c_ones),
                     (bmV[:, :], c_bmV)]:
        C.dma("pool", dst, src, writes=[b_const], stream=0)
    C.op("dve", lambda e: e.memset(epsb[:, :], EPS), writes=[b_const])

    xT = sb("xT", [128, 8, 512]); b_xT = [Buf("xT%d" % c) for c in range(8)]
    hT = sb("hT", [128, 8, 512], BF); b_hT = [Buf("hT%d" % c) for c in range(8)]
    yT = sb("yT", [128, 8, 512]); b_yT = [Buf("yT%d" % c) for c in range(8)]
    sq = sb("sq", [128, 2, 512], BF); b_sq = [Buf(), Buf()]
    rstd = sb("rstd", [128, 512]); b_rstd = Buf()
    tmpA = sb("tmpA", [128, 2, 512]); b_tmpA = [Buf(), Buf()]
    oT = sb("oT", [128, 12, 512], BF); b_oT = [Buf("oT0"), Buf("oT1"), Buf("oT2")]
    mixT = sb("mixT", [128, 8, 512], BF); b_mixT = Buf("mixT")
    pT2 = sb("pT", [128, 2, 2, 512], BF); b_pT2 = [Buf("pT0"), Buf("pT1")]
    uT = sb("uT", [128, 4, 516]); b_uT = Buf("uT")
    uhalo = sb("uhalo", [128, DEPTH, 4, 2]); b_uhalo = Buf()
    akT = [sb("akT%d" % l, [128, 640], BF) for l in range(DEPTH)]; b_akT = [Buf(), Buf()]
    avt = [sb("avt%d" % l, [128, 5, 128], BF) for l in range(DEPTH)]; b_avt = [Buf(), Buf()]
    S = [sb("S%d" % l, [128, 2, 128]) for l in range(DEPTH)]; b_S = [Buf(), Buf()]
    Sb = [sb("Sb%d" % l, [128, 2, 128], BF) for l in range(DEPTH)]
    ssb = sb("ssb", [128, 8, 256]); b_ssb = Buf()
    pbf = sb("pbf", [128, 8, 256], BF); b_pbf = Buf()
    pTs = sb("pTs", [128, 8, 2, 128], BF); b_pTs = Buf()
    st8 = sb("st8", [128, 8, 8]); b_st8 = Buf()
    kvf = sb("kvf", [128, 256]); b_kvf = Buf()
    innT = sb("innT", [128, 4, 128], BF); b_innT = Buf()
    ortok = sb("ortok", [128, 512], BF); b_ortok = Buf()
    xin = sb("xin", [128, 1, 1024]); _bx = Buf(); b_xin = [_bx, _bx]
    pin = sb("pin", [128, 1, 256]); _bp = Buf(); b_pin = [_bp, _bp]
    NS = 6
    wring = sb("wring", [128, NS, 4096], BF); b_wr = [Buf("wr%d" % i) for i in range(NS)]
    AR = sb("AR", [128, 16896], BF); b_AR = Buf("AR")

    def arv(off_bytes, nbytes, dt, pat=None, **kw):
        v = AR[:, off_bytes // 2:(off_bytes + nbytes) // 2]
        if dt == F32:
            v = v.bitcast(F32)
        if pat:
            v = v.rearrange(pat, **kw)
        return v

    for l in range(DEPTH):
        C.op("dve", lambda e: e.memset(S[l][:, :, :], 0.0), writes=[b_S[l]])
        C.op("dve", lambda e: e.memset(Sb[l][:, :, :], 0.0), writes=[b_S[l]])
        C.op("dve", lambda e: e.memset(akT[l][:, :], 0.0), writes=[b_akT[l]])
        C.op("dve", lambda e: e.memset(avt[l][:, :, :], 0.0), writes=[b_avt[l]])
    C.op("dve", lambda e: e.memset(uhalo[:, :, :, :], 0.0), writes=[b_uhalo])

    psb = [nc.alloc_psum_tensor("psb%d" % i, [128, 512], F32) for i in range(8)]
    b_ps = [Buf("ps%d" % i, x=True) for i in range(8)]
    bank_i = [0]

    held = set()

    def bank(hold=False):
        i = bank_i[0]
        while i in held:
            i = (i + 1) % 8
        bank_i[0] = (i + 1) % 8
        if hold:
            held.add(i)
        return psb[i], b_ps[i]

    def unhold(bb):
        held.discard(b_ps.index(bb))

    ev_i = [0]

    def evac(out, in_, reads, writes):
        ev_i[0] = (ev_i[0] + 1) % 3
        if ev_i[0]:
            C.op("act", lambda e: e.copy(out, in_), reads, writes)
        else:
            C.op("dve", lambda e: e.tensor_copy(out, in_), reads, writes)

    wr_i = [0]

    scr_t = {}
    scr_b = {}

    def wload(arr, idx, kc, ncols):
        name, ap = arr
        if name not in scr_t:
            scr_t[name] = nc.dram_tensor("scr_" + name, list(ap.shape), BF, kind="Internal").ap()
        src = ap
        dst = scr_t[name]
        for j in idx:
            src = src[j]
            dst = dst[j]
        src = src[:, :, 0:ncols]
        dst = dst[:, :, 0:ncols]
        i = wr_i[0]
        wr_i[0] = (i + 1) % NS
        v = wring[:, i, 0:kc * ncols].rearrange("p (k n) -> p k n", k=kc)
        key = (name,) + tuple(idx)
        if key not in scr_b:
            scr_b[key] = Buf("scr")
            C.dma("pool", v, src, writes=[b_wr[i]], stream=0)
            C.dma("sp", dst, v, reads=[b_wr[i]], writes=[scr_b[key]], own=b_wr[i])
        else:
            C.dma("sp", v, dst, reads=[scr_b[key]], writes=[b_wr[i]], stream=3)
        return v, b_wr[i]

    A_winF = ("winF", winF); A_winT = ("winT", winT); A_wbr = ("wbr", wbr); A_wout = ("wout", wout)
    A_wff1 = ("wff1", wff1); A_wff2 = ("wff2", wff2); A_wpg = ("wpg", wpg); A_wpp = ("wpp", wpp)

    P = lambda fn, r, w, inc=True: C.op("pe", fn, r, w, inc)
    A = lambda fn, r, w: C.op("act", fn, r, w)
    V = lambda fn, r, w: C.op("dve", fn, r, w)

    def stat_begin():
        bk, bb = bank(hold=True)
        return {"bk": bk, "bb": bb, "n": 0}

    def stat_add(st, srcT, b_src, c, N):
        k = st["n"]
        A(lambda e: e.activation(sq[:, k % 2, :N], srcT[:, c, :N], AF.Square), [b_src[c]], [b_sq[k % 2]])
        P(lambda e: e.matmul(st["bk"][:, :N], onesb[:, :], sq[:, k % 2, :N], start=(k == 0), stop=(k == 7)),
          [b_sq[k % 2], b_const], [st["bb"]])
        st["n"] = k + 1

    def stat_finish(st, N):
        assert st["n"] == 8
        bk, bb = st["bk"], st["bb"]
        A(lambda e: e.activation(rstd[:, :N], bk[:, :N], AF.Ln, bias=epsb[:, 0:1], scale=1.0), [bb, b_const], [b_rstd])
        A(lambda e: e.activation(rstd[:, :N], rstd[:, :N], AF.Exp, scale=-0.5), [b_rstd], [b_rstd])
        unhold(bb)

    def pre_norm(l, gi, N, st=None):
        if st is None:
            st = stat_begin()
            for c in range(8):
                stat_add(st, xT, b_xT, c, N)
        stat_finish(st, N)
        for c in range(8):
            V(lambda e: e.scalar_tensor_tensor(hT[:, c, :N], xT[:, c, :N], gT[:, l, gi, c:c + 1], rstd[:, :N],
                                               ALU.mult, ALU.mult), [b_xT[c], b_rstd, b_const], [b_hT[c]])

    def post_norm_add(l, gi, N, st=None):
        if st is None:
            st = stat_begin()
            for c in range(8):
                stat_add(st, yT, b_yT, c, N)
        stat_finish(st, N)
        st2 = stat_begin()
        for c in range(8):
            V(lambda e: e.scalar_tensor_tensor(tmpA[:, c % 2, :N], yT[:, c, :N], gT[:, l, gi, c:c + 1], rstd[:, :N],
                                               ALU.mult, ALU.mult), [b_yT[c], b_rstd, b_const], [b_tmpA[c % 2]])
            C.op("pool", lambda e: e.tensor_tensor(xT[:, c, :N], xT[:, c, :N], tmpA[:, c % 2, :N], ALU.add),
                 [b_xT[c], b_tmpA[c % 2]], [b_xT[c]])
            stat_add(st2, xT, b_xT, c, N)
        return st2

    def fmm4(wv, wb, N):
        banks = [bank(hold=True) for _ in range(4)]
        for kc in range(8):
            for m in range(4):
                bk, bb = banks[m]
                P(lambda e: e.matmul(bk[:, :N], wv[:, kc, m * 128:(m + 1) * 128], hT[:, kc, :N], start=(kc == 0), stop=(kc == 7)),
                  [wb, b_hT[kc]], [bb])
        return banks

    def fmm(bk, bb, wv, wb, kcs, cols, rhsT, b_rhs, N):
        for kc in range(kcs):
            P(lambda e: e.matmul(bk[:, :N], wv[:, kc, cols], rhsT[:, kc, :N], start=(kc == 0), stop=(kc == kcs - 1)),
              [wb, b_rhs], [bb])

    def process_layer(l, kind, N, first_group, last_group, pslot=0, prefetch=None):
        pT = pT2[:, pslot]; b_pT = b_pT2[pslot]
        nt = N // 128
        smp = (kind == "s")
        zq = arv(0, 9 * N * 2, BF, "p (c n) -> p c n", c=9)
        qd = arv(9216, 2 * N * 2, BF, "p (c n) -> p c n", c=2)
        rkt = arv(11264, nt * 256 * 2, BF, "p (t n) -> p t n", t=nt)
        rvt = arv(13312, nt * 512 * 2, BF, "p (t n) -> p t n", t=nt)
        rgt = arv(17408, nt * 512 * 4, F32, "p (t n) -> p t n", t=nt)
        ccT = arv(25600, 4 * N * 4, F32, "p (c n) -> p c n", c=4)
        hidT = arv(0, 32 * N * 2, BF, "p (c n) -> p c n", c=32)
        b_zq = Buf("zq"); b_qd = Buf("qd"); b_rkt = Buf("rkt"); b_rvt = Buf("rvt"); b_rgt = Buf("rgt"); b_ccT = Buf("ccT")
        b_hid = Buf("hid")
        if smp:
            S0x = [arv(27648, 4096, F32, "p (b e) -> p b e", b=8), arv(19456, 4096, F32, "p (b e) -> p b e", b=8)]
            S0bx = [arv(31744, 2048, BF, "p (b e) -> p b e", b=8), arv(23552, 2048, BF, "p (b e) -> p b e", b=8)]
            b_S0x = [Buf(), Buf()]
            kcTb = lambda b: oT[:, b // 2, 128 + (b % 2) * 128:256 + (b % 2) * 128]
            vcb = lambda b: mixT[:, b // 2, 128 + (b % 2) * 128:256 + (b % 2) * 128]
            aqm4 = yT[:, 0:8, 128:256].bitcast(BF).rearrange("p c (two i) -> p c two i", two=2)
            aqm_b = lambda b: aqm4[:, b // 2, b % 2, :]
            qdm = yT[:, 0:8, 256:320].bitcast(BF)
            qdp = yT[:, 0:4, 320:384].bitcast(BF)
            kinx = [arv(2304, 2048, F32, "p (b s) -> p b s", b=4), arv(6400, 2048, F32, "p (b s) -> p b s", b=4)]
            vinx = [arv(4352, 2048, F32, "p (b s) -> p b s", b=4), arv(8448, 2048, F32, "p (b s) -> p b s", b=4)]
            b_kinx = [Buf(), Buf()]; b_vinx = [Buf(), Buf()]
            b_kcT = Buf(); b_vc = Buf(); b_aqm = Buf(); b_qdm = Buf(); b_qdp = Buf()
        C.barrier()

        if smp:
            b_sc = [Buf("sc%d" % i) for i in range(6)]
            C.dma("pool", xT[:, :, 128:384], c_biasS.rearrange("p (a b) -> p a b", a=8), writes=[b_sc[0]], stream=1)
            C.dma("pool", xT[:, 0:4, 384:512], c_dmTs.rearrange("p (h i) -> p h i", h=4), writes=[b_sc[1]], stream=1)
            C.dma("pool", xT[:, 4:8, 384:512], c_qdecs.rearrange("p (h i) -> p h i", h=4), writes=[b_sc[2]], stream=1)
            C.dma("pool", yT[:, 0:2, 384:512], c_kdecs.rearrange("p (a i) -> p a i", a=2), writes=[b_sc[3]], stream=1)
            C.dma("pool", hT[:, 0:8, 128:384], c_bmQ.rearrange("p (c x) -> p c x", c=8), writes=[b_sc[4]], stream=1)
            C.dma("pool", hT[:, 0:8, 384:512], c_bmQ2.rearrange("p (c x) -> p c x", c=8), writes=[b_sc[5]], stream=1)
            V(lambda e: e.memset(joinb[:, :], 0.0), [b_sc], [b_const])
            for two in range(2):
                C.dma("pool", mixT[:, 0:8, 128 + two * 128:256 + two * 128],
                      cv[l].rearrange("(c two) s f -> two s c f", two=2)[two], writes=[b_vc], stream=0)
            for b4 in range(4):
                kin, vin = kinx[b4 % 2], vinx[b4 % 2]
                b_kin, b_vin = b_kinx[b4 % 2], b_vinx[b4 % 2]
                C.dma("pool", kin, ck[l, b4 * 4:(b4 + 1) * 4].rearrange("b s f -> s b f"), writes=[b_kin], stream=1)
                C.dma("pool", vin, cv[l, b4 * 4:(b4 + 1) * 4].rearrange("b s f -> s b f"), writes=[b_vin], stream=1)
                bk, bb = bank()
                for j in range(4):
                    P(lambda e: e.transpose(bk[:, j * 128:(j + 1) * 128], kin[:, j, :], ident[:, :]), [b_kin, b_const], [bb])
                evac(oT[:, 2 * b4:2 * b4 + 2, 128:384].rearrange("p c (two s) -> p c two s", two=2),
                     bk[:, :].rearrange("p (c two s) -> p c two s", c=2, two=2), [bb], [b_kcT])
                C.dma("pool", wks[l, b4 * 4:(b4 + 1) * 4, 0:120, :].rearrange("b s f -> s b f"), kin[8:128, :, :], reads=[b_kin], stream=2)
                C.dma("pool", wvs[l, b4 * 4:(b4 + 1) * 4, 0:120, :].rearrange("b s f -> s b f"), vin[8:128, :, :], reads=[b_vin], stream=2)

        if STAGE < 1:
            return
        C.mark("%s%d prenorm" % (kind, l))
        pre_norm(l, 0, N)
        C.mark("%s%d Fproj" % (kind, l))

        if os.environ.get("MK_SUB", "1") == "0":
            return
        def fblock(bi, ncols=512):
            return wload(A_winF, (l, bi), 8, ncols)

        wv, wb = fblock(0)
        SUB = os.environ.get("MK_SUB", "1")
        if SUB == "a":
            return
        pre = fmm4(wv, wb, N)
        for m in range(4):
            bk, bb = pre[m]
            unhold(bb)
            if SUB == "b":
                continue
            evac(zq[:, m, :N], bk[:, :N], [bb], [b_zq])
            if SUB == "c":
                continue
            if m < 2 and not smp:
                for t in range(nt):
                    V(lambda e: e.tensor_tensor(qd[:, m, t * 128:(t + 1) * 128], bk[:, t * 128:(t + 1) * 128],
                                                qdec[:, m, :], ALU.mult), [bb, b_const], [b_qd])
        if SUB in "abcd":
            return
        wv, wb = fblock(1)
        for m in range(4):
            bk, bb = bank()
            fmm(bk, bb, wv, wb, 8, slice(m * 128, (m + 1) * 128), hT, b_hT, N)
            evac(zq[:, 4 + m, :N], bk[:, :N], [bb], [b_zq])
        if SUB == "e":
            return
        wv, wb = fblock(2, 128)
        bk, bb = bank()
        fmm(bk, bb, wv, wb, 8, slice(0, 128), hT, b_hT, N)
        evac(akT[l][:, 128:128 + N], bk[:, :N], [bb], [b_akT[l]])

        if STAGE < 2:
            return
        C.mark("%s%d conv" % (kind, l))
        if smp:
            uv = uT[:, :, 0:160].rearrange("p c (b t) -> p c b t", b=16)
            C.dma("pool", xin[0:32, 0, 0:512], scv[l, :, :], writes=[b_xin[0]], stream=1)
            bk, bb = bank()
            for c in range(4):
                P(lambda e: e.transpose(bk[:, c * 32:(c + 1) * 32], xin[0:32, 0, c * 128:(c + 1) * 128], ident[0:32, 0:32]),
                  [b_xin[0], b_const], [bb])
            V(lambda e: e.tensor_copy(uv[:, :, :, 0:2], bk[:, 0:128].rearrange("p (c b j) -> p c b j", c=4, b=16)),
              [bb], [b_uT])
            ucur = lambda c: uv[:, c, :, 2:10]
            ush = lambda c, j: uv[:, c, :, j:j + 8]
            v3 = lambda ap: ap.rearrange("p (b t) -> p b t", b=16)
        else:
            V(lambda e: e.tensor_copy(uT[:, :, 0:2], uhalo[:, l, :, :]), [b_uhalo], [b_uT])
            ucur = lambda c: uT[:, c, 2:2 + N]
            ush = lambda c, j: uT[:, c, j:j + N]
            v3 = lambda ap: ap
        def conv_cc():
            wv, wb = fblock(3)
            for c in range(4):
                bk, bb = bank()
                fmm(bk, bb, wv, wb, 8, slice(c * 128, (c + 1) * 128), hT, b_hT, N)
                evac(ccT[:, c, :N], bk[:, :N], [bb], [b_ccT])

        def conv_ch():
            wv, wb = fblock(4)
            for c in range(4):
                bk, bb = bank()
                fmm(bk, bb, wv, wb, 8, slice(c * 128, (c + 1) * 128), hT, b_hT, N)
                V(lambda e: e.tensor_tensor(ucur(c), v3(ccT[:, c, :N]), v3(bk[:, :N]), ALU.mult), [bb, b_ccT], [b_uT])

        def conv_cb():
          wv, wb = fblock(5)
          for c in range(4):
              ycv = tmpA[:, c % 2, :N]
              V(lambda e: e.tensor_scalar(v3(ycv), ush(c, 0), cwT[:, l, c, 0:1], None, ALU.mult),
                [b_uT, b_const], [b_tmpA[c % 2]])
              for j in (1, 2):
                  V(lambda e: e.scalar_tensor_tensor(v3(ycv), ush(c, j), cwT[:, l, c, j:j + 1], v3(ycv), ALU.mult, ALU.add),
                    [b_uT, b_const, b_tmpA[c % 2]], [b_tmpA[c % 2]])
              bk, bb = bank()
              fmm(bk, bb, wv, wb, 8, slice(c * 128, (c + 1) * 128), hT, b_hT, N)
              V(lambda e: e.tensor_tensor(oT[:, 8 + c, :N], bk[:, :N], ycv, ALU.mult), [bb, b_tmpA[c % 2]], [b_oT[2]])
          if smp:
              bk, bb = bank()
              V(lambda e: e.tensor_copy(tmpA[:, 0, 0:128].rearrange("p (c b j) -> p c b j", c=4, b=16), uv[:, :, :, 8:10]),
                [b_uT], [b_tmpA[0]])
              for c in range(4):
                  P(lambda e: e.transpose(bk[0:32, c * 128:(c + 1) * 128], tmpA[:, 0, c * 32:(c + 1) * 32], ident[:, :]),
                    [b_tmpA[0], b_const], [bb])
              V(lambda e: e.tensor_copy(xin[0:32, 0, 512:1024], bk[0:32, 0:512]), [bb], [b_xin[1]])
              C.dma("pool", cvs[l, :, :], xin[0:32, 0, 512:1024], reads=[b_xin[1]], stream=2)
          else:
              V(lambda e: e.tensor_copy(uhalo[:, l, :, :], uT[:, :, N:N + 2]), [b_uT], [b_uhalo])
              if last_group:
                  bk, bb = bank()
                  V(lambda e: e.tensor_copy(tmpA[:, 0, 0:8].rearrange("p (c j) -> p c j", c=4), uT[:, :, N:N + 2]),
                    [b_uT], [b_tmpA[0]])
                  P(lambda e: e.transpose(bk[0:8, 0:128], tmpA[:, 0, 0:8], ident[:, :]), [b_tmpA[0], b_const], [bb])
                  V(lambda e: e.tensor_copy(xin[0:8, 0, 0:128], bk[0:8, 0:128]), [bb], [b_xin[1]])
                  for c in range(4):
                      C.dma("pool", cvp[l, :, c * 128:(c + 1) * 128], xin[2 * c:2 * c + 2, 0, 0:128], reads=[b_xin[1]], stream=2)

        if smp:
            conv_cc(); conv_ch(); conv_cb()

        if STAGE < 3:
            return
        C.mark("%s%d Tproj" % (kind, l))
        for bi in range(3):
            wv, wb = wload(A_winT, (l, bi), 8, 512)
            for t in range(nt):
                bk, bb = bank()
                for kc in range(8):
                    P(lambda e: e.matmul(bk[:, :], hT[:, kc, t * 128:(t + 1) * 128], wv[:, kc, :], start=(kc == 0), stop=(kc == 7)),
                      [wb, b_hT], [bb])
                if bi == 0:
                    if smp:
                        V(lambda e: e.tensor_tensor(rkt[:, t, :].rearrange("p (a i) -> p a i", a=2),
                                                    bk[:, 0:256].rearrange("p (a i) -> p a i", a=2), yT[:, 0:2, 384:512], ALU.mult),
                          [bb, b_const], [b_rkt])
                    else:
                        V(lambda e: e.tensor_tensor(rkt[:, t, :], bk[:, 0:256], kdec[:, :], ALU.mult), [bb, b_const], [b_rkt])
                    A(lambda e: e.copy(avt[l][:, 1 + t, :], bk[:, 384:512]), [bb], [b_avt[l]])
                    if smp or (last_group and t == nt - 1):
                        A(lambda e: e.copy(kvf[:, :], bk[:, 256:512]), [bb], [b_kvf])
                        if smp:
                            for b in range(16):
                                C.dma("pool", wks[l, b, 120:128, :], kvf[b * 8:(b + 1) * 8, 0:128], reads=[b_kvf], stream=2)
                                C.dma("pool", wvs[l, b, 120:128, :], kvf[b * 8:(b + 1) * 8, 128:256], reads=[b_kvf], stream=2)
                        else:
                            C.dma("pool", wkp[l, :, :], kvf[:, 0:128], reads=[b_kvf], stream=2)
                            C.dma("pool", wvp[l, :, :], kvf[:, 128:256], reads=[b_kvf], stream=2)
                elif bi == 1:
                    evac(rvt[:, t, :], bk[:, :], [bb], [b_rvt])
                else:
                    A(lambda e: e.activation(rgt[:, t, :], bk[:, :], AF.Silu), [bb], [b_rgt])

        if STAGE < 4:
            return
        C.mark("%s%d mixers" % (kind, l))
        if smp:
            wv, wb = fblock(12)
            for h in range(4):
                bk, bb = bank()
                fmm(bk, bb, wv, wb, 8, slice(h * 128, (h + 1) * 128), hT, b_hT, N)
                V(lambda e: e.tensor_tensor(qdp[:, h, :], bk[:, :N], xT[:, 4 + h, 384:512], ALU.mult), [bb, b_const], [b_qdp])

        NK = 256
        bias = xT[:, :, 128:384] if smp else biasP
        koff = 0
        nb = 2
        mx = st8[:, 1, :]; ng = st8[:, 2, :]; rs = st8[:, 3, :]; es = st8[:, 4, :]

        def tile_ret(t):
            tc = slice(t * 128, (t + 1) * 128)
            first_tile = (not smp) and first_group and t == 0
            bIs = [bank(), bank()]
            for h in range(4):
                hp, s = h // 2, h % 2
                pr = slice(s * 64, s * 64 + 64)
                bI, bbI = bIs[s]
                P(lambda e: e.matmul(bI[:, hp * 128:(hp + 1) * 128], zq[pr, 2 + hp, tc], zq[pr, hp, tc], start=True, stop=True),
                  [b_zq], [bbI])
            if smp:
                dmv = xT[:, 0:4, 384:512].rearrange("p (hp s) i -> p s hp i", s=2)
            else:
                dmv = dmT[:, :].rearrange("p (hp s i) -> p s hp i", hp=2, s=2)
            inv = innT[:, :, :].rearrange("p (hp s) i -> p s hp i", s=2)
            for s in range(2):
                bI, bbI = bIs[s]
                V(lambda e: e.tensor_tensor(inv[:, s], bI[:, 0:256].rearrange("p (hp i) -> p hp i", hp=2), dmv[:, s], ALU.mult),
                  [bbI, b_const], [b_innT])
            bO, bbO = bank(hold=True)
            if not smp:
                for h in range(4):
                    hp, s = h // 2, h % 2
                    pr = slice(s * 64, s * 64 + 64)
                    P(lambda e: e.matmul(bO[:, h * 128:(h + 1) * 128], innT[:, h, :], rvt[:, t, h * 128:(h + 1) * 128],
                                         start=True, stop=False), [b_innT, b_rvt], [bbO])
                    P(lambda e: e.matmul(bO[:, h * 128:(h + 1) * 128], qd[pr, hp, tc], Sb[l][pr, hp, :],
                                         start=False, stop=True), [b_qd, b_S[l]], [bbO])
                bS, bbS = bank()
                for h in range(4):
                    hp, s = h // 2, h % 2
                    pr = slice(s * 64, s * 64 + 64)
                    P(lambda e: e.matmul(bS[pr, hp * 128:(hp + 1) * 128], rkt[:, t, h * 64:(h + 1) * 64],
                                         rvt[:, t, h * 128:(h + 1) * 128], start=True, stop=True), [b_rkt, b_rvt], [bbS])
                for hp in range(2):
                    V(lambda e: e.scalar_tensor_tensor(S[l][:, hp, :], S[l][:, hp, :], cdec[:, hp:hp + 1],
                                                       bS[:, hp * 128:(hp + 1) * 128], ALU.mult, ALU.add),
                      [bbS, b_S[l], b_const], [b_S[l]])
                A(lambda e: e.copy(Sb[l][:, :, :], S[l][:, :, :]), [b_S[l]], [b_S[l]])
                if last_group and t == nt - 1:
                    for s in range(2):
                        C.dma("pool", retp[l].rearrange("(hp s) d e -> s d hp e", s=2)[s],
                              S[l][s * 64:(s + 1) * 64, :, :], reads=[b_S[l]], stream=2)
            else:
                def load_state(h):
                    for bh in range(2):
                        C.dma("pool", S0x[h % 2][bh * 64:(bh + 1) * 64, :, :],
                              sret[l, bh * 8:(bh + 1) * 8, h, :, :].rearrange("b d e -> d b e"), writes=[b_S0x[h % 2]], stream=1)

                load_state(0)
                for h in range(4):
                    S0, S0b, b_S0 = S0x[h % 2], S0bx[h % 2], b_S0x[h % 2]
                    if h + 1 < 4:
                        load_state(h + 1)
                    A(lambda e: e.copy(S0b[:, :, :], S0[:, :, :]), [b_S0], [b_S0])
                    V(lambda e: e.tensor_tensor(qdm, qdp[:, h:h + 1, :].broadcast_to([128, 8, 128]), hT[:, 0:8, 384:512], ALU.mult),
                      [b_qdp, b_const], [b_qdm])
                    P(lambda e: e.matmul(bO[:, h * 128:(h + 1) * 128], innT[:, h, :], rvt[:, t, h * 128:(h + 1) * 128],
                                         start=True, stop=False), [b_innT, b_rvt], [bbO])
                    for b in range(16):
                        bh, b2 = b // 8, b % 8
                        pr = slice(bh * 64, bh * 64 + 64)
                        P(lambda e: e.matmul(bO[:, h * 128:(h + 1) * 128], qdm[pr, b2, :], S0b[pr, b2, :],
                                             start=False, stop=(b == 15)), [b_qdm, b_S0], [bbO], inc=(b in (7, 15)))
                        if b == 7:
                            C.eng["pe"].wait_ge(C.sem["pe"], C.cnt["pe"])
                    vbd = pbf[:, :, :].rearrange("p h k -> p (h k)").rearrange("p (b e) -> p b e", b=16)
                    V(lambda e: e.tensor_tensor(vbd, rvt[:, t, h * 128:(h + 1) * 128].rearrange("p (o e) -> p o e", o=1).broadcast_to([128, 16, 128]),
                                                bmV[:, :].rearrange("p (b o) -> p b o", o=1).broadcast_to([128, 16, 128]), ALU.mult),
                      [b_rvt, b_const], [b_pbf])
                    bS1, bbS1 = bank()
                    bS2, bbS2 = bank()
                    assert bbS1 is not bbO and bbS2 is not bbO
                    for bh in range(2):
                        pr = slice(bh * 64, bh * 64 + 64)
                        for q4, (bSx, bbSx) in enumerate(((bS1, bbS1), (bS2, bbS2))):
                            P(lambda e: e.matmul(bSx[pr, :], rkt[:, t, h * 64:(h + 1) * 64],
                                                 vbd[:, bh * 8 + q4 * 4: bh * 8 + q4 * 4 + 4, :], start=True, stop=True),
                              [b_rkt, b_pbf], [bbSx])
                    c8 = float(np.float32(np.exp(np.float32(8.0) * np.log1p(-np.exp2(np.float32(-5.0 - h))))))
                    for q4, (bSx, bbSx) in enumerate(((bS1, bbS1), (bS2, bbS2))):
                        V(lambda e: e.scalar_tensor_tensor(S0[:, q4 * 4:(q4 + 1) * 4, :], S0[:, q4 * 4:(q4 + 1) * 4, :], c8,
                                                           bSx[:, :].rearrange("p (b e) -> p b e", b=4), ALU.mult, ALU.add),
                          [bbSx, b_S0], [b_S0])
                    for bh in range(2):
                        C.dma("pool", rets[l, bh * 8:(bh + 1) * 8, h, :, :].rearrange("b d e -> d b e"),
                              S0[bh * 64:(bh + 1) * 64, :, :], reads=[b_S0], stream=2)
            ssqc = st8[:, 0, 0:4]
            V(lambda e: e.memset(st8[:, 0, 0:4], 0.0), [], [b_st8])
            for h in range(4):
                A(lambda e: e.activation(ssb[:, 0, 0:128], bO[:, h * 128:(h + 1) * 128], AF.Square,
                                         accum_out=st8[:, 0, h:h + 1]), [bbO], [b_st8, b_ssb])
            A(lambda e: e.activation(st8[:, 0, 4:8], ssqc, AF.Ln, bias=epsb[:, 0:1], scale=1.0 / 128.0), [b_st8, b_const], [b_st8])
            A(lambda e: e.activation(st8[:, 0, 4:8], st8[:, 0, 4:8], AF.Exp, scale=-0.5), [b_st8], [b_st8])
            for h in range(4):
                V(lambda e: e.scalar_tensor_tensor(ortok[:, h * 128:(h + 1) * 128], bO[:, h * 128:(h + 1) * 128],
                                                   st8[:, 0, 4 + h:5 + h], rgt[:, t, h * 128:(h + 1) * 128], ALU.mult, ALU.mult),
                  [bbO, b_st8, b_rgt], [b_ortok])
            unhold(bbO)
            bT, bbT = bank()
            bTb = bT[:, :].bitcast(BF)
            for h in range(4):
                P(lambda e: e.transpose(bTb[:, h * 128:(h + 1) * 128], ortok[:, h * 128:(h + 1) * 128], identb[:, :]),
                  [b_ortok, b_const], [bbT])
            evac(oT[:, 0:4, tc], bTb[:, 0:512].rearrange("p (c n) -> p c n", c=4), [bbT], [b_oT[0]])


        def tile_att_scores(t):
            tc = slice(t * 128, (t + 1) * 128)
            first_tile = (not smp) and first_group and t == 0
            scb = []
            for half in range(2):
                b1, bb1 = bank(hold=True); b2, bb2 = bank(hold=True)
                scb.append(((b1, bb1), (b2, bb2)))
            for h in [0, 4, 1, 5, 2, 6, 3, 7]:
                c, s = h % 4, h // 4
                pr = slice(s * 64, s * 64 + 64)
                bkk, bbk = scb[h // 4][(h % 4) // 2]
                oc = (h % 2) * 256
                if smp:
                    if s == 0:
                        V(lambda e: e.tensor_tensor(aqm4, zq[:, 4 + c:5 + c, 0:128].rearrange("p (a b) i -> p a b i", a=1).broadcast_to([128, 8, 2, 128]),
                                                    hT[:, 0:8, 128:384].rearrange("p c (two i) -> p c two i", two=2), ALU.mult),
                          [b_zq, b_const], [b_aqm])
                    for b in range(16):
                        P(lambda e: e.matmul(bkk[:, oc:oc + 128], aqm_b(b)[pr, :], kcTb(b)[pr, :], start=(b == 0), stop=(b == 15)),
                          [b_aqm, b_kcT], [bbk], inc=(b == 15))
                    P(lambda e: e.matmul(bkk[:, oc + 128:oc + 256], zq[pr, 4 + c, tc], akT[l][pr, 128:256], start=True, stop=True),
                      [b_zq, b_akT[l]], [bbk])
                else:
                    P(lambda e: e.matmul(bkk[:, oc:oc + NK], zq[pr, 4 + c, tc], akT[l][pr, (t + 2) * 128 - NK:(t + 2) * 128],
                                         start=True, stop=True), [b_zq, b_akT[l]], [bbk])
            for h2 in range(4):
                bkk, bbk = scb[h2 // 2][h2 % 2]
                V(lambda e: e.scalar_tensor_tensor(ssb[:, 2 * h2:2 * h2 + 2, 0:NK],
                                                   bkk[:, :].rearrange("p (h k) -> p h k", h=2)[:, :, 0:NK], 0.125,
                                                   bias[:, 2 * h2:2 * h2 + 2, koff:256], ALU.mult, ALU.add),
                  [bbk, b_const], [b_ssb])
            for half in range(2):
                unhold(scb[half][0][1]); unhold(scb[half][1][1])
            if first_tile:
                V(lambda e: e.tensor_scalar(ssb[:, :, 0:128], ssb[:, :, 0:128], negf[:, 0:1], None, ALU.add),
                  [b_ssb, b_const], [b_ssb])
            V(lambda e: e.tensor_reduce(mx, ssb[:, :, 0:NK], AX.X, ALU.max), [b_ssb], [b_st8])
            V(lambda e: e.tensor_tensor(mx, mx, sinkR[:, l, :], ALU.max), [b_st8, b_const], [b_st8])
            V(lambda e: e.tensor_scalar(ng, mx, -1.0, None, ALU.mult), [b_st8], [b_st8])
            V(lambda e: e.memset(rs, 0.0), [], [b_st8])

        def tile_att_exp(t):
            for h in range(8):
                A(lambda e: e.activation(pbf[:, h, 0:NK], ssb[:, h, 0:NK], AF.Exp, bias=ng[:, h:h + 1], scale=1.0,
                                         accum_out=rs[:, h:h + 1]), [b_ssb, b_st8], [b_pbf, b_st8])

        def tile_att_norm(t):
            V(lambda e: e.tensor_tensor(es, sinkR[:, l, :], ng, ALU.add), [b_st8, b_const], [b_st8])
            A(lambda e: e.activation(es, es, AF.Exp), [b_st8], [b_st8])
            V(lambda e: e.tensor_tensor(rs, rs, es, ALU.add), [b_st8], [b_st8])
            V(lambda e: e.reciprocal(rs, rs), [b_st8], [b_st8])
            V(lambda e: e.tensor_tensor(pbf[:, :, 0:NK], pbf[:, :, 0:NK],
                                        rs.rearrange("p (h o) -> p h o", o=1).broadcast_to([128, 8, NK]), ALU.mult),
              [b_pbf, b_st8], [b_pbf])

        def tile_att_out(t):
            tc = slice(t * 128, (t + 1) * 128)
            first_tile = (not smp) and first_group and t == 0
            for half in range(2):
                bT, bbT = bank()
                bTb = bT[:, :].bitcast(BF)
                for hh in range(4):
                    h = half * 4 + hh
                    for blk in range(nb):
                        P(lambda e: e.transpose(bTb[:, (hh * 2 + blk) * 128:(hh * 2 + blk + 1) * 128],
                                                pbf[:, h, blk * 128:(blk + 1) * 128], identb[:, :]), [b_pbf, b_const], [bbT])
                evac(pTs[:, half * 4:(half + 1) * 4, 0:nb, :],
                     bTb[:, :].rearrange("p (h b q) -> p h b q", h=4, b=2)[:, :, 0:nb, :], [bbT], [b_pTs])
            bV, bbV = bank()
            for h in range(8):
                kv = h // 4
                po = slice((h % 2) * 64, (h % 2) * 64 + 64)
                oc = (h // 2) * 128
                if smp:
                    P(lambda e: e.matmul(bV[po, oc:oc + 128], avt[l][:, 1 + t, kv * 64:(kv + 1) * 64], pTs[:, h, 1, :],
                                         start=True, stop=False), [b_avt[l], b_pTs], [bbV], inc=False)
                    for b in range(16):
                        P(lambda e: e.matmul(bV[po, oc + b * 8:oc + (b + 1) * 8], vcb(b)[:, kv * 64:(kv + 1) * 64],
                                             pTs[:, h, 0, b * 8:(b + 1) * 8], start=False, stop=(b == 15)),
                          [b_vc, b_pTs], [bbV], inc=(b == 15))
                else:
                    for blk in range(nb):
                        slot = t + blk if nb == 2 else t + 1
                        P(lambda e: e.matmul(bV[po, oc:oc + 128], avt[l][:, slot, kv * 64:(kv + 1) * 64], pTs[:, h, blk, :],
                                             start=(blk == 0), stop=(blk == nb - 1)), [b_avt[l], b_pTs], [bbV])
            evac(oT[:, 4:8, tc], bV[:, :].rearrange("p (c n) -> p c n", c=4), [bbV], [b_oT[1]])

        ntl = nt if STAGE >= 5 else 0
        if smp:
            for t in range(ntl):
                tile_ret(t); tile_att_scores(t); tile_att_exp(t); tile_att_norm(t); tile_att_out(t)
        else:
            convq = [conv_cc, conv_ch, conv_cb]
            for t in range(ntl):
                tile_att_scores(t)
                if t > 0:
                    tile_att_out(t - 1)
                tile_att_exp(t)
                tile_ret(t)
                tile_att_norm(t)
                if convq:
                    convq.pop(0)()
            if ntl:
                tile_att_out(ntl - 1)
            while convq:
                convq.pop(0)()
        if not smp:
            A(lambda e: e.copy(akT[l][:, 0:128], akT[l][:, N:N + 128]), [b_akT[l]], [b_akT[l]])
            A(lambda e: e.copy(avt[l][:, 0, :], avt[l][:, nt, :]), [b_avt[l]], [b_avt[l]])

        if DBG and smp and l == 0:
            for c in range(12):
                V(lambda e: e.tensor_copy(tmpA[:, 0, 0:128], oT[:, c, 0:128]), [b_oT[c // 4]], [b_tmpA[0]])
                C.dma("pool", dbg[:, c, :], tmpA[:, 0, 0:128], reads=[b_tmpA[0]], stream=2)
        if STAGE < 6:
            return
        C.mark("%s%d gating" % (kind, l))
        for n in range(3):
            wbv, wbb = wload(A_wbr, (l, n), 4, 1024)
            for half in range(2):
                wv, wb = fblock(6 + n * 2 + half)
                for m in range(4):
                    dc = half * 4 + m
                    bA, bbA = bank()
                    fmm(bA, bbA, wv, wb, 8, slice(m * 128, (m + 1) * 128), hT, b_hT, N)
                    bB, bbB = bank()
                    fmm(bB, bbB, wbv, wbb, 4, slice(dc * 128, (dc + 1) * 128), oT[:, n * 4:(n + 1) * 4, :], b_oT[n], N)
                    sg = tmpA[:, dc % 2, :N]
                    A(lambda e: e.activation(sg, bA[:, :N], AF.Sigmoid), [bbA], [b_tmpA[dc % 2]])
                    if n == 0:
                        V(lambda e: e.tensor_tensor(yT[:, dc, :N], sg, bB[:, :N], ALU.mult), [b_tmpA[dc % 2], bbB], [b_yT[dc]])
                    else:
                        V(lambda e: e.tensor_tensor(sg, sg, bB[:, :N], ALU.mult), [b_tmpA[dc % 2], bbB], [b_tmpA[dc % 2]])
                        if n == 1:
                            V(lambda e: e.tensor_tensor(yT[:, dc, :N], yT[:, dc, :N], sg, ALU.add), [b_tmpA[dc % 2], b_yT[dc]], [b_yT[dc]])
                        else:
                            V(lambda e: e.tensor_tensor(mixT[:, dc, :N], yT[:, dc, :N], sg, ALU.add), [b_tmpA[dc % 2], b_yT[dc]], [b_mixT])
        C.mark("%s%d wout+norm" % (kind, l))
        st = stat_begin()
        for half in range(2):
            wv, wb = wload(A_wout, (l, half), 8, 512)
            for m in range(4):
                bk, bb = bank()
                fmm(bk, bb, wv, wb, 8, slice(m * 128, (m + 1) * 128), mixT, b_mixT, N)
                evac(yT[:, half * 4 + m, :N], bk[:, :N], [bb], [b_yT[half * 4 + m]])
                if half * 4 + m > 0:
                    stat_add(st, yT, b_yT, half * 4 + m - 1, N)
        stat_add(st, yT, b_yT, 7, N)
        st_x = post_norm_add(l, 1, N, st)

        if STAGE < 7:
            return
        C.mark("%s%d ffn" % (kind, l))
        C.barrier()
        pre_norm(l, 2, N, st_x)
        if prefetch is not None:
            prefetch()
        for blk in range(8):
            wv, wb = wload(A_wff1, (l, blk), 8, 512)
            pre = fmm4(wv, wb, N) if blk == 0 else None
            for m in range(4):
                if pre is not None:
                    bk, bb = pre[m]
                    unhold(bb)
                else:
                    bk, bb = bank()
                    fmm(bk, bb, wv, wb, 8, slice(m * 128, (m + 1) * 128), hT, b_hT, N)
                r = tmpA[:, m % 2, :N]
                A(lambda e: e.activation(r, bk[:, :N], AF.Relu), [bb], [b_tmpA[m % 2]])
                V(lambda e: e.tensor_tensor(hidT[:, blk * 4 + m, :N], r, r, ALU.mult), [b_tmpA[m % 2]], [b_hid])
        for half in range(2):
            acc = [bank(hold=True) for _ in range(4)]
            for kg in range(4):
                wv, wb = wload(A_wff2, (l, half * 4 + kg), 8, 512)
                for kc in range(8):
                    for m in range(4):
                        bk, bb = acc[m]
                        P(lambda e: e.matmul(bk[:, :N], wv[:, kc, m * 128:(m + 1) * 128], hidT[:, kg * 8 + kc, :N],
                                             start=(kg == 0 and kc == 0), stop=(kg == 3 and kc == 7)), [wb, b_hid], [bb])
            for m in range(4):
                bk, bb = acc[m]
                evac(yT[:, half * 4 + m, :N], bk[:, :N], [bb], [b_yT[half * 4 + m]])
                unhold(bb)
        post_norm_add_ffn = None
        st = stat_begin()
        for c in range(8):
            stat_add(st, yT, b_yT, c, N)
        st_x = post_norm_add(l, 3, N, st)

        if STAGE < 8:
            return
        C.mark("%s%d ple" % (kind, l))
        pre_norm(l, 4, N, st_x)
        wpv, wpb = wload(A_wpp, (l,), 2, 1024)
        for half in range(2):
            wv, wb = wload(A_wpg, (l, half), 8, 512)
            pre = fmm4(wv, wb, N) if half == 0 else None
            for m in range(4):
                dc = half * 4 + m
                if pre is not None:
                    bA, bbA = pre[m]
                    unhold(bbA)
                else:
                    bA, bbA = bank()
                    fmm(bA, bbA, wv, wb, 8, slice(m * 128, (m + 1) * 128), hT, b_hT, N)
                bB, bbB = bank()
                fmm(bB, bbB, wpv, wpb, 2, slice(dc * 128, (dc + 1) * 128), pT, b_pT, N)
                sg = tmpA[:, dc % 2, :N]
                A(lambda e: e.activation(sg, bA[:, :N], AF.Sigmoid), [bbA], [b_tmpA[dc % 2]])
                V(lambda e: e.tensor_tensor(sg, sg, bB[:, :N], ALU.mult), [b_tmpA[dc % 2], bbB], [b_tmpA[dc % 2]])
                V(lambda e: e.tensor_tensor(yT[:, dc, :N], xT[:, dc, :N], sg, ALU.add), [b_tmpA[dc % 2], b_xT[dc]], [b_yT[dc]])

    def load_x(src, nt):
        for t in range(nt):
            C.dma("pool", xin[:, 0, :], src[t * 128:(t + 1) * 128, :], writes=[b_xin[t % 2]], stream=1)
            for hf in range(2):
                bk, bb = bank()
                for j in range(4):
                    c = hf * 4 + j
                    P(lambda e: e.transpose(bk[:, j * 128:(j + 1) * 128], xin[:, 0, c * 128:(c + 1) * 128], ident[:, :]),
                      [b_xin[t % 2], b_const], [bb])
                evac(xT[:, hf * 4:(hf + 1) * 4, t * 128:(t + 1) * 128], bk[:, :].rearrange("p (c n) -> p c n", c=4), [bb], [b_xT[hf * 4:(hf + 1) * 4]])

    def load_p(src, nt, pslot=0):
        pT = pT2[:, pslot]; b_pT = b_pT2[pslot]
        for t in range(nt):
            C.dma("pool", pin[:, 0, :], src[t * 128:(t + 1) * 128, :], writes=[b_pin[t % 2]], stream=1)
            bk, bb = bank()
            for j in range(2):
                P(lambda e: e.transpose(bk[:, j * 128:(j + 1) * 128], pin[:, 0, j * 128:(j + 1) * 128], ident[:, :]),
                  [b_pin[t % 2], b_const], [bb])
            evac(pT[:, :, t * 128:(t + 1) * 128], bk[:, 0:256].rearrange("p (c n) -> p c n", c=2), [bb], [b_pT])

    def store_x(dst, nt, xT=None, b_xT=None):
        xT = yT if xT is None else xT
        b_xT = b_yT if b_xT is None else b_xT
        for t in range(nt):
            for hf in range(2):
                bk, bb = bank()
                for j in range(4):
                    c = hf * 4 + j
                    P(lambda e: e.transpose(bk[:, j * 128:(j + 1) * 128], xT[:, c, t * 128:(t + 1) * 128], ident[:, :]),
                      [b_xT[c], b_const], [bb])
                evac(xin[:, 0, hf * 512:(hf + 1) * 512], bk[:, :], [bb], [b_xin[t % 2]])
            C.dma("pool", dst[t * 128:(t + 1) * 128, :], xin[:, 0, :], reads=[b_xin[t % 2]], stream=2)

    ngp = NTP // 4

    def phase1(l, g, last, next_load=None):
        C.mark("phase1 %d" % l)
        N, nt = 512, 4
        rkt = arv(11264, nt * 256 * 2, BF, "p (t n) -> p t n", t=nt)
        rvt = arv(13312, nt * 512 * 2, BF, "p (t n) -> p t n", t=nt)
        pkg = arv(20480, PK * 4, F32)
        b_rkt = Buf(); b_rvt = Buf()
        C.barrier()
        pre_norm(l, 0, N)
        if next_load is not None:
            next_load()
        for bi in range(2):
            wv, wb = wload(A_winT, (l, bi), 8, 512)
            for t in range(nt):
                bk, bb = bank()
                for kc in range(8):
                    P(lambda e: e.matmul(bk[:, :], hT[:, kc, t * 128:(t + 1) * 128], wv[:, kc, :], start=(kc == 0), stop=(kc == 7)),
                      [wb, b_hT], [bb])
                if bi == 0:
                    V(lambda e: e.tensor_tensor(rkt[:, t, :], bk[:, 0:256], kdec[:, :], ALU.mult), [bb, b_const], [b_rkt])
                    if last and t == nt - 1:
                        A(lambda e: e.copy(pkg[:, 384:512], bk[:, 384:512]), [bb], [b_pkg])
                else:
                    evac(rvt[:, t, :], bk[:, :], [bb], [b_rvt])
        for t in range(nt):
            bS, bbS = bank()
            for h in range(4):
                hp, s_ = h // 2, h % 2
                pr = slice(s_ * 64, s_ * 64 + 64)
                P(lambda e: e.matmul(bS[pr, hp * 128:(hp + 1) * 128], rkt[:, t, h * 64:(h + 1) * 64],
                                     rvt[:, t, h * 128:(h + 1) * 128], start=True, stop=True), [b_rkt, b_rvt], [bbS])
            for hp in range(2):
                V(lambda e: e.scalar_tensor_tensor(S[l][:, hp, :], S[l][:, hp, :], cdec[:, hp:hp + 1],
                                                   bS[:, hp * 128:(hp + 1) * 128], ALU.mult, ALU.add),
                  [bbS, b_S[l], b_const], [b_S[l]])
        if last:
            tcl = slice(384, 512)
            wv, wb = wload(A_winF, (l, 2), 8, 128)
            bk, bb = bank()
            for kc in range(8):
                P(lambda e: e.matmul(bk[:, 0:128], wv[:, kc, 0:128], hT[:, kc, tcl], start=(kc == 0), stop=(kc == 7)), [wb, b_hT], [bb])
            A(lambda e: e.copy(pkg[:, 256:384], bk[:, 0:128]), [bb], [b_pkg])
            cc2 = tmpA[:, 0, 0:8]
            wv, wb = wload(A_winF, (l, 3), 8, 512)
            for c in range(4):
                bk, bb = bank()
                for kc in range(8):
                    P(lambda e: e.matmul(bk[:, 0:128], wv[:, kc, c * 128:(c + 1) * 128], hT[:, kc, tcl], start=(kc == 0), stop=(kc == 7)),
                      [wb, b_hT], [bb])
                A(lambda e: e.copy(cc2[:, 2 * c:2 * c + 2], bk[:, 126:128]), [bb], [b_tmpA[0]])
            wv, wb = wload(A_winF, (l, 4), 8, 512)
            for c in range(4):
                bk, bb = bank()
                for kc in range(8):
                    P(lambda e: e.matmul(bk[:, 0:128], wv[:, kc, c * 128:(c + 1) * 128], hT[:, kc, tcl], start=(kc == 0), stop=(kc == 7)),
                      [wb, b_hT], [bb])
                V(lambda e: e.tensor_tensor(pkg[:, 512 + 2 * c:514 + 2 * c], cc2[:, 2 * c:2 * c + 2], bk[:, 126:128], ALU.mult),
                  [bb, b_tmpA[0]], [b_pkg])
            V(lambda e: e.tensor_copy(pkg[:, 0:256], S[l][:, :, :].rearrange("p a e -> p (a e)")), [b_S[l]], [b_pkg])

    def exchange(l):
        C.mark("exchange %d" % l)
        pkg = arv(20480, PK * 4, F32)
        C.dma("pool", pkg_in[l], pkg, reads=[b_pkg], writes=[b_pki[l]], own=b_pkg)
        C._wait("pool", C._deps("pool", [b_pki[l]], [b_pko[l]]))
        ins = nc.gpsimd.collective_compute("AllGather", ALU.bypass,
                                           replica_groups=[[2 * i, 2 * i + 1] for i in range(NCORES // 2)],
                                           ins=[pkg_in[l].opt()], outs=[pkg_out[l].opt()])
        k = "cc%d" % l
        C.sem[k] = C.es.enter_context(nc.semaphore("s_" + k)); C.cnt[k] = 1
        ins.then_inc(C.sem[k])
        b_pki[l].r[k] = 1
        b_pko[l].w = (k, 1); b_pko[l].r = {}
        C.barrier()
        b_g = Buf()
        gbuf = arv(0, 4 * PK * 4, F32, "p (r n) -> p r n", r=4)
        for r0 in range(0, GS, 4):
            nr = min(4, GS - r0)
            C.dma("pool", gbuf[:, 0:nr, :], pkg_out[l][r0 * 128:(r0 + nr) * 128, :].rearrange("(r p) n -> p r n", p=128),
                  reads=[b_pko[l]], writes=[b_g])
            for r in range(nr):
                if r0 + r == 0:
                    V(lambda e: e.tensor_scalar(pkg, gbuf[:, r, :], sel8[:, 0:1], None, ALU.mult), [b_g, b_const], [b_pkg])
                else:
                    V(lambda e: e.scalar_tensor_tensor(pkg, gbuf[:, r, :], sel8[:, r0 + r:r0 + r + 1], pkg, ALU.mult, ALU.add),
                      [b_g, b_const, b_pkg], [b_pkg])
        V(lambda e: e.tensor_copy(S[l][:, :, :].rearrange("p a e -> p (a e)"), pkg[:, 0:256]), [b_pkg], [b_S[l]])
        A(lambda e: e.copy(Sb[l][:, :, :].rearrange("p a e -> p (a e)"), pkg[:, 0:256]), [b_pkg], [b_S[l]])
        A(lambda e: e.copy(akT[l][:, 0:128], pkg[:, 256:384]), [b_pkg], [b_akT[l]])
        A(lambda e: e.copy(avt[l][:, 0, :], pkg[:, 384:512]), [b_pkg], [b_avt[l]])
        V(lambda e: e.tensor_copy(uhalo[:, l, :, :].rearrange("p c j -> p (c j)"), pkg[:, 512:520]), [b_pkg], [b_uhalo])
        C.barrier()

    b_pkg = Buf("pkg"); b_pki = [Buf(), Buf()]; b_pko = [Buf(), Buf()]

    def xT_load(src, bsrc):
        C.dma("act", xT[:, :, :], src, reads=[bsrc], writes=[b_xT], own=b_xT[0])

    def xT_save(dst, bdst, N=512, src=None, bsrc=None):
        src = xT if src is None else src
        bsrc = b_xT if bsrc is None else bsrc
        C.dma("pool", dst[:, :, 0:N], src[:, :, 0:N], reads=[bsrc], writes=[bdst], own=bsrc[0])

    passes = []
    for l in range(DEPTH):
        for g in range(ngp):
            passes.append((l, "p", g))
        if SAMPLE:
            passes.append((l, "s", 0))
    p_loaded = set()
    p_staged = set()

    def stage_p(i):
        if i >= len(passes) or i in p_loaded:
            return
        l_, k_, g_ = passes[i]
        if k_ != "p":
            return
        C.dma("pool", xin[:, 0, :].rearrange("p (t f) -> p t f", t=4),
              pp[l_, g_ * 512:(g_ + 1) * 512, :].rearrange("(t p) f -> p t f", p=128), writes=[_bx], stream=1)
        p_staged.add(i)

    def stage_ps(i):
        if i >= len(passes) or i in p_loaded or passes[i][1] != "s":
            return
        C.dma("pool", pin[:, 0, :], ps_[passes[i][0]], writes=[_bp], stream=1)
        p_staged.add(i)

    def finish_p(i):
        if i not in p_staged:
            return
        pT = pT2[:, i % 2]; b_pT = b_pT2[i % 2]
        if passes[i][1] == "s":
            bk, bb = bank()
            for j in range(2):
                P(lambda e: e.transpose(bk[:, j * 128:(j + 1) * 128], pin[:, 0, j * 128:(j + 1) * 128], ident[:, :]),
                  [_bp, b_const], [bb])
            evac(pT[:, :, 0:128], bk[:, 0:256].rearrange("p (c n) -> p c n", c=2), [bb], [b_pT])
            p_loaded.add(i)
            return
        for t in range(4):
            bk, bb = bank()
            for j in range(2):
                P(lambda e: e.transpose(bk[:, j * 128:(j + 1) * 128], xin[:, 0, t * 256 + j * 128:t * 256 + (j + 1) * 128], ident[:, :]),
                  [_bx, b_const], [bb])
            evac(pT[:, :, t * 128:(t + 1) * 128], bk[:, 0:256].rearrange("p (c n) -> p c n", c=2), [bb], [b_pT])
        p_loaded.add(i)

    def ensure_p(i):
        if i >= len(passes) or i in p_loaded:
            return
        l_, k_, g_ = passes[i]
        if k_ == "p":
            load_p(pp[l_, g_ * 512:(g_ + 1) * 512, :], 4, i % 2)
        else:
            load_p(ps_[l_], 1, i % 2)
        p_loaded.add(i)

    pi = 0
    for l in range(DEPTH):
        for g in range(ngp):
            if l == 0:
                load_x(xp[g * 512:(g + 1) * 512, :], 4)
                xT_save(xs0[g], b_xs0[g])
                nl = None
            else:
                if g == 0:
                    xT_load(xs1[g], b_xs1[g])
                nl = (lambda gg=g + 1: xT_load(xs1[gg], b_xs1[gg])) if g + 1 < ngp else None
            phase1(l, g, last=(g == ngp - 1), next_load=nl)
        exchange(l)
        for g in range(ngp):
            C.mark("io %d" % l)
            if l == 0:
                xT_load(xs0[g], b_xs0[g])
            else:
                xT_load(xs1[g], b_xs1[g])
            ensure_p(pi)
            if g < ngp - 1:
                stage_p(pi + 1)
            else:
                stage_ps(pi + 1)
            process_layer(l, "p", 512, first_group=(g == 0), last_group=(g == ngp - 1), pslot=pi % 2,
                          prefetch=(lambda i=pi + 1: finish_p(i)))
            pi += 1
            C.mark("io %d" % l)
            if l == 0:
                xT_save(xs1[g], b_xs1[g], src=yT, bsrc=b_yT)
            else:
                store_x(yp[g * 512:(g + 1) * 512, :], 4)
        if SAMPLE:
            if l == 0:
                load_x(xs, 1)
            else:
                C.dma("pool", xT[:, :, 0:128], xs1[ngp][:, :, 0:128], reads=[b_xs1[ngp]], writes=[b_xT], own=b_xT[0])
            ensure_p(pi)
            process_layer(l, "s", 128, first_group=False, last_group=False, pslot=pi % 2)
            pi += 1
            if l == 0:
                xT_save(xs1[ngp], b_xs1[ngp], 128, src=yT, bsrc=b_yT)
            else:
                store_x(ys, 1)
    C.mark("end")
    C.final_wait("sp")
    C.close()
    if os.environ.get("MK_MARKS"):
        import json
        json.dump(C.marks, open(os.environ["MK_MARKS"], "w"))
    return nc, C


def host_consts():
    f = np.float32
    hh = np.arange(4, dtype=f)
    lg = np.log1p(-np.exp2(-5.0 - hh)).astype(f)
    i = np.arange(128, dtype=f)
    c = {}
    c["c_ident"] = np.eye(128, dtype=f)
    c["c_ones"] = np.full((128, 128), 1.0 / 1024.0, f)
    diff = i[None, :] - i[:, None]
    dm = np.where(diff[None] >= 0, np.exp(lg[:, None, None] * np.maximum(diff[None], 0.0)), 0.0).astype(f) * f(0.125)
    c["c_dmT"] = np.ascontiguousarray(dm.transpose(1, 0, 2)).reshape(128, 512)
    seq = (np.arange(128) // 8)
    tt = (np.arange(128) % 8).astype(f)
    same = (seq[:, None] == seq[None, :])
    dts = tt[None, :] - tt[:, None]
    dms = np.where(same[None] & (dts[None] >= 0), np.exp(lg[:, None, None] * np.maximum(dts[None], 0.0)), 0.0).astype(f) * f(0.125)
    c["c_dmTs"] = np.ascontiguousarray(dms.transpose(1, 0, 2)).reshape(128, 512)
    qd = np.exp(lg[:, None] * (i[None, :] + 1.0)).astype(f)
    q2 = np.zeros((128, 2, 128), f)
    for hp in range(2):
        for s in range(2):
            q2[s * 64:(s + 1) * 64, hp, :] = qd[2 * hp + s][None, :]
    c["c_qdec"] = q2.reshape(128, 256)
    qds = np.exp(lg[:, None] * (tt[None, :] + 1.0)).astype(f)
    c["c_qdecs"] = np.ascontiguousarray(np.broadcast_to(qds[None], (128, 4, 128))).reshape(128, 512)
    kd = np.exp(lg[:, None] * (127.0 - i[None, :])).astype(f) * f(0.125)
    c["c_kdec"] = np.ascontiguousarray(np.repeat(kd.T[:, :, None], 64, axis=2)).reshape(128, 256)
    kds = np.exp(lg[:, None] * (7.0 - tt[None, :])).astype(f) * f(0.125)
    c["c_kdecs"] = np.ascontiguousarray(np.repeat(kds.T[:, :, None], 64, axis=2)).reshape(128, 256)
    cd = np.exp(lg * f(128.0)).astype(f)
    c2 = np.zeros((128, 2), f)
    for hp in range(2):
        for s in range(2):
            c2[s * 64:(s + 1) * 64, hp] = cd[2 * hp + s]
    c["c_cdec"] = c2
    slopes = np.exp2(-8.0 * (np.arange(8, dtype=f) + 1.0) / 8.0).astype(f)
    q = np.arange(128)[:, None]; kk = np.arange(256)[None, :]
    dist = (128 + q - kk)
    allowed = (dist >= 0) & (dist < 128)
    bp = np.where(allowed[:, None, :], -slopes[None, :, None] * dist[:, None, :].astype(f), f(NEG)).astype(f)
    c["c_biasP"] = np.ascontiguousarray(bp).reshape(128, 2048)
    ti = (np.arange(128) % 8)[:, None]; si = (np.arange(128) // 8)[:, None]
    kc_ = np.arange(128)[None, :]
    dist_c = 128 + ti - kc_
    al_c = dist_c < 128
    tj = (np.arange(128) % 8)[None, :]; sj = (np.arange(128) // 8)[None, :]
    dist_n = ti - tj
    al_n = (si == sj) & (dist_n >= 0)
    dist_s = np.concatenate([dist_c, dist_n], axis=1)
    al_s = np.concatenate([al_c, al_n], axis=1)
    bs = np.where(al_s[:, None, :], -slopes[None, :, None] * dist_s[:, None, :].astype(f), f(NEG)).astype(f)
    c["c_biasS"] = np.ascontiguousarray(bs).reshape(128, 2048)
    bmq = (np.arange(16)[:, None] == seq[None, :]).astype(f)
    c["c_bmQ"] = np.ascontiguousarray(np.broadcast_to(bmq[None], (128, 16, 128))).reshape(128, 2048)
    bq2 = np.zeros((128, 8, 128), f)
    for bh in range(2):
        bq2[bh * 64:(bh + 1) * 64] = (np.arange(8)[:, None] + bh * 8 == seq[None, :]).astype(f)[None]
    c["c_bmQ2"] = bq2.reshape(128, 1024)
    c["c_bmV"] = (seq[:, None] == np.arange(16)[None, :]).astype(f)
    return c


def blk(w, cols=None):
    if cols is not None:
        w = w[:, cols]
    K, n = w.shape
    return np.ascontiguousarray(w.reshape(K // 128, 128, n).transpose(1, 0, 2))


def host_weights(w_in, w_branch, w_out, w_ff1, w_ff2, w_ple_gate, w_ple_proj):
    r = lambda a, n: np.arange(a, a + n)
    aqperm = np.concatenate([np.concatenate([r(OFF["aq"] + c * 64, 64), r(OFF["aq"] + (c + 4) * 64, 64)]) for c in range(4)])
    rqdup = np.concatenate([np.concatenate([r(OFF["rq"] + h * 64, 64), r(OFF["rq"] + h * 64, 64)]) for h in range(4)])
    fcols = [np.concatenate([r(OFF["rq"], 256), r(OFF["rk"], 256)]), aqperm, np.tile(r(OFF["ak"], 128), 4),
             r(OFF["cc"], 512), r(OFF["ch"], 512), r(OFF["cb"], 512)]
    for n in range(3):
        for half in range(2):
            fcols.append(r(OFF["gt"] + n * 1024 + half * 512, 512))
    fcols.append(rqdup)
    tcols = [np.concatenate([r(OFF["rk"], 256), r(OFF["ak"], 128), r(OFF["av"], 128)]), r(OFF["rv"], 512), r(OFF["rg"], 512)]
    out = {}
    out["winF"] = np.stack([np.stack([blk(w_in[l], cc) for cc in fcols]) for l in range(DEPTH)])
    out["winT"] = np.stack([np.stack([blk(w_in[l], cc) for cc in tcols]) for l in range(DEPTH)])
    out["wbr"] = np.stack([np.stack([blk(w_branch[l, n]) for n in range(3)]) for l in range(DEPTH)])
    out["wout"] = np.stack([np.stack([blk(w_out[l][:, h * 512:(h + 1) * 512]) for h in range(2)]) for l in range(DEPTH)])
    out["wff1"] = np.stack([np.stack([blk(w_ff1[l][:, b * 512:(b + 1) * 512]) for b in range(8)]) for l in range(DEPTH)])
    out["wff2"] = np.stack([np.stack([blk(w_ff2[l][kg * 1024:(kg + 1) * 1024, h * 512:(h + 1) * 512])
                                      for h in range(2) for kg in range(4)]) for l in range(DEPTH)])
    out["wpg"] = np.stack([np.stack([blk(w_ple_gate[l][:, h * 512:(h + 1) * 512]) for h in range(2)]) for l in range(DEPTH)])
    out["wpp"] = np.stack([blk(w_ple_proj[l]) for l in range(DEPTH)])
    return out


_CACHE = {}


def run(inputs, NTP, n_cores, seq_of_core, samp_of_core, SAMPLE=True):
    key = (NTP, SAMPLE, n_cores)
    if key not in _CACHE:
        _CACHE[key] = build(NTP, SAMPLE, n_cores)
    nc, C = _CACHE[key]
    f = np.float32
    g = lambda k: np.asarray(inputs[k], dtype=f)
    shared = host_consts()
    shared.update(host_weights(g("w_in"), g("w_branch"), g("w_out"), g("w_ff1"), g("w_ff2"), g("w_ple_gate"), g("w_ple_proj")))
    gs = np.stack([g("g_mix_pre"), g("g_mix_post"), g("g_ffn_pre"), g("g_ffn_post"), g("g_ple")], axis=1)
    shared["gT"] = np.ascontiguousarray(gs.reshape(DEPTH, 5, 8, 128).transpose(3, 0, 1, 2))
    shared["cwT"] = np.ascontiguousarray(g("conv_w").reshape(DEPTH, 3, 4, 128).transpose(3, 0, 2, 1))
    shared["sinkR"] = np.ascontiguousarray(np.broadcast_to(g("attn_sinks")[None], (128, DEPTH, 8)))
    TP = NTP * 128
    xpr, ppr, xsm, psm = g("x_prompt"), g("p_prompt"), g("x_sample"), g("p_sample")
    sr, ckk, cvv, scc = g("state_ret"), g("cache_win_k"), g("cache_win_v"), g("state_conv")
    in_maps = []
    for c in range(n_cores):
        sq_, t0 = seq_of_core[c]
        b0 = samp_of_core[c]
        m = dict(shared)
        m["xp"] = np.ascontiguousarray(xpr[sq_, t0:t0 + TP])
        m["pp"] = np.ascontiguousarray(ppr[:, sq_, t0:t0 + TP])
        m["xs"] = np.ascontiguousarray(xsm[b0:b0 + 16].reshape(128, D))
        m["ps"] = np.ascontiguousarray(psm[:, b0:b0 + 16].reshape(DEPTH, 128, 256))
        m["sret"] = np.ascontiguousarray(sr[:, b0:b0 + 16])
        m["ck"] = np.ascontiguousarray(ckk[:, b0:b0 + 16].reshape(DEPTH, 16, 128, 128))
        m["cv"] = np.ascontiguousarray(cvv[:, b0:b0 + 16].reshape(DEPTH, 16, 128, 128))
        m["scv"] = np.ascontiguousarray(scc[:, b0:b0 + 16].reshape(DEPTH, 32, 512))
        sel = np.zeros((128, 8), f)
        if c % 2 == 1:
            sel[:, 0] = 1.0
        m["sel8"] = sel
        m["negf"] = np.full((128, 1), NEG if c % 2 == 0 else 0.0, f)
        in_maps.append(m)
    res = run_bass_kernel_spmd(nc, in_maps, core_ids=list(range(n_cores)))
    return res.results


def kernel(**inputs):
    NTP = 16
    n = 8
    seq_of_core = [(c // 2, (c % 2) * 2048) for c in range(n)]
    samp_of_core = [16 * c for c in range(n)]
    R = run(inputs, NTP, n, seq_of_core, samp_of_core)
    f = np.float32
    yp = np.stack([np.concatenate([R[2 * i]["yp"], R[2 * i + 1]["yp"]], axis=0) for i in range(4)]).astype(f)
    ys = np.concatenate([R[c]["ys"].reshape(16, 8, D) for c in range(n)]).astype(f)
    odd = [1, 3, 5, 7]
    retp = np.stack([R[c]["retp"] for c in odd], axis=1).astype(f)
    wkp = np.stack([R[c]["wkp"].reshape(DEPTH, 128, 2, 64) for c in odd], axis=1).astype(f)
    wvp = np.stack([R[c]["wvp"].reshape(DEPTH, 128, 2, 64) for c in odd], axis=1).astype(f)
    cvp = np.stack([R[c]["cvp"] for c in odd], axis=1).astype(f)
    rets = np.concatenate([R[c]["rets"] for c in range(n)], axis=1).astype(f)
    wks = np.concatenate([R[c]["wks"].reshape(DEPTH, 16, 128, 2, 64) for c in range(n)], axis=1).astype(f)
    wvs = np.concatenate([R[c]["wvs"].reshape(DEPTH, 16, 128, 2, 64) for c in range(n)], axis=1).astype(f)
    cvs = np.concatenate([R[c]["cvs"].reshape(DEPTH, 16, 2, 512) for c in range(n)], axis=1).astype(f)
    return (yp, ys, retp, wkp, wvp, cvp, rets, wks, wvs, cvs)
```
